# Optimizing a Trainium2 kernel written in Bass

```python
import math
import jax, jax.numpy as jnp
from jax import lax
import numpy as np

D_MODEL = 1024
BATCH = 32
SEQ = 2048
DEPTH = 4

N_MIXERS = 3
HEAD_DIM = 64
N_HEADS = D_MODEL // (2 * HEAD_DIM)
Q_BLOCK = 128
ROPE_THETA = 10000.0
GMLP_WIDTH = 3 * D_MODEL // 2
GMLP_GROUPS = 8
GMLP_GW = GMLP_WIDTH // GMLP_GROUPS
CHUNK = 128
CONV_WIDTH = 3
D_FF = ((8 * D_MODEL + 3 * 256 - 1) // (3 * 256)) * 256
NORM_EPS = 1e-6
SUBLN_EPS = 1e-5
LN_EPS = 1e-5
NEG_INF = -1e30

kernel_name = "hybrid_diffattn_gmlp_shortconv_trunk"


def n_layers_of(kind):
    return len(range(kind, DEPTH, N_MIXERS))


def rms_norm(x, g, eps=NORM_EPS):
    xf = x.astype(jnp.float32)
    y = xf * lax.rsqrt(jnp.mean(xf * xf, axis=-1, keepdims=True) + eps)
    return (y * g.astype(jnp.float32)).astype(x.dtype)


def layer_norm(x, g, b, eps=LN_EPS):
    xf = x.astype(jnp.float32)
    mu = jnp.mean(xf, axis=-1, keepdims=True)
    var = jnp.mean(jnp.square(xf - mu), axis=-1, keepdims=True)
    y = (xf - mu) * lax.rsqrt(var + eps)
    return (y * g.astype(jnp.float32) + b.astype(jnp.float32)).astype(x.dtype)


def rope_tables(positions):
    inv_freq = 1.0 / (ROPE_THETA ** (jnp.arange(0, HEAD_DIM, 2, dtype=jnp.float32) / HEAD_DIM))
    ang = positions.astype(jnp.float32)[..., None] * inv_freq
    return jnp.cos(ang)[:, :, None, :], jnp.sin(ang)[:, :, None, :]


def apply_rope(x, cos, sin):
    xf = x.astype(jnp.float32)
    x1, x2 = jnp.split(xf, 2, axis=-1)
    return jnp.concatenate([x1 * cos - x2 * sin, x2 * cos + x1 * sin], axis=-1).astype(x.dtype)


def diff_attention(h, cos, sin, w_in, lam, subln, w_out, lambda_init):
    B, S, _ = h.shape
    q, k, v = jnp.split(h @ w_in, 3, axis=-1)
    q = apply_rope(q.reshape(B, S, 2 * N_HEADS, HEAD_DIM), cos, sin) * (HEAD_DIM ** -0.5)
    k = apply_rope(k.reshape(B, S, 2 * N_HEADS, HEAD_DIM), cos, sin)
    v = v.reshape(B, S, N_HEADS, 2 * HEAD_DIM)
    lamf = lam.astype(jnp.float32)
    lam_full = (jnp.exp(jnp.sum(lamf[0] * lamf[1])) - jnp.exp(jnp.sum(lamf[2] * lamf[3]))
                + lambda_init)
    outs = []
    for i in range(S // Q_BLOCK):
        kv_len = (i + 1) * Q_BLOCK
        qb = q[:, i * Q_BLOCK:kv_len]
        kb = k[:, :kv_len]
        vb = v[:, :kv_len]
        s = jnp.einsum('bqmd,bkmd->bmqk', qb, kb, preferred_element_type=jnp.float32)
        qpos = i * Q_BLOCK + jnp.arange(Q_BLOCK)
        mask = jnp.arange(kv_len)[None, :] <= qpos[:, None]
        p = jax.nn.softmax(jnp.where(mask, s, NEG_INF), axis=-1)
        p = p.reshape(B, N_HEADS, 2, Q_BLOCK, kv_len)
        a = p[:, :, 0] - lam_full * p[:, :, 1]
        outs.append(jnp.einsum('bhqk,bkhe->bqhe', a.astype(vb.dtype), vb))
    o = jnp.concatenate(outs, axis=1)
    o = rms_norm(o, subln, SUBLN_EPS) * (1.0 - lambda_init)
    return o.reshape(B, S, D_MODEL) @ w_out


def chunked_gmlp(h, w_in, b_in, ln_g, ln_b, w_s, b_s, w_out):
    B, S, _ = h.shape
    z = jax.nn.gelu(h @ w_in + b_in, approximate=False)
    u, v = jnp.split(z, 2, axis=-1)
    v = layer_norm(v, ln_g, ln_b).reshape(B, S // CHUNK, CHUNK, GMLP_GROUPS, GMLP_GW)
    causal = jnp.tril(jnp.ones((CHUNK, CHUNK), dtype=w_s.dtype))
    ws = w_s * causal[None]
    sv = jnp.einsum('gts,bnsgc->bntgc', ws, v) + b_s.T[None, None, :, :, None]
    return (u * sv.reshape(B, S, GMLP_WIDTH)) @ w_out


def short_conv(h, w_in, conv_w, w_out):
    gb, gc, xs = jnp.split(h @ w_in, 3, axis=-1)
    hc = gc * xs
    conv = lax.conv_general_dilated(
        hc, conv_w[:, None, :], window_strides=(1,), padding=[(CONV_WIDTH - 1, 0)],
        dimension_numbers=('NWC', 'WIO', 'NWC'), feature_group_count=D_MODEL)
    return (gb * conv) @ w_out


def swiglu(h, w_gate_up, w_down):
    g, u = jnp.split(h @ w_gate_up, 2, axis=-1)
    return (jax.nn.silu(g) * u) @ w_down


def setup_inputs(seed: int = 0) -> dict:
    key = jax.random.key(seed)
    ks = jax.random.split(key, 24)
    nA, nB, nC = n_layers_of(0), n_layers_of(1), n_layers_of(2)
    D = D_MODEL
    nrm = lambda k, shape, scale: jax.random.normal(k, shape, jnp.float32) * scale
    offset = jax.random.randint(ks[1], (BATCH, 1), 0, 4096, dtype=jnp.int32)
    return {
        "x": jax.random.normal(ks[0], (BATCH, SEQ, D), jnp.float32),
        "positions": offset + jnp.arange(SEQ, dtype=jnp.int32)[None, :],
        "mix_norm": 1.0 + nrm(ks[2], (DEPTH, D), 0.02),
        "ffn_norm": 1.0 + nrm(ks[3], (DEPTH, D), 0.02),
        "final_norm": 1.0 + nrm(ks[4], (D,), 0.02),
        "attn_w_in": nrm(ks[5], (nA, D, 3 * D), D ** -0.5),
        "attn_lambda": nrm(ks[6], (nA, 4, HEAD_DIM), 0.1),
        "attn_subln": 1.0 + nrm(ks[7], (nA, 2 * HEAD_DIM), 0.02),
        "attn_w_out": nrm(ks[8], (nA, D, D), D ** -0.5),
        "gmlp_w_in": nrm(ks[9], (nB, D, 2 * GMLP_WIDTH), D ** -0.5),
        "gmlp_b_in": nrm(ks[10], (nB, 2 * GMLP_WIDTH), 0.02),
        "gmlp_ln_g": 1.0 + nrm(ks[11], (nB, GMLP_WIDTH), 0.02),
        "gmlp_ln_b": nrm(ks[12], (nB, GMLP_WIDTH), 0.02),
        "gmlp_w_s": nrm(ks[13], (nB, GMLP_GROUPS, CHUNK, CHUNK), CHUNK ** -0.5),
        "gmlp_b_s": 1.0 + nrm(ks[14], (nB, GMLP_GROUPS, CHUNK), 0.02),
        "gmlp_w_out": nrm(ks[15], (nB, GMLP_WIDTH, D), GMLP_WIDTH ** -0.5),
        "conv_w_in": nrm(ks[16], (nC, D, 3 * D), D ** -0.5),
        "conv_w": nrm(ks[17], (nC, CONV_WIDTH, D), CONV_WIDTH ** -0.5),
        "conv_w_out": nrm(ks[18], (nC, D, D), D ** -0.5),
        "ffn_w_gate_up": nrm(ks[19], (DEPTH, D, 2 * D_FF), D ** -0.5),
        "ffn_w_down": nrm(ks[20], (DEPTH, D_FF, D), D_FF ** -0.5),
    }


def reference(x, positions, mix_norm, ffn_norm, final_norm,
              attn_w_in, attn_lambda, attn_subln, attn_w_out,
              gmlp_w_in, gmlp_b_in, gmlp_ln_g, gmlp_ln_b, gmlp_w_s, gmlp_b_s, gmlp_w_out,
              conv_w_in, conv_w, conv_w_out,
              ffn_w_gate_up, ffn_w_down):
    cos, sin = rope_tables(positions)
    for i in range(DEPTH):
        kind, j = i % N_MIXERS, i // N_MIXERS
        h = rms_norm(x, mix_norm[i])
        if kind == 0:
            lambda_init = 0.8 - 0.6 * math.exp(-0.3 * i)
            m = diff_attention(h, cos, sin, attn_w_in[j], attn_lambda[j], attn_subln[j],
                               attn_w_out[j], lambda_init)
        elif kind == 1:
            m = chunked_gmlp(h, gmlp_w_in[j], gmlp_b_in[j], gmlp_ln_g[j], gmlp_ln_b[j],
                             gmlp_w_s[j], gmlp_b_s[j], gmlp_w_out[j])
        else:
            m = short_conv(h, conv_w_in[j], conv_w[j], conv_w_out[j])
        x = x + m
        x = x + swiglu(rms_norm(x, ffn_norm[i]), ffn_w_gate_up[i], ffn_w_down[i])
    return rms_norm(x, final_norm)
```

```python
import math
import contextlib
import numpy as np
import concourse.bass as bass
import concourse.mybir as mybir
from concourse.bass_utils import run_bass_kernel_spmd

F32 = mybir.dt.float32
BF16 = mybir.dt.bfloat16
I32 = mybir.dt.int32
AF = mybir.ActivationFunctionType
ALU = mybir.AluOpType

D = 1024
S = 2048
NCH = 8
NT = 4
NB = 16
DFF = 2816
NF = 22
GW = 1536
NORM_EPS = 1e-6
SUBLN_EPS = 1e-5
LN_EPS = 1e-5
N_CORES = 8


class Buf:
    __slots__ = ("name", "last_w", "readers", "dsem", "dcnt")

    def __init__(self, name=""):
        self.name = name
        self.last_w = None
        self.readers = {}
        self.dsem = None
        self.dcnt = 0


class EngW:
    def __init__(self, name, sem, is_pe=False):
        self.name = name
        self.sem = sem
        self.cnt = 0
        self.waited = {}
        self.prog = []
        self.is_pe = is_pe


class K:
    def __init__(self, nc, stack):
        self.nc = nc
        self.stack = stack
        self.nsem = 0
        self.pe = EngW("pe", self.new_sem("pe"), True)
        self.act = EngW("act", self.new_sem("act"))
        self.dve = EngW("dve", self.new_sem("dve"))
        self.pool = EngW("pool", self.new_sem("pool"))
        self.sp = EngW("sp", self.new_sem("sp"))

    def new_sem(self, name):
        self.nsem += 1
        return self.stack.enter_context(self.nc.semaphore(f"s{self.nsem}_{name}"))

    def sb(self, name, shape, dt):
        return self.stack.enter_context(self.nc.sbuf_tensor(name, list(shape), dt))

    def ps(self, name, shape, dt=F32):
        return self.stack.enter_context(self.nc.psum_tensor(name, list(shape), dt))

    def _wait(self, E, tok):
        if tok is None:
            return
        sem, val = tok
        if sem is E.sem and E.is_pe:
            return
        if E.waited.get(sem, 0) >= val:
            return
        E.waited[sem] = val
        E.prog.append(lambda eng, sem=sem, val=val: eng.wait_ge(sem, val))

    def _deps(self, E, reads, writes):
        for b in reads:
            self._wait(E, b.last_w)
        for b in writes:
            self._wait(E, b.last_w)
            for s, v in b.readers.items():
                self._wait(E, (s, v))

    def _mark(self, tok, reads, writes):
        for b in reads:
            if b.readers.get(tok[0], 0) < tok[1]:
                b.readers[tok[0]] = tok[1]
        for b in writes:
            b.last_w = tok
            b.readers = {}

    def op(self, E, fn, reads=(), writes=()):
        self._deps(E, reads, writes)
        E.cnt += 1
        sem = E.sem
        E.prog.append(lambda eng, fn=fn, sem=sem: fn(eng).then_inc(sem, 1))
        tok = (sem, E.cnt)
        self._mark(tok, reads, writes)
        return tok

    def group(self, E, fns, reads=(), writes=()):
        self._deps(E, reads, writes)
        E.cnt += 1
        sem = E.sem
        for f in fns[:-1]:
            E.prog.append(lambda eng, f=f: f(eng))
        E.prog.append(lambda eng, f=fns[-1], sem=sem: f(eng).then_inc(sem, 1))
        tok = (sem, E.cnt)
        self._mark(tok, reads, writes)
        return tok

    def dma(self, Q, pairs, reads=(), writes=(), sembuf=None, **kw):
        self._deps(Q, reads, writes)
        sb = sembuf if sembuf is not None else (writes[0] if writes else reads[0])
        if sb.dsem is None:
            sb.dsem = self.new_sem("d_" + sb.name)
        dsem = sb.dsem
        for (o, i) in pairs:
            sb.dcnt += 16
            Q.prog.append(lambda eng, o=o, i=i, dsem=dsem, kw=kw: eng.dma_start(out=o, in_=i, **kw).then_inc(dsem, 16))
        tok = (dsem, sb.dcnt)
        self._mark(tok, reads, writes)
        return tok

    def emit(self):
        nc = self.nc
        with nc.Block() as block:
            @block.tensor
            def _(e):
                for f in self.pe.prog:
                    f(e)

            @block.scalar
            def _(e):
                for f in self.act.prog:
                    f(e)

            @block.vector
            def _(e):
                for f in self.dve.prog:
                    f(e)

            @block.gpsimd
            def _(e):
                for f in self.pool.prog:
                    f(e)

            @block.sync
            def _(e):
                for f in self.sp.prog:
                    f(e)


class Rot:
    def __init__(self, items):
        self.items = list(items)
        self.i = 0

    def next(self):
        it = self.items[self.i % len(self.items)]
        self.i += 1
        return it


def mm(out, lhsT, rhs, start, stop):
    return lambda e: e.matmul(out, lhsT=lhsT, rhs=rhs, start=start, stop=stop)


def build_program(NS=4, nlayers=4, dbg=()):
    nc = bass.Bass("TRN2", target_bir_lowering=False)

    def din(name, shape, dt=F32):
        return nc.dram_tensor(name, list(shape), dt, kind="ExternalInput").ap()

    x_d = din("x", [NS, S, D])
    pos_d = din("positions", [NS, S], I32)
    mixn_d = din("mix_norm", [4, D])
    ffnn_d = din("ffn_norm", [4, D])
    finn_d = din("final_norm", [D])
    a_win_d = din("attn_w_in", [2, D, 3 * D])
    a_lam_d = din("attn_lambda", [2, 4, 64])
    a_sub_d = din("attn_subln", [2, 128])
    a_wout_d = din("attn_w_out", [2, D, D])
    g_win_d = din("gmlp_w_in", [1, D, 2 * GW])
    g_bin_d = din("gmlp_b_in", [1, 2 * GW])
    g_lng_d = din("gmlp_ln_g", [1, GW])
    g_lnb_d = din("gmlp_ln_b", [1, GW])
    g_ws_d = din("gmlp_w_s", [1, 8, 128, 128])
    g_bs_d = din("gmlp_b_s", [1, 8, 128])
    g_wout_d = din("gmlp_w_out", [1, GW, D])
    c_win_d = din("conv_w_in", [1, D, 3 * D])
    c_w_d = din("conv_w", [1, 3, D])
    c_wout_d = din("conv_w_out", [1, D, D])
    f_wgu_d = din("ffn_w_gate_up", [4, D, 2 * DFF])
    f_wd_d = din("ffn_w_down", [4, DFF, D])
    cst_d = din("cst", [128, 132])
    out_d = nc.dram_tensor("out", [NS, S, D], F32, kind="ExternalOutput").ap()

    with contextlib.ExitStack() as st:
        k = K(nc, st)
        PE, ACT, DVE, POOL, SP = k.pe, k.act, k.dve, k.pool, k.sp

        xs = k.sb("xs", [128, NCH, S], F32)
        hT = k.sb("hT", [128, NCH, S], BF16)
        big = k.sb("big", [128, NCH * S], BF16)
        qkv = k.sb("qkv", [128, 3 * S], BF16)
        qk = qkv[:, 0:2 * S].rearrange("p (m t) -> p m t", m=2)
        vv = qkv[:, 2 * S:3 * S].rearrange("p (b e) -> p b e", b=NB)
        pT = k.sb("pT", [128, 3, 2, 512], BF16)
        ring = k.sb("ring", [128, 4, 2048], BF16)
        cs = k.sb("cs", [128, 2, S], F32)
        tm = k.sb("tm", [128, 4, 512], F32)
        ep = tm
        sq = k.sb("sq", [128, 3, 512], BF16)
        ident = k.sb("ident", [128, 128], F32)
        onesf = k.sb("onesf", [128, 128], F32)
        onesb = k.sb("onesb", [128, 128], BF16)
        tri2 = k.sb("tri2", [128, 2, 128], BF16)
        permb = k.sb("permb", [128, 128], BF16)
        cst = k.sb("cst_sb", [128, 4], F32)
        gmix = k.sb("gmix", [128, 4, NCH], F32)
        gffn = k.sb("gffn", [128, 4, NCH], F32)
        gfin = k.sb("gfin", [128, NCH], F32)
        convw = k.sb("convw", [128, 3, NCH], F32)
        bu = k.sb("bu", [128, 12], F32)
        lng = k.sb("lng", [128, 12], F32)
        lnb = k.sb("lnb", [128, 12], F32)
        subl = k.sb("subl", [128, 2], F32)
        nlam = k.sb("nlam", [128, 2], F32)
        small = k.sb("small", [128, 16], F32)
        wsTm = k.sb("wsTm", [128, 8, 128], BF16)
        Rt = k.sb("Rt", [128, 12, 128], F32)
        bvhl = k.sb("bvhl", [2, GW], BF16)
        psum = k.ps("psum", [128, 8, 512], F32)

        aT = big[:, 0:11 * 1024].rearrange("p (f t) -> p f t", f=11)
        oT = big[:, 0:NCH * S].rearrange("p (c t) -> p c t", c=NCH)
        bigf = big[:, :].bitcast(F32)
        stg = bigf[:, 0:2048].rearrange("p (s d) -> p s d", s=2)
        rtmp = bigf[:, 2048:2048 + 3 * S].rearrange("p (a t) -> p a t", a=3)
        rtmpi = rtmp.bitcast(I32)
        uT = big[:, 0:12 * 512].rearrange("p (c t) -> p c t", c=12)
        zv4 = bigf[:, 3072:3072 + 2 * GW].rearrange("p (b c) -> p b c", b=2)
        vhat = big[:, 12288:12288 + GW]
        svt = bigf[:, 6912:6912 + 1024].rearrange("p (a c t) -> p a c t", a=2, c=4)
        gst = k.sb("gst", [128, 32], F32)
        hcv = qkv[:, :].bitcast(F32)[:, 0:2 + S]

        XB = [[Buf(f"x{c}_{t}") for t in range(NT)] for c in range(NCH)]
        HB = [[Buf(f"h{c}_{t}") for t in range(NT)] for c in range(NCH)]
        PB = [Buf(f"pb{i}") for i in range(8)]
        RING = [Buf(f"ring{i}") for i in range(4)]
        ringrot = Rot(range(4))
        SQ = [Buf(f"sq{i}") for i in range(3)]
        sqrot = Rot(range(3))
        TM = [Buf(f"tm{i}") for i in range(4)]
        tmrot = Rot(range(4))
        EPB = TM
        PT = [Buf(f"pT{i}") for i in range(3)]
        ptrot = Rot(range(3))
        QB = [[Buf(f"q{m}_{t}") for t in range(NT)] for m in range(2)]
        VB = [Buf(f"v{t}") for t in range(NT)]
        OB = [[Buf(f"o{c}_{t}") for t in range(NT)] for c in range(NCH)]
        AB = [[Buf(f"a{f}_{t}") for t in range(2)] for f in range(11)]
        CSB = Buf("cs")
        STG = [Buf("stg0"), Buf("stg1")]
        stgrot = Rot(range(2))
        RTMP = Buf("rtmp")
        CONST = Buf("const")
        OUTB = Buf("outd")
        OUTS = [Buf("outs0"), Buf("outs1")]
        GM = {n: Buf(n) for n in ["uT", "zv0", "zv1", "zv2", "zv3", "vhat", "svt0", "svt1", "stats", "hc"]}
        svrot = Rot(range(2))
        wbank = Rot(range(8))
        wbank4 = Rot(range(4))

        def bank(i):
            return psum[:, i, :]

        def tsl(t):
            return slice(t * 512, (t + 1) * 512)

        scr = {}

        def prep(name, W, Kdim, N, J):
            n_ch = N // J
            C = Kdim // 128
            t_ = nc.dram_tensor("scr_" + name, [n_ch, 128, C, J], BF16, kind="Internal").ap()
            b = Buf("scr_" + name)
            pairs = []
            for n in range(n_ch):
                pairs.append((t_[n], W[:, n * J:(n + 1) * J].rearrange("(c p) j -> p c j", p=128)))
            k.dma(POOL, pairs, writes=[b])
            scr[name] = (t_, b)

        k.dma(SP, [(cst[:], cst_d[:, 0:4])], writes=[CONST])
        k.dma(POOL, [(permb[:], cst_d[:, 4:132])], writes=[CONST], sembuf=Buf("permld"))
        ncs = dict(allow_slow_non_contiguous=True)
        k.dma(SP, [(gmix[:], mixn_d.rearrange("l (c p) -> p l c", p=128)),
                   (gffn[:], ffnn_d.rearrange("l (c p) -> p l c", p=128)),
                   (gfin[:], finn_d.rearrange("(c p) -> p c", p=128)),
                   (convw[:], c_w_d[0].rearrange("w (c p) -> p w c", p=128)),
                   (bu[:], g_bin_d[0, 0:GW].rearrange("(c p) -> p c", p=128)),
                   (lng[:], g_lng_d[0].rearrange("(c p) -> p c", p=128)),
                   (lnb[:], g_lnb_d[0].rearrange("(c p) -> p c", p=128)),
                   (subl[:], a_sub_d.rearrange("j p -> p j"))], writes=[CONST], **ncs)
        k.op(POOL, lambda e: e.memset(onesf[:], 1.0), [], [CONST])
        k.op(POOL, lambda e: e.memset(onesb[:], 1.0), [], [CONST])
        k.op(POOL, lambda e: e.affine_select(out=ident[:], in_=onesf[:], pattern=[[1, 128]], compare_op=ALU.is_equal,
                                             fill=0.0, base=0, channel_multiplier=-1), [CONST], [CONST])
        for m in range(2):
            k.op(POOL, lambda e, m=m: e.affine_select(out=tri2[:, m, :], in_=onesb[:], pattern=[[1, 128]], compare_op=ALU.is_ge,
                                                      fill=0.0, base=0, channel_multiplier=-1), [CONST], [CONST])

        lamt = bigf[:, 0:512].rearrange("p (j f) -> p j f", j=2)
        k.dma(SP, [(lamt[:, j, :], a_lam_d[j].rearrange("a f -> (a f)").partition_broadcast(128)) for j in range(2)],
              writes=[RTMP])
        for j in range(2):
            li = 0.8 - 0.6 * math.exp(-0.3 * (3 * j))
            prod = bigf[:, 512:640]
            k.op(DVE, lambda e, j=j: e.tensor_tensor(out=prod.rearrange("p (a f) -> p a f", a=2), in0=lamt[:, j, :].rearrange("p (a b f) -> p a b f", a=2, b=2)[:, :, 0, :],
                                                     in1=lamt[:, j, :].rearrange("p (a b f) -> p a b f", a=2, b=2)[:, :, 1, :], op=ALU.mult), [RTMP], [RTMP])
            k.op(DVE, lambda e: e.reduce_sum(out=small[:, 0:2], in_=prod.rearrange("p (a f) -> p a f", a=2), axis=mybir.AxisListType.X), [RTMP], [RTMP])
            k.op(ACT, lambda e: e.activation(out=small[:, 2:4], in_=small[:, 0:2], func=AF.Exp), [RTMP], [RTMP])
            k.op(DVE, lambda e: e.tensor_tensor(out=small[:, 4:5], in0=small[:, 3:4], in1=small[:, 2:3], op=ALU.subtract), [RTMP], [RTMP])
            k.op(DVE, lambda e, j=j, li=li: e.tensor_scalar(out=nlam[:, j:j + 1], in0=small[:, 4:5], scalar1=-li, scalar2=None, op0=ALU.add), [RTMP], [CONST])
            k.op(DVE, lambda e, j=j, li=li: e.tensor_scalar(out=subl[:, j:j + 1], in0=subl[:, j:j + 1], scalar1=(1.0 - li), scalar2=None, op0=ALU.mult), [CONST], [CONST])

        for i in range(nlayers):
            kind, j = i % 3, i // 3
            if kind == 0:
                prep(f"win{i}", a_win_d[j], D, 3 * D, 128)
                prep(f"wout{i}", a_wout_d[j], D, D, 128)
            elif kind == 1:
                prep(f"wu{i}", g_win_d[0][:, 0:GW], D, GW, 128)
                prep(f"wv{i}", g_win_d[0][:, GW:2 * GW], D, GW, 256)
                prep(f"wout{i}", g_wout_d[0], GW, D, 128)
            else:
                prep(f"win{i}", c_win_d[0], D, 3 * D, 128)
                prep(f"wout{i}", c_wout_d[0], D, D, 128)
            prep(f"wgu{i}", f_wgu_d[i], D, 2 * DFF, 128)
            prep(f"wd{i}", f_wd_d[i], DFF, D, 128)

        if nlayers > 1:
            wsf = bigf[:, 1024:2048].rearrange("p (g s) -> p g s", g=8)
            wsTf = bigf[:, 2048:3072].rearrange("p (g t) -> p g t", g=8)
            bsr = bigf[:, 3072:4096].rearrange("p (g t) -> p g t", g=8)
            bvf = bigf[0:2, 4096:4096 + GW]
            bvt = bigf[0:2, 4096 + GW:4096 + 2 * GW]
            GS = Buf("gsetup")
            k.dma(SP, [(wsf, g_ws_d[0].rearrange("g t s -> t g s")),
                       (bsr.rearrange("p g t -> p (g t)"), g_bs_d[0].rearrange("g t -> (g t)").partition_broadcast(128)),
                       (bvf[0:1, :], g_bin_d[0:1, GW:2 * GW])], writes=[GS])
            for g in range(8):
                bk = wbank.next()
                k.group(PE, [lambda e, g=g, bk=bk: e.transpose(psum[:, bk, 0:128], wsf[:, g, :], ident[:])], [GS, CONST], [PB[bk]])
                k.op(DVE, lambda e, g=g, bk=bk: e.tensor_copy(out=wsTf[:, g, :], in_=psum[:, bk, 0:128]), [PB[bk]], [GS])
            k.op(POOL, lambda e: e.affine_select(out=wsTf, in_=wsTf, pattern=[[0, 8], [1, 128]], compare_op=ALU.is_ge,
                                                 fill=0.0, base=0, channel_multiplier=-1), [GS], [GS])
            k.op(DVE, lambda e: e.tensor_copy(out=wsTm[:], in_=wsTf), [GS], [CONST])
            rws = bigf[:, 7168:8192].rearrange("p (g t) -> p g t", g=8)
            for hlf in range(2):
                bk = wbank.next()
                k.group(PE, [lambda e, hlf=hlf, bk=bk: e.matmul(psum[:, bk, :], lhsT=onesf[:], rhs=wsTf[:, 4 * hlf:4 * hlf + 4, :], start=True, stop=True)],
                        [GS, CONST], [PB[bk]])
                k.op(DVE, lambda e, hlf=hlf, bk=bk: e.tensor_copy(out=rws[:, 4 * hlf:4 * hlf + 4, :], in_=psum[:, bk, :].rearrange("p (g t) -> p g t", g=4)), [PB[bk]], [GS])
            for jc in range(12):
                kk, r3 = jc // 3, jc % 3
                if r3 == 0:
                    parts = [(0, 128, 2 * kk)]
                elif r3 == 2:
                    parts = [(0, 128, 2 * kk + 1)]
                else:
                    parts = [(0, 64, 2 * kk), (64, 128, 2 * kk + 1)]
                for (p0, p1, g) in parts:
                    k.op(DVE, lambda e, jc=jc, p0=p0, p1=p1, g=g: e.scalar_tensor_tensor(
                        out=Rt[p0:p1, jc, :], in0=rws[p0:p1, g, :], scalar=lnb[p0:p1, jc:jc + 1], in1=bsr[p0:p1, g, :],
                        op0=ALU.mult, op1=ALU.add), [GS, CONST], [CONST])
            k.op(DVE, lambda e: e.tensor_copy(out=bvhl[0:1, :], in_=bvf[0:1, :]), [GS], [CONST])
            k.op(DVE, lambda e: e.tensor_copy(out=bvt[0:1, :], in_=bvhl[0:1, :]), [CONST], [GS])
            k.op(DVE, lambda e: e.tensor_tensor(out=bvt[0:1, :], in0=bvf[0:1, :], in1=bvt[0:1, :], op=ALU.subtract), [GS], [GS])
            bvl_tmp = sq[0:1, :, :].rearrange("p a t -> p (a t)")
            k.op(DVE, lambda e: e.tensor_copy(out=bvl_tmp, in_=bvt[0:1, :]), [GS], [GS] + SQ)
            k.dma(SP, [(bvhl[1:2, :], bvl_tmp)], reads=[GS] + SQ, writes=[CONST], sembuf=Buf("bvmove"))
            gs_done = [GS]
        else:
            gs_done = []

        pending = []

        def defer(n, fn):
            pending.append([n, fn])

        def tick():
            for it in list(pending):
                it[0] -= 1
                if it[0] <= 0:
                    pending.remove(it)
                    it[1]()

        def flush():
            while pending:
                tick()

        def load_w(pieces):
            si = ringrot.next()
            pairs = []
            views = []
            off = 0
            bufs = []
            for (name, n) in pieces:
                t_, b = scr[name]
                C, J = t_.shape[2], t_.shape[3]
                v = ring[:, si, off:off + C * J].rearrange("p (c j) -> p c j", c=C)
                pairs.append((v, t_[n]))
                views.append(v)
                off += C * J
                if b not in bufs:
                    bufs.append(b)
            assert off <= 2048
            k.dma(SP, pairs, reads=bufs, writes=[RING[si]])
            return si, views

        def norm_stage1(t):
            bk = wbank_cur[0].next()
            for c in range(NCH):
                qi = sqrot.next()
                k.op(ACT, lambda e, c=c, qi=qi: e.activation(out=sq[:, qi, :], in_=xs[:, c, tsl(t)], func=AF.Square), [XB[c][t]], [SQ[qi]])
                k.group(PE, [mm(bank(bk), onesb[:], sq[:, qi, :], c == 0, c == NCH - 1)], [SQ[qi], CONST], [PB[bk]])
            return bk

        def norm_stage2(t, bk, gcol, dst, dstB, nfeat=1024.0, eps=NORM_EPS):
            ti = tmrot.next()
            rs = tm[:, ti, :]
            k.op(DVE, lambda e: e.tensor_scalar(out=rs, in0=bank(bk), scalar1=1.0 / nfeat, scalar2=eps, op0=ALU.mult, op1=ALU.add), [PB[bk]], [TM[ti]])
            k.op(ACT, lambda e: e.activation(out=rs, in_=rs, func=AF.Sqrt), [TM[ti]], [TM[ti]])
            k.op(DVE, lambda e: e.reciprocal(out=rs, in_=rs), [TM[ti]], [TM[ti]])
            for c in range(NCH):
                k.op(DVE, lambda e, c=c: e.scalar_tensor_tensor(out=dst(c), in0=xs[:, c, tsl(t)], scalar=gcol(c), in1=rs, op0=ALU.mult, op1=ALU.mult),
                     [XB[c][t], TM[ti], CONST], [dstB(c)])

        def norm_tile_h(t, gtile, li):
            bk = norm_stage1(t)
            norm_stage2(t, bk, lambda c: gtile[:, li, c:c + 1], lambda c: hT[:, c, tsl(t)], lambda c: HB[c][t])

        wbank_cur = [wbank]

        def ffn(i):
            wbank_cur[0] = wbank
            for half in range(2):
                tiles = [2 * half, 2 * half + 1]
                if half == 0:
                    for t in tiles:
                        norm_tile_h(t, gffn, i)
                for fp in range(2):
                    for fl in range(11):
                        f = 11 * fp + fl
                        si, (wg, wu_) = load_w([(f"wgu{i}", f), (f"wgu{i}", NF + f)])
                        for tt, t in enumerate(tiles):
                            bg, bu_ = wbank.next(), wbank.next()
                            k.group(PE, [mm(bank(bg), wg[:, c, :], hT[:, c, tsl(t)], c == 0, c == NCH - 1) for c in range(NCH)],
                                    [RING[si]] + [HB[c][t] for c in range(NCH)], [PB[bg]])
                            k.group(PE, [mm(bank(bu_), wu_[:, c, :], hT[:, c, tsl(t)], c == 0, c == NCH - 1) for c in range(NCH)],
                                    [RING[si]] + [HB[c][t] for c in range(NCH)], [PB[bu_]])
                            ti = tmrot.next()
                            k.op(ACT, lambda e, ti=ti, bg=bg: e.activation(out=tm[:, ti, :], in_=bank(bg), func=AF.Silu), [PB[bg]], [TM[ti]])
                            k.op(DVE, lambda e, ti=ti, bu_=bu_, fl=fl, tt=tt: e.tensor_tensor(out=aT[:, fl, tt * 512:(tt + 1) * 512], in0=tm[:, ti, :], in1=bank(bu_), op=ALU.mult),
                                 [TM[ti], PB[bu_]], [AB[fl][tt]])
                        if half == 0 and fp == 0 and fl == 8:
                            for t in (2, 3):
                                norm_tile_h(t, gffn, i)
                    t_, b_ = scr[f"wd{i}"]
                    for dc in range(NCH):
                        si = ringrot.next()
                        v = ring[:, si, 0:11 * 128].rearrange("p (c j) -> p c j", c=11)
                        k.dma(SP, [(v, t_[dc][:, 11 * fp:11 * fp + 11, :])], reads=[b_], writes=[RING[si]])
                        for tt, t in enumerate(tiles):
                            bo = wbank.next()
                            k.group(PE, [mm(bank(bo), v[:, fl, :], aT[:, fl, tt * 512:(tt + 1) * 512], fl == 0, fl == 10) for fl in range(11)],
                                    [RING[si]] + [AB[fl][tt] for fl in range(11)], [PB[bo]])
                            k.op(DVE, lambda e, dc=dc, t=t, bo=bo: e.tensor_tensor(out=xs[:, dc, tsl(t)], in0=xs[:, dc, tsl(t)], in1=bank(bo), op=ALU.add),
                                 [PB[bo], XB[dc][t]], [XB[dc][t]])

        def out_proj(i, n_in):
            for dc in range(NCH):
                si, (wo,) = load_w([(f"wout{i}", dc)])
                for t in range(NT):
                    bo = wbank.next()
                    k.group(PE, [mm(bank(bo), wo[:, c, :], oT[:, c, tsl(t)], c == 0, c == n_in - 1) for c in range(n_in)],
                            [RING[si]] + [OB[c][t] for c in range(n_in)], [PB[bo]])
                    k.op(DVE, lambda e, dc=dc, t=t, bo=bo: e.tensor_tensor(out=xs[:, dc, tsl(t)], in0=xs[:, dc, tsl(t)], in1=bank(bo), op=ALU.add),
                         [PB[bo], XB[dc][t]], [XB[dc][t]])

        def attention(i, j):
            wbank_cur[0] = wbank4
            wb = wbank4
            def head(h):
                si1, (wq, wk) = load_w([(f"win{i}", h), (f"win{i}", 8 + h)])
                si2, (wv_,) = load_w([(f"win{i}", 16 + h)])
                for t in (range(NT) if "noqk" not in dbg else ()):
                    for m, (w_, scl) in enumerate(((wq, 0.125), (wk, 1.0))):
                        ba = wb.next()
                        k.group(PE, [mm(bank(ba), w_[:, c, :], hT[:, c, tsl(t)], c == 0, c == NCH - 1) for c in range(NCH)],
                                [RING[si1]] + [HB[c][t] for c in range(NCH)], [PB[ba]])
                        t1 = tmrot.next()
                        k.op(ACT, lambda e, t1=t1, ba=ba, scl=scl: e.activation(out=tm[:, t1, :], in_=bank(ba), func=AF.Copy, scale=scl), [PB[ba]], [TM[t1]])
                        qi = sqrot.next()
                        k.op(DVE, lambda e, qi=qi, t1=t1: e.tensor_copy(out=sq[:, qi, :], in_=tm[:, t1, :]), [TM[t1]], [SQ[qi]])
                        k.op(DVE, lambda e, t1=t1, t=t: e.tensor_tensor(out=tm[:, t1, :], in0=tm[:, t1, :], in1=cs[:, 0, tsl(t)], op=ALU.mult), [TM[t1], CSB], [TM[t1]])

                        def rope2(qi=qi, t1=t1, t=t, m=m):
                            bb = wb.next()
                            k.group(PE, [mm(bank(bb), permb[:], sq[:, qi, :], True, True)], [SQ[qi], CONST], [PB[bb]])
                            t2 = tmrot.next()
                            k.op(DVE, lambda e: e.tensor_tensor(out=tm[:, t2, :], in0=cs[:, 1, tsl(t)], in1=bank(bb), op=ALU.mult), [PB[bb], CSB], [TM[t2]])
                            k.op(DVE, lambda e: e.tensor_tensor(out=qk[:, m, tsl(t)], in0=tm[:, t1, :], in1=tm[:, t2, :], op=ALU.add),
                                 [TM[t1], TM[t2]], [QB[m][t]])
                        defer(2, rope2)
                        tick()
                for tg in (range(NT) if "nov" not in dbg else ()):
                    bv = wb.next()
                    fns = []
                    for bl in range(4):
                        tok0 = tg * 512 + bl * 128
                        for c in range(NCH):
                            fns.append(mm(psum[:, bv, bl * 128:(bl + 1) * 128], hT[:, c, tok0:tok0 + 128], wv_[:, c, :], c == 0, c == NCH - 1))
                    k.group(PE, fns, [RING[si2]] + [HB[c][tg] for c in range(NCH)], [PB[bv]])
                    k.op(ACT, lambda e, tg=tg, bv=bv: e.activation(out=vv[:, 4 * tg:4 * tg + 4, :], in_=bank(bv).rearrange("p (b e) -> p b e", b=4), func=AF.Copy),
                         [PB[bv]], [VB[tg]])
                    tick()
                flush()
                units = []
                for jq in range(NT):
                    for c in range(4 * jq + 4):
                        units.append((jq, c))
                state = {}

                def s_stage(u):
                    jq, c = units[u]
                    r = c - 4 * jq
                    q0 = 128 * r if r > 0 else 0
                    ba, bb = wb.next(), wb.next()
                    fns = []
                    for m, b_ in ((0, ba), (1, bb)):
                        fns.append(mm(psum[:, b_, q0:512], qk[64 * m:64 * m + 64, 1, c * 128:(c + 1) * 128],
                                      qk[64 * m:64 * m + 64, 0, jq * 512 + q0:(jq + 1) * 512], True, True))
                    k.group(PE, fns, [QB[0][jq], QB[1][c // 4]], [PB[ba], PB[bb]])
                    pi = ptrot.next()
                    k.op(ACT, lambda e: e.activation(out=pT[:, pi, 0, q0:512], in_=psum[:, ba, q0:512], func=AF.Exp), [PB[ba]], [PT[pi]])
                    k.op(ACT, lambda e: e.activation(out=pT[:, pi, 1, q0:512], in_=psum[:, bb, q0:512], func=AF.Exp), [PB[bb]], [PT[pi]])
                    if r >= 0:
                        k.op(DVE, lambda e: e.tensor_tensor(out=pT[:, pi, :, q0:q0 + 128], in0=pT[:, pi, :, q0:q0 + 128], in1=tri2[:], op=ALU.mult),
                             [PT[pi], CONST], [PT[pi]])
                    state[u] = (pi, q0)

                def pv_stage(u):
                    jq, c = units[u]
                    pi, q0 = state.pop(u)
                    last = (c == 4 * jq + 3)
                    fns = []
                    for m in range(2):
                        fns.append(mm(psum[:, 4 + m, q0:512], vv[:, c, :], pT[:, pi, m, q0:512], c == 0, last))
                    for m in range(2):
                        fns.append(mm(psum[:, 6 + m, q0:512], onesb[:], pT[:, pi, m, q0:512], c == 0, last))
                    k.group(PE, fns, [PT[pi], VB[c // 4], CONST], [PB[4], PB[5], PB[6], PB[7]])
                    if last and "noepi" not in dbg:
                        epilogue(jq)

                def epilogue(jq):
                    o01 = ep[:, 0:2, :]
                    r01 = ep[:, 2:4, :]
                    k.op(ACT, lambda e: e.activation(out=ep[:, 0, :], in_=psum[:, 4, :], func=AF.Copy), [PB[4]], [EPB[0]])
                    k.op(ACT, lambda e: e.activation(out=ep[:, 1, :], in_=psum[:, 5, :], func=AF.Copy), [PB[5]], [EPB[1]])
                    k.op(DVE, lambda e: e.reciprocal(out=ep[:, 2, :], in_=psum[:, 6, :]), [PB[6]], [EPB[2]])
                    k.op(DVE, lambda e: e.reciprocal(out=ep[:, 3, :], in_=psum[:, 7, :]), [PB[7]], [EPB[3]])
                    k.op(DVE, lambda e: e.tensor_tensor(out=o01, in0=o01, in1=r01, op=ALU.mult), [EPB[0], EPB[1], EPB[2], EPB[3]], [EPB[0], EPB[1]])
                    k.op(DVE, lambda e: e.scalar_tensor_tensor(out=ep[:, 2, :], in0=ep[:, 1, :], scalar=nlam[:, j:j + 1], in1=ep[:, 0, :], op0=ALU.mult, op1=ALU.add),
                         [EPB[0], EPB[1], CONST], [EPB[2]])
                    qi = sqrot.next()
                    k.op(ACT, lambda e: e.activation(out=sq[:, qi, :], in_=ep[:, 2, :], func=AF.Square), [EPB[2]], [SQ[qi]])

                    def ep2():
                        bs_ = wb.next()
                        k.group(PE, [mm(bank(bs_), onesb[:], sq[:, qi, :], True, True)], [SQ[qi], CONST], [PB[bs_]])
                        k.op(DVE, lambda e: e.tensor_scalar(out=ep[:, 3, :], in0=bank(bs_), scalar1=1.0 / 128.0, scalar2=SUBLN_EPS, op0=ALU.mult, op1=ALU.add), [PB[bs_]], [EPB[3]])
                        k.op(ACT, lambda e: e.activation(out=ep[:, 3, :], in_=ep[:, 3, :], func=AF.Sqrt), [EPB[3]], [EPB[3]])
                        k.op(DVE, lambda e: e.reciprocal(out=ep[:, 3, :], in_=ep[:, 3, :]), [EPB[3]], [EPB[3]])
                        k.op(DVE, lambda e: e.scalar_tensor_tensor(out=oT[:, h, tsl(jq)], in0=ep[:, 2, :], scalar=subl[:, j:j + 1], in1=ep[:, 3, :], op0=ALU.mult, op1=ALU.mult),
                             [EPB[2], EPB[3], CONST], [OB[h][jq]])
                    defer(2, ep2)

                nu = len(units) if "nocore" not in dbg else 0
                if nu:
                    s_stage(0)
                for u in range(nu):
                    if u + 1 < nu:
                        s_stage(u + 1)
                    pv_stage(u)
                    tick()
                flush()
            for h in range(NCH):
                head(h)
            wbank_cur[0] = wbank
            if "noout" not in dbg:
                out_proj(i, NCH)

        def shortconv(i):
            wbank_cur[0] = wbank
            k.op(DVE, lambda e: e.memset(hcv[:, 0:2], 0.0), [], [GM["hc"]])

            def chunk(jc):
                si1, (wgb, wgc) = load_w([(f"win{i}", jc), (f"win{i}", 8 + jc)])
                si2, (wxs,) = load_w([(f"win{i}", 16 + jc)])
                for t in range(NT):
                    t0 = t * 512
                    bks = []
                    for (w_, si) in ((wgb, si1), (wgc, si1), (wxs, si2)):
                        bk = wbank.next()
                        k.group(PE, [mm(bank(bk), w_[:, c, :], hT[:, c, tsl(t)], c == 0, c == NCH - 1) for c in range(NCH)],
                                [RING[si]] + [HB[c][t] for c in range(NCH)], [PB[bk]])
                        bks.append(bk)
                    ba, bb, bc = bks
                    t1 = tmrot.next()
                    k.op(ACT, lambda e, t1=t1, bc=bc: e.activation(out=tm[:, t1, :], in_=bank(bc), func=AF.Copy), [PB[bc]], [TM[t1]])
                    k.op(DVE, lambda e, t1=t1, bb=bb, t0=t0: e.tensor_tensor(out=hcv[:, 2 + t0:2 + t0 + 512], in0=tm[:, t1, :], in1=bank(bb), op=ALU.mult),
                         [TM[t1], PB[bb]], [GM["hc"]])
                    t2 = tmrot.next()
                    k.op(DVE, lambda e, t2=t2, t0=t0: e.tensor_scalar(out=tm[:, t2, :], in0=hcv[:, 2 + t0:2 + t0 + 512], scalar1=convw[:, 2, jc:jc + 1], scalar2=None, op0=ALU.mult),
                         [GM["hc"], CONST], [TM[t2]])
                    k.op(DVE, lambda e, t2=t2, t0=t0: e.scalar_tensor_tensor(out=tm[:, t2, :], in0=hcv[:, 1 + t0:1 + t0 + 512], scalar=convw[:, 1, jc:jc + 1], in1=tm[:, t2, :], op0=ALU.mult, op1=ALU.add),
                         [GM["hc"], CONST, TM[t2]], [TM[t2]])
                    k.op(DVE, lambda e, t2=t2, t0=t0: e.scalar_tensor_tensor(out=tm[:, t2, :], in0=hcv[:, t0:t0 + 512], scalar=convw[:, 0, jc:jc + 1], in1=tm[:, t2, :], op0=ALU.mult, op1=ALU.add),
                         [GM["hc"], CONST, TM[t2]], [TM[t2]])
                    k.op(DVE, lambda e, t2=t2, ba=ba, t=t: e.tensor_tensor(out=oT[:, jc, tsl(t)], in0=tm[:, t2, :], in1=bank(ba), op=ALU.mult),
                         [TM[t2], PB[ba]], [OB[jc][t]])
            for jc in range(NCH):
                chunk(jc)
            out_proj(i, NCH)

        def gmlp(i):
            wbank_cur[0] = wbank
            ZV = [GM["zv0"], GM["zv1"]]

            def tile(t):
                for n in range(12):
                    si, (w_,) = load_w([(f"wu{i}", n)])
                    bk = wbank.next()
                    k.group(PE, [mm(bank(bk), w_[:, c, :], hT[:, c, tsl(t)], c == 0, c == NCH - 1) for c in range(NCH)],
                            [RING[si]] + [HB[c][t] for c in range(NCH)], [PB[bk]])
                    k.op(ACT, lambda e, n=n, bk=bk: e.activation(out=uT[:, n, :], in_=bank(bk), func=AF.Gelu, bias=bu[:, n:n + 1]), [PB[bk], CONST], [GM["uT"]])
                for bp in range(2):
                    for w2 in range(3):
                        sl = [load_w([(f"wv{i}", 2 * w2 + hh)]) for hh in range(2)]
                        for b2 in range(2):
                            bl = 2 * bp + b2
                            tok0 = t * 512 + bl * 128
                            bk = wbank.next()
                            fns = []
                            for hh in range(2):
                                wvt = sl[hh][1][0]
                                col0 = (2 * w2 + hh) * 256
                                for c in range(NCH):
                                    fns.append(mm(psum[:, bk, hh * 256:(hh + 1) * 256], hT[:, c, tok0:tok0 + 128], wvt[:, c, :], c == 0, False))
                                fns.append(mm(psum[:, bk, hh * 256:(hh + 1) * 256], onesb[0:2, :], bvhl[0:2, col0:col0 + 256], False, True))
                            k.group(PE, fns, [RING[sl[0][0]], RING[sl[1][0]], CONST] + [HB[c][t] for c in range(NCH)], [PB[bk]])
                            k.op(ACT, lambda e, b2=b2, w2=w2, bk=bk: e.activation(out=zv4[:, b2, w2 * 512:(w2 + 1) * 512], in_=bank(bk), func=AF.Gelu), [PB[bk]], [ZV[b2]])
                    for b2 in range(2):
                        bl = 2 * bp + b2
                        for w2 in range(3):
                            k.op(DVE, lambda e, b2=b2, w2=w2: e.bn_stats(out=gst[:, 6 * w2:6 * w2 + 6], in_=zv4[:, b2, w2 * 512:(w2 + 1) * 512]), [ZV[b2]], [GM["stats"]])
                        k.op(DVE, lambda e: e.bn_aggr(out=gst[:, 18:20], in_=gst[:, 0:18]), [GM["stats"]], [GM["stats"]])
                        k.op(DVE, lambda e: e.tensor_scalar(out=gst[:, 20:21], in0=gst[:, 19:20], scalar1=LN_EPS, scalar2=None, op0=ALU.add), [GM["stats"]], [GM["stats"]])
                        k.op(ACT, lambda e: e.activation(out=gst[:, 20:21], in_=gst[:, 20:21], func=AF.Sqrt), [GM["stats"]], [GM["stats"]])
                        k.op(DVE, lambda e: e.reciprocal(out=gst[:, 20:21], in_=gst[:, 20:21]), [GM["stats"]], [GM["stats"]])
                        k.op(DVE, lambda e, b2=b2: e.tensor_scalar(out=vhat, in0=zv4[:, b2, :], scalar1=gst[:, 18:19], scalar2=gst[:, 20:21], op0=ALU.subtract, op1=ALU.mult),
                             [ZV[b2], GM["stats"]], [GM["vhat"]])
                        for grp in range(3):
                            bk = wbank.next()
                            fns = []
                            for jl in range(4):
                                jc = 4 * grp + jl
                                kk, r3 = jc // 3, jc % 3
                                if r3 == 0:
                                    parts = [(0, 128, 2 * kk)]
                                elif r3 == 2:
                                    parts = [(0, 128, 2 * kk + 1)]
                                else:
                                    parts = [(0, 64, 2 * kk), (64, 128, 2 * kk + 1)]
                                for (p0, p1, g) in parts:
                                    fns.append(mm(psum[p0:p1, bk, jl * 128:(jl + 1) * 128], vhat[:, jc * 128 + p0:jc * 128 + p1], wsTm[:, g, :], True, True))
                            k.group(PE, fns, [GM["vhat"], CONST], [PB[bk]])
                            sv = svrot.next()
                            SVB = GM["svt0"] if sv == 0 else GM["svt1"]
                            for jl in range(4):
                                jc = 4 * grp + jl
                                k.op(DVE, lambda e, jl=jl, jc=jc, bk=bk, sv=sv: e.scalar_tensor_tensor(out=svt[:, sv, jl, :], in0=psum[:, bk, jl * 128:(jl + 1) * 128], scalar=lng[:, jc:jc + 1], in1=Rt[:, jc, :],
                                                                                                      op0=ALU.mult, op1=ALU.add), [PB[bk], CONST], [SVB])
                            k.op(DVE, lambda e, grp=grp, bl=bl, sv=sv: e.tensor_tensor(out=uT[:, 4 * grp:4 * grp + 4, bl * 128:(bl + 1) * 128], in0=uT[:, 4 * grp:4 * grp + 4, bl * 128:(bl + 1) * 128],
                                                                                       in1=svt[:, sv, :, :], op=ALU.mult), [SVB, GM["uT"]], [GM["uT"]])
                for dc in range(NCH):
                    si, (wo,) = load_w([(f"wout{i}", dc)])
                    bo = wbank.next()
                    k.group(PE, [mm(bank(bo), wo[:, c, :], uT[:, c, :], c == 0, c == 11) for c in range(12)], [RING[si], GM["uT"]], [PB[bo]])
                    k.op(DVE, lambda e, dc=dc, bo=bo: e.tensor_tensor(out=xs[:, dc, tsl(t)], in0=xs[:, dc, tsl(t)], in1=bank(bo), op=ALU.add),
                         [PB[bo], XB[dc][t]], [XB[dc][t]])
            for t in range(NT):
                tile(t)

        for s in range(NS):
            for b in range(NB):
                gi = stgrot.next()
                k.dma(SP, [(stg[:, gi, :], x_d[s, b * 128:(b + 1) * 128, :])], writes=[STG[gi]] + ([RTMP] + gs_done if (s == 0 and b < 2) else []), sembuf=STG[gi])
                for hf in range(2):
                    bk = wbank.next()
                    k.group(PE, [lambda e, c=c, bk=bk, gi=gi, hf=hf: e.transpose(psum[:, bk, c * 128:(c + 1) * 128], stg[:, gi, (4 * hf + c) * 128:(4 * hf + c + 1) * 128], ident[:])
                                 for c in range(4)], [STG[gi], CONST], [PB[bk]])
                    eng = ACT if hf == 0 else DVE
                    wr = [XB[4 * hf + c][b // 4] for c in range(4)]
                    if hf == 0:
                        k.op(ACT, lambda e, bk=bk, b=b: e.activation(out=xs[:, 0:4, b * 128:(b + 1) * 128], in_=bank(bk).rearrange("p (c t) -> p c t", c=4), func=AF.Copy), [PB[bk]], wr)
                    else:
                        k.op(DVE, lambda e, bk=bk, b=b: e.tensor_copy(out=xs[:, 4:8, b * 128:(b + 1) * 128], in_=bank(bk).rearrange("p (c t) -> p c t", c=4)), [PB[bk]], wr)
            HI = 6.28125
            LO = 2.0 * math.pi - 6.28125
            angf = rtmp[:, 0, :]
            posi = rtmpi[:, 0, :]
            cosr = rtmp[:, 1, :]
            kf = rtmp[:, 2, :]
            ki = rtmpi[:, 2, :]
            k.dma(SP, [(posi, pos_d[s:s + 1, :].partition_broadcast(128))], reads=[STG[0], STG[1]], writes=[RTMP])
            k.op(DVE, lambda e: e.tensor_copy(out=angf, in_=posi), [RTMP], [RTMP])
            k.op(DVE, lambda e: e.tensor_scalar(out=angf, in0=angf, scalar1=cst[:, 0:1], scalar2=None, op0=ALU.mult), [RTMP, CONST], [RTMP])
            k.op(DVE, lambda e: e.tensor_scalar(out=cosr, in0=angf, scalar1=math.pi / 2.0, scalar2=None, op0=ALU.add), [RTMP], [RTMP])
            for which in range(2):
                src = cosr if which == 0 else angf
                k.op(DVE, lambda e, src=src: e.tensor_scalar(out=kf, in0=src, scalar1=1.0 / (2.0 * math.pi), scalar2=None, op0=ALU.mult), [RTMP], [RTMP])
                k.op(DVE, lambda e: e.tensor_copy(out=ki, in_=kf), [RTMP], [RTMP])
                k.op(DVE, lambda e: e.tensor_copy(out=kf, in_=ki), [RTMP], [RTMP])
                k.op(DVE, lambda e, src=src: e.scalar_tensor_tensor(out=src, in0=kf, scalar=-HI, in1=src, op0=ALU.mult, op1=ALU.add), [RTMP], [RTMP])
                k.op(DVE, lambda e, src=src: e.scalar_tensor_tensor(out=src, in0=kf, scalar=-LO, in1=src, op0=ALU.mult, op1=ALU.add), [RTMP], [RTMP])
                k.op(DVE, lambda e, src=src: e.tensor_scalar(out=src, in0=src, scalar1=math.pi, scalar2=-math.pi, op0=ALU.min, op1=ALU.max), [RTMP], [RTMP])
                if which == 0:
                    k.op(ACT, lambda e, src=src: e.activation(out=cs[:, 0, :], in_=src, func=AF.Sin), [RTMP], [CSB])
                else:
                    k.op(ACT, lambda e, src=src: e.activation(out=cs[:, 1, :], in_=src, func=AF.Sin, scale=cst[:, 1:2]), [RTMP, CONST], [CSB])

            for i in range(nlayers):
                kind, j = i % 3, i // 3
                for t in range(NT):
                    norm_tile_h(t, gmix, i)
                if "nomix" in dbg:
                    pass
                elif kind == 0:
                    attention(i, j)
                elif kind == 1:
                    gmlp(i)
                else:
                    shortconv(i)
                if "noffn" not in dbg:
                    ffn(i)

            wbank_cur[0] = wbank
            for t in range(NT):
                bk = norm_stage1(t)
                norm_stage2(t, bk, lambda c: gfin[:, c:c + 1], lambda c, t=t: xs[:, c, tsl(t)], lambda c, t=t: XB[c][t])
            for b in range(NB):
                gi = stgrot.next()
                for hf in range(2):
                    bk = wbank.next()
                    k.group(PE, [lambda e, c=c, bk=bk, b=b, hf=hf: e.transpose(psum[:, bk, c * 128:(c + 1) * 128], xs[:, 4 * hf + c, b * 128:(b + 1) * 128], ident[:])
                                 for c in range(4)], [XB[4 * hf + c][b // 4] for c in range(4)] + [CONST], [PB[bk]])
                    if hf == 0:
                        k.op(ACT, lambda e, bk=bk, gi=gi: e.activation(out=stg[:, gi, 0:512], in_=bank(bk), func=AF.Copy), [PB[bk]], [STG[gi]])
                    else:
                        k.op(DVE, lambda e, bk=bk, gi=gi: e.tensor_copy(out=stg[:, gi, 512:1024], in_=bank(bk)), [PB[bk]], [STG[gi]])
                k.dma(POOL, [(out_d[s, b * 128:(b + 1) * 128, :], stg[:, gi, :])], reads=[STG[gi]], writes=[OUTB], sembuf=OUTS[gi])

        for g_ in OUTS:
            if g_.dsem is not None:
                k._wait(POOL, (g_.dsem, g_.dcnt))
        k.emit()
    return nc


_CACHE = {}


def _consts():
    c = np.zeros((128, 132), np.float32)
    f = (1.0 / (np.float32(10000.0) ** (np.arange(0, 64, 2, dtype=np.float32) / np.float32(64)))).astype(np.float32)
    p = np.arange(128)
    c[:, 0] = f[p % 32]
    c[:, 1] = np.where((p // 32) % 2 == 0, -1.0, 1.0)
    perm = np.zeros((128, 128), np.float32)
    perm[p ^ 32, p] = 1.0
    c[:, 4:132] = perm
    return c


def kernel(**inputs):
    B = inputs["x"].shape[0]
    NS = B // N_CORES
    key = NS
    if key not in _CACHE:
        _CACHE[key] = build_program(NS)
    nc = _CACHE[key]
    cst = _consts()
    in_maps = []
    for r in range(N_CORES):
        m = {}
        for name, v in inputs.items():
            a = np.asarray(v)
            if name in ("x", "positions"):
                a = a[r * NS:(r + 1) * NS]
            m[name] = np.ascontiguousarray(a)
        m["cst"] = cst
        in_maps.append(m)
    res = run_bass_kernel_spmd(nc, in_maps, core_ids=list(range(N_CORES)))
    return np.concatenate([np.asarray(r_["out"]) for r_ in res.results], axis=0).astype(np.float32)
```

```python
import math
import contextlib
import numpy as np
import concourse.bass as bass
import concourse.mybir as mybir
from concourse.bass_utils import run_bass_kernel_spmd

F32 = mybir.dt.float32
BF16 = mybir.dt.bfloat16
I32 = mybir.dt.int32
AF = mybir.ActivationFunctionType
ALU = mybir.AluOpType

D = 1024
S = 2048
NCH = 8
NT = 4
NB = 16
DFF = 2816
NF = 22
GW = 1536
NORM_EPS = 1e-6
SUBLN_EPS = 1e-5
LN_EPS = 1e-5
N_CORES = 8


class Buf:
    __slots__ = ("name", "last_w", "readers", "dsem", "dcnt")

    def __init__(self, name=""):
        self.name = name
        self.last_w = None
        self.readers = {}
        self.dsem = None
        self.dcnt = 0


class EngW:
    def __init__(self, name, sem, is_pe=False):
        self.name = name
        self.sem = sem
        self.cnt = 0
        self.waited = {}
        self.prog = []
        self.is_pe = is_pe


class K:
    def __init__(self, nc, stack):
        self.nc = nc
        self.stack = stack
        self.nsem = 0
        self.pe = EngW("pe", self.new_sem("pe"), True)
        self.act = EngW("act", self.new_sem("act"))
        self.dve = EngW("dve", self.new_sem("dve"))
        self.pool = EngW("pool", self.new_sem("pool"))
        self.sp = EngW("sp", self.new_sem("sp"))

    def new_sem(self, name):
        self.nsem += 1
        return self.stack.enter_context(self.nc.semaphore(f"s{self.nsem}_{name}"))

    def sb(self, name, shape, dt):
        return self.stack.enter_context(self.nc.sbuf_tensor(name, list(shape), dt))

    def ps(self, name, shape, dt=F32):
        return self.stack.enter_context(self.nc.psum_tensor(name, list(shape), dt))

    def _wait(self, E, tok):
        if tok is None:
            return
        sem, val = tok
        if sem is E.sem and E.is_pe:
            return
        if E.waited.get(sem, 0) >= val:
            return
        E.waited[sem] = val
        E.prog.append(lambda eng, sem=sem, val=val: eng.wait_ge(sem, val))

    def _deps(self, E, reads, writes):
        for b in reads:
            self._wait(E, b.last_w)
        for b in writes:
            self._wait(E, b.last_w)
            for s, v in b.readers.items():
                self._wait(E, (s, v))

    def _mark(self, tok, reads, writes):
        for b in reads:
            if b.readers.get(tok[0], 0) < tok[1]:
                b.readers[tok[0]] = tok[1]
        for b in writes:
            b.last_w = tok
            b.readers = {}

    def op(self, E, fn, reads=(), writes=()):
        self._deps(E, reads, writes)
        E.cnt += 1
        sem = E.sem
        E.prog.append(lambda eng, fn=fn, sem=sem: fn(eng).then_inc(sem, 1))
        tok = (sem, E.cnt)
        self._mark(tok, reads, writes)
        return tok

    def group(self, E, fns, reads=(), writes=()):
        self._deps(E, reads, writes)
        E.cnt += 1
        sem = E.sem
        for f in fns[:-1]:
            E.prog.append(lambda eng, f=f: f(eng))
        E.prog.append(lambda eng, f=fns[-1], sem=sem: f(eng).then_inc(sem, 1))
        tok = (sem, E.cnt)
        self._mark(tok, reads, writes)
        return tok

    def dma(self, Q, pairs, reads=(), writes=(), sembuf=None, **kw):
        self._deps(Q, reads, writes)
        sb = sembuf if sembuf is not None else (writes[0] if writes else reads[0])
        if sb.dsem is None:
            sb.dsem = self.new_sem("d_" + sb.name)
        dsem = sb.dsem
        for (o, i) in pairs:
            sb.dcnt += 16
            Q.prog.append(lambda eng, o=o, i=i, dsem=dsem, kw=kw: eng.dma_start(out=o, in_=i, **kw).then_inc(dsem, 16))
        tok = (dsem, sb.dcnt)
        self._mark(tok, reads, writes)
        return tok

    def emit(self):
        nc = self.nc
        with nc.Block() as block:
            @block.tensor
            def _(e):
                for f in self.pe.prog:
                    f(e)

            @block.scalar
            def _(e):
                for f in self.act.prog:
                    f(e)

            @block.vector
            def _(e):
                for f in self.dve.prog:
                    f(e)

            @block.gpsimd
            def _(e):
                for f in self.pool.prog:
                    f(e)

            @block.sync
            def _(e):
                for f in self.sp.prog:
                    f(e)


class Rot:
    def __init__(self, items):
        self.items = list(items)
        self.i = 0

    def next(self):
        it = self.items[self.i % len(self.items)]
        self.i += 1
        return it


def mm(out, lhsT, rhs, start, stop):
    return lambda e: e.matmul(out, lhsT=lhsT, rhs=rhs, start=start, stop=stop)


def build_program(NS=4, nlayers=4, dbg=()):
    nc = bass.Bass("TRN2", target_bir_lowering=False)

    def din(name, shape, dt=F32):
        return nc.dram_tensor(name, list(shape), dt, kind="ExternalInput").ap()

    x_d = din("x", [NS, S, D])
    pos_d = din("positions", [NS, S], I32)
    mixn_d = din("mix_norm", [4, D])
    ffnn_d = din("ffn_norm", [4, D])
    finn_d = din("final_norm", [D])
    a_win_d = din("attn_w_in", [2, D, 3 * D])
    a_lam_d = din("attn_lambda", [2, 4, 64])
    a_sub_d = din("attn_subln", [2, 128])
    a_wout_d = din("attn_w_out", [2, D, D])
    g_win_d = din("gmlp_w_in", [1, D, 2 * GW])
    g_bin_d = din("gmlp_b_in", [1, 2 * GW])
    g_lng_d = din("gmlp_ln_g", [1, GW])
    g_lnb_d = din("gmlp_ln_b", [1, GW])
    g_ws_d = din("gmlp_w_s", [1, 8, 128, 128])
    g_bs_d = din("gmlp_b_s", [1, 8, 128])
    g_wout_d = din("gmlp_w_out", [1, GW, D])
    c_win_d = din("conv_w_in", [1, D, 3 * D])
    c_w_d = din("conv_w", [1, 3, D])
    c_wout_d = din("conv_w_out", [1, D, D])
    f_wgu_d = din("ffn_w_gate_up", [4, D, 2 * DFF])
    f_wd_d = din("ffn_w_down", [4, DFF, D])
    cst_d = din("cst", [128, 132])
    out_d = nc.dram_tensor("out", [NS, S, D], F32, kind="ExternalOutput").ap()

    with contextlib.ExitStack() as st:
        k = K(nc, st)
        PE, ACT, DVE, POOL, SP = k.pe, k.act, k.dve, k.pool, k.sp

        xs = k.sb("xs", [128, NCH, S], F32)
        hT = k.sb("hT", [128, NCH, S], BF16)
        big = k.sb("big", [128, NCH * S], BF16)
        qkv = k.sb("qkv", [128, 3 * S], BF16)
        qk = qkv[:, 0:2 * S].rearrange("p (m t) -> p m t", m=2)
        vv = qkv[:, 2 * S:3 * S].rearrange("p (b e) -> p b e", b=NB)
        pT = k.sb("pT", [128, 3, 2, 512], BF16)
        ring = k.sb("ring", [128, 4, 2048], BF16)
        cs = k.sb("cs", [128, 2, S], F32)
        tm = k.sb("tm", [128, 4, 512], F32)
        ep = tm
        sq = k.sb("sq", [128, 3, 512], BF16)
        ident = k.sb("ident", [128, 128], F32)
        onesf = k.sb("onesf", [128, 128], F32)
        onesb = k.sb("onesb", [128, 128], BF16)
        tri2 = k.sb("tri2", [128, 2, 128], BF16)
        permb = k.sb("permb", [128, 128], BF16)
        cst = k.sb("cst_sb", [128, 4], F32)
        gmix = k.sb("gmix", [128, 4, NCH], F32)
        gffn = k.sb("gffn", [128, 4, NCH], F32)
        gfin = k.sb("gfin", [128, NCH], F32)
        convw = k.sb("convw", [128, 3, NCH], F32)
        bu = k.sb("bu", [128, 12], F32)
        lng = k.sb("lng", [128, 12], F32)
        lnb = k.sb("lnb", [128, 12], F32)
        subl = k.sb("subl", [128, 2], F32)
        nlam = k.sb("nlam", [128, 2], F32)
        small = k.sb("small", [128, 16], F32)
        epst = k.sb("epst", [128, 4], F32)
        wsTm = k.sb("wsTm", [128, 8, 128], BF16)
        Rt = k.sb("Rt", [128, 12, 128], F32)
        bvhl = k.sb("bvhl", [2, GW], BF16)
        psum = k.ps("psum", [128, 8, 512], F32)

        aT = big[:, 0:11 * 1024].rearrange("p (f t) -> p f t", f=11)
        oT = big[:, 0:NCH * S].rearrange("p (c t) -> p c t", c=NCH)
        bigf = big[:, :].bitcast(F32)
        stg = bigf[:, 0:2048].rearrange("p (s d) -> p s d", s=2)
        rtmp = bigf[:, 2048:2048 + 3 * S].rearrange("p (a t) -> p a t", a=3)
        rtmpi = rtmp.bitcast(I32)
        uT = big[:, 0:12 * 512].rearrange("p (c t) -> p c t", c=12)
        zv4 = bigf[:, 3072:3072 + 2 * GW].rearrange("p (b c) -> p b c", b=2)
        vhat = big[:, 12288:12288 + GW]
        svt = bigf[:, 6912:6912 + 1024].rearrange("p (a c t) -> p a c t", a=2, c=4)
        gst = k.sb("gst", [128, 32], F32)
        hcv = qkv[:, :].bitcast(F32)[:, 0:2 + S]

        XB = [[Buf(f"x{c}_{t}") for t in range(NT)] for c in range(NCH)]
        HB = [[Buf(f"h{c}_{t}") for t in range(NT)] for c in range(NCH)]
        PB = [Buf(f"pb{i}") for i in range(8)]
        RING = [Buf(f"ring{i}") for i in range(4)]
        ringrot = Rot(range(4))
        SQ = [Buf(f"sq{i}") for i in range(3)]
        sqrot = Rot(range(3))
        TM = [Buf(f"tm{i}") for i in range(4)]
        tmrot = Rot(range(4))
        EPB = TM
        PT = [Buf(f"pT{i}") for i in range(3)]
        ptrot = Rot(range(3))
        QB = [[Buf(f"q{m}_{t}") for t in range(NT)] for m in range(2)]
        VB = [Buf(f"v{t}") for t in range(NT)]
        OB = [[Buf(f"o{c}_{t}") for t in range(NT)] for c in range(NCH)]
        AB = [[Buf(f"a{f}_{t}") for t in range(2)] for f in range(11)]
        CSB = Buf("cs")
        STG = [Buf("stg0"), Buf("stg1")]
        stgrot = Rot(range(2))
        RTMP = Buf("rtmp")
        CONST = Buf("const")
        OUTB = Buf("outd")
        OUTS = [Buf("outs0"), Buf("outs1")]
        GM = {n: Buf(n) for n in ["uT", "zv0", "zv1", "zv2", "zv3", "vhat", "svt0", "svt1", "stats", "hc"]}
        svrot = Rot(range(2))
        wbank = Rot(range(8))
        wbank4 = Rot(range(4))

        def bank(i):
            return psum[:, i, :]

        def tsl(t):
            return slice(t * 512, (t + 1) * 512)

        scr = {}

        def prep(name, W, Kdim, N, J):
            n_ch = N // J
            C = Kdim // 128
            t_ = nc.dram_tensor("scr_" + name, [n_ch, 128, C, J], BF16, kind="Internal").ap()
            b = Buf("scr_" + name)
            pairs = []
            for n in range(n_ch):
                pairs.append((t_[n], W[:, n * J:(n + 1) * J].rearrange("(c p) j -> p c j", p=128)))
            k.dma(POOL, pairs, writes=[b])
            scr[name] = (t_, b)

        k.dma(SP, [(cst[:], cst_d[:, 0:4])], writes=[CONST])
        k.dma(POOL, [(permb[:], cst_d[:, 4:132])], writes=[CONST], sembuf=Buf("permld"))
        ncs = dict(allow_slow_non_contiguous=True)
        k.dma(ACT, [(gmix[:], mixn_d.rearrange("l (c p) -> p l c", p=128)),
                   (gffn[:], ffnn_d.rearrange("l (c p) -> p l c", p=128)),
                   (gfin[:], finn_d.rearrange("(c p) -> p c", p=128)),
                   (convw[:], c_w_d[0].rearrange("w (c p) -> p w c", p=128)),
                   (bu[:], g_bin_d[0, 0:GW].rearrange("(c p) -> p c", p=128)),
                   (lng[:], g_lng_d[0].rearrange("(c p) -> p c", p=128)),
                   (lnb[:], g_lnb_d[0].rearrange("(c p) -> p c", p=128)),
                   (subl[:], a_sub_d.rearrange("j p -> p j"))], writes=[CONST], sembuf=Buf("parld"), **ncs)
        k.op(POOL, lambda e: e.memset(epst[:, 0:1], NORM_EPS), [], [CONST])
        k.op(POOL, lambda e: e.memset(epst[:, 1:2], SUBLN_EPS), [], [CONST])
        k.op(POOL, lambda e: e.memset(epst[:, 2:3], LN_EPS), [], [CONST])
        k.op(POOL, lambda e: e.memset(onesf[:], 1.0), [], [CONST])
        k.op(POOL, lambda e: e.memset(onesb[:], 1.0), [], [CONST])
        k.op(POOL, lambda e: e.affine_select(out=ident[:], in_=onesf[:], pattern=[[1, 128]], compare_op=ALU.is_equal,
                                             fill=0.0, base=0, channel_multiplier=-1), [CONST], [CONST])
        for m in range(2):
            k.op(POOL, lambda e, m=m: e.affine_select(out=tri2[:, m, :], in_=onesb[:], pattern=[[1, 128]], compare_op=ALU.is_ge,
                                                      fill=0.0, base=0, channel_multiplier=-1), [CONST], [CONST])

        lamt = bigf[:, 0:512].rearrange("p (j f) -> p j f", j=2)
        k.dma(SP, [(lamt[:, j, :], a_lam_d[j].rearrange("a f -> (a f)").partition_broadcast(128)) for j in range(2)],
              writes=[RTMP])
        for j in range(2):
            li = 0.8 - 0.6 * math.exp(-0.3 * (3 * j))
            prod = bigf[:, 512:640]
            k.op(DVE, lambda e, j=j: e.tensor_tensor(out=prod.rearrange("p (a f) -> p a f", a=2), in0=lamt[:, j, :].rearrange("p (a b f) -> p a b f", a=2, b=2)[:, :, 0, :],
                                                     in1=lamt[:, j, :].rearrange("p (a b f) -> p a b f", a=2, b=2)[:, :, 1, :], op=ALU.mult), [RTMP], [RTMP])
            k.op(DVE, lambda e: e.reduce_sum(out=small[:, 0:2], in_=prod.rearrange("p (a f) -> p a f", a=2), axis=mybir.AxisListType.X), [RTMP], [RTMP])
            k.op(ACT, lambda e: e.activation(out=small[:, 2:4], in_=small[:, 0:2], func=AF.Exp), [RTMP], [RTMP])
            k.op(DVE, lambda e: e.tensor_tensor(out=small[:, 4:5], in0=small[:, 3:4], in1=small[:, 2:3], op=ALU.subtract), [RTMP], [RTMP])
            k.op(DVE, lambda e, j=j, li=li: e.tensor_scalar(out=nlam[:, j:j + 1], in0=small[:, 4:5], scalar1=-li, scalar2=None, op0=ALU.add), [RTMP], [CONST])
            k.op(DVE, lambda e, j=j, li=li: e.tensor_scalar(out=subl[:, j:j + 1], in0=subl[:, j:j + 1], scalar1=(1.0 - li), scalar2=None, op0=ALU.mult), [CONST], [CONST])

        for i in range(nlayers):
            kind, j = i % 3, i // 3
            if kind == 0:
                prep(f"win{i}", a_win_d[j], D, 3 * D, 128)
                prep(f"wout{i}", a_wout_d[j], D, D, 128)
            elif kind == 1:
                prep(f"wu{i}", g_win_d[0][:, 0:GW], D, GW, 128)
                prep(f"wv{i}", g_win_d[0][:, GW:2 * GW], D, GW, 256)
                prep(f"wout{i}", g_wout_d[0], GW, D, 128)
            else:
                prep(f"win{i}", c_win_d[0], D, 3 * D, 128)
                prep(f"wout{i}", c_wout_d[0], D, D, 128)
            prep(f"wgu{i}", f_wgu_d[i], D, 2 * DFF, 128)
            prep(f"wd{i}", f_wd_d[i], DFF, D, 128)

        if nlayers > 1:
            wsf = bigf[:, 1024:2048].rearrange("p (g s) -> p g s", g=8)
            wsTf = bigf[:, 2048:3072].rearrange("p (g t) -> p g t", g=8)
            bsr = bigf[:, 3072:4096].rearrange("p (g t) -> p g t", g=8)
            bvf = bigf[0:2, 4096:4096 + GW]
            bvt = bigf[0:2, 4096 + GW:4096 + 2 * GW]
            GS = Buf("gsetup")
            k.dma(SP, [(wsf, g_ws_d[0].rearrange("g t s -> t g s")),
                       (bsr.rearrange("p g t -> p (g t)"), g_bs_d[0].rearrange("g t -> (g t)").partition_broadcast(128)),
                       (bvf[0:1, :], g_bin_d[0:1, GW:2 * GW])], writes=[GS])
            for g in range(8):
                bk = wbank.next()
                k.group(PE, [lambda e, g=g, bk=bk: e.transpose(psum[:, bk, 0:128], wsf[:, g, :], ident[:])], [GS, CONST], [PB[bk]])
                k.op(DVE, lambda e, g=g, bk=bk: e.tensor_copy(out=wsTf[:, g, :], in_=psum[:, bk, 0:128]), [PB[bk]], [GS])
            k.op(POOL, lambda e: e.affine_select(out=wsTf, in_=wsTf, pattern=[[0, 8], [1, 128]], compare_op=ALU.is_ge,
                                                 fill=0.0, base=0, channel_multiplier=-1), [GS], [GS])
            k.op(DVE, lambda e: e.tensor_copy(out=wsTm[:], in_=wsTf), [GS], [CONST])
            rws = bigf[:, 7168:8192].rearrange("p (g t) -> p g t", g=8)
            for hlf in range(2):
                bk = wbank.next()
                k.group(PE, [lambda e, hlf=hlf, bk=bk: e.matmul(psum[:, bk, :], lhsT=onesf[:], rhs=wsTf[:, 4 * hlf:4 * hlf + 4, :], start=True, stop=True)],
                        [GS, CONST], [PB[bk]])
                k.op(DVE, lambda e, hlf=hlf, bk=bk: e.tensor_copy(out=rws[:, 4 * hlf:4 * hlf + 4, :], in_=psum[:, bk, :].rearrange("p (g t) -> p g t", g=4)), [PB[bk]], [GS])
            for jc in range(12):
                kk, r3 = jc // 3, jc % 3
                if r3 == 0:
                    parts = [(0, 128, 2 * kk)]
                elif r3 == 2:
                    parts = [(0, 128, 2 * kk + 1)]
                else:
                    parts = [(0, 64, 2 * kk), (64, 128, 2 * kk + 1)]
                for (p0, p1, g) in parts:
                    k.op(DVE, lambda e, jc=jc, p0=p0, p1=p1, g=g: e.scalar_tensor_tensor(
                        out=Rt[p0:p1, jc, :], in0=rws[p0:p1, g, :], scalar=lnb[p0:p1, jc:jc + 1], in1=bsr[p0:p1, g, :],
                        op0=ALU.mult, op1=ALU.add), [GS, CONST], [CONST])
            k.op(DVE, lambda e: e.tensor_copy(out=bvhl[0:1, :], in_=bvf[0:1, :]), [GS], [CONST])
            k.op(DVE, lambda e: e.tensor_copy(out=bvt[0:1, :], in_=bvhl[0:1, :]), [CONST], [GS])
            k.op(DVE, lambda e: e.tensor_tensor(out=bvt[0:1, :], in0=bvf[0:1, :], in1=bvt[0:1, :], op=ALU.subtract), [GS], [GS])
            bvl_tmp = sq[0:1, :, :].rearrange("p a t -> p (a t)")
            k.op(DVE, lambda e: e.tensor_copy(out=bvl_tmp, in_=bvt[0:1, :]), [GS], [GS] + SQ)
            k.dma(SP, [(bvhl[1:2, :], bvl_tmp)], reads=[GS] + SQ, writes=[CONST], sembuf=Buf("bvmove"))
            gs_done = [GS]
        else:
            gs_done = []

        pending = []

        def defer(n, fn):
            pending.append([n, fn])

        def tick():
            for it in list(pending):
                it[0] -= 1
                if it[0] <= 0:
                    pending.remove(it)
                    it[1]()

        def flush():
            while pending:
                tick()

        def load_w(pieces):
            si = ringrot.next()
            pairs = []
            views = []
            off = 0
            bufs = []
            for (name, n) in pieces:
                t_, b = scr[name]
                C, J = t_.shape[2], t_.shape[3]
                v = ring[:, si, off:off + C * J].rearrange("p (c j) -> p c j", c=C)
                pairs.append((v, t_[n]))
                views.append(v)
                off += C * J
                if b not in bufs:
                    bufs.append(b)
            assert off <= 2048
            k.dma(SP, pairs, reads=bufs, writes=[RING[si]])
            return si, views

        def norm_stage1(t):
            bk = wbank_cur[0].next()
            for c in range(NCH):
                qi = sqrot.next()
                k.op(ACT, lambda e, c=c, qi=qi: e.activation(out=sq[:, qi, :], in_=xs[:, c, tsl(t)], func=AF.Square), [XB[c][t]], [SQ[qi]])
                k.group(PE, [mm(bank(bk), onesb[:], sq[:, qi, :], c == 0, c == NCH - 1)], [SQ[qi], CONST], [PB[bk]])
            return bk

        def norm_stage2(t, bk, gcol, dst, dstB, nfeat=1024.0, eps=NORM_EPS):
            ti = tmrot.next()
            rs = tm[:, ti, :]
            k.op(ACT, lambda e: e.activation(out=rs, in_=bank(bk), func=AF.Ln, scale=1.0 / nfeat, bias=epst[:, 0:1]), [PB[bk], CONST], [TM[ti]])
            k.op(ACT, lambda e: e.activation(out=rs, in_=rs, func=AF.Exp, scale=-0.5), [TM[ti]], [TM[ti]])
            for c in range(NCH):
                k.op(DVE, lambda e, c=c: e.scalar_tensor_tensor(out=dst(c), in0=xs[:, c, tsl(t)], scalar=gcol(c), in1=rs, op0=ALU.mult, op1=ALU.mult),
                     [XB[c][t], TM[ti], CONST], [dstB(c)])

        def norm_tile_h(t, gtile, li):
            bk = norm_stage1(t)
            norm_stage2(t, bk, lambda c: gtile[:, li, c:c + 1], lambda c: hT[:, c, tsl(t)], lambda c: HB[c][t])

        wbank_cur = [wbank]

        def ffn(i):
            wbank_cur[0] = wbank
            for half in range(2):
                tiles = [2 * half, 2 * half + 1]
                if half == 0:
                    for t in tiles:
                        norm_tile_h(t, gffn, i)
                for fp in range(2):
                    for fl in range(11):
                        f = 11 * fp + fl
                        si, (wg, wu_) = load_w([(f"wgu{i}", f), (f"wgu{i}", NF + f)])
                        for tt, t in enumerate(tiles):
                            bg, bu_ = wbank.next(), wbank.next()
                            k.group(PE, [mm(bank(bg), wg[:, c, :], hT[:, c, tsl(t)], c == 0, c == NCH - 1) for c in range(NCH)],
                                    [RING[si]] + [HB[c][t] for c in range(NCH)], [PB[bg]])
                            k.group(PE, [mm(bank(bu_), wu_[:, c, :], hT[:, c, tsl(t)], c == 0, c == NCH - 1) for c in range(NCH)],
                                    [RING[si]] + [HB[c][t] for c in range(NCH)], [PB[bu_]])
                            ti = tmrot.next()
                            k.op(ACT, lambda e, ti=ti, bg=bg: e.activation(out=tm[:, ti, :], in_=bank(bg), func=AF.Silu), [PB[bg]], [TM[ti]])
                            k.op(DVE, lambda e, ti=ti, bu_=bu_, fl=fl, tt=tt: e.tensor_tensor(out=aT[:, fl, tt * 512:(tt + 1) * 512], in0=tm[:, ti, :], in1=bank(bu_), op=ALU.mult),
                                 [TM[ti], PB[bu_]], [AB[fl][tt]])
                        if half == 0 and fp == 0 and fl == 8:
                            for t in (2, 3):
                                norm_tile_h(t, gffn, i)
                    t_, b_ = scr[f"wd{i}"]
                    for dc in range(NCH):
                        si = ringrot.next()
                        v = ring[:, si, 0:11 * 128].rearrange("p (c j) -> p c j", c=11)
                        k.dma(SP, [(v, t_[dc][:, 11 * fp:11 * fp + 11, :])], reads=[b_], writes=[RING[si]])
                        for tt, t in enumerate(tiles):
                            bo = wbank.next()
                            k.group(PE, [mm(bank(bo), v[:, fl, :], aT[:, fl, tt * 512:(tt + 1) * 512], fl == 0, fl == 10) for fl in range(11)],
                                    [RING[si]] + [AB[fl][tt] for fl in range(11)], [PB[bo]])
                            k.op(DVE, lambda e, dc=dc, t=t, bo=bo: e.tensor_tensor(out=xs[:, dc, tsl(t)], in0=xs[:, dc, tsl(t)], in1=bank(bo), op=ALU.add),
                                 [PB[bo], XB[dc][t]], [XB[dc][t]])

        def out_proj(i, n_in):
            for dc in range(NCH):
                si, (wo,) = load_w([(f"wout{i}", dc)])
                for t in range(NT):
                    bo = wbank.next()
                    k.group(PE, [mm(bank(bo), wo[:, c, :], oT[:, c, tsl(t)], c == 0, c == n_in - 1) for c in range(n_in)],
                            [RING[si]] + [OB[c][t] for c in range(n_in)], [PB[bo]])
                    k.op(DVE, lambda e, dc=dc, t=t, bo=bo: e.tensor_tensor(out=xs[:, dc, tsl(t)], in0=xs[:, dc, tsl(t)], in1=bank(bo), op=ALU.add),
                         [PB[bo], XB[dc][t]], [XB[dc][t]])

        def attention(i, j):
            wbank_cur[0] = wbank4
            wb = wbank4
            def head(h):
                si1, (wq, wk) = load_w([(f"win{i}", h), (f"win{i}", 8 + h)])
                si2, (wv_,) = load_w([(f"win{i}", 16 + h)])
                for t in (range(NT) if "noqk" not in dbg else ()):
                    for m, (w_, scl) in enumerate(((wq, 0.125), (wk, 1.0))):
                        ba = wb.next()
                        k.group(PE, [mm(bank(ba), w_[:, c, :], hT[:, c, tsl(t)], c == 0, c == NCH - 1) for c in range(NCH)],
                                [RING[si1]] + [HB[c][t] for c in range(NCH)], [PB[ba]])
                        t1 = tmrot.next()
                        k.op(ACT, lambda e, t1=t1, ba=ba, scl=scl: e.activation(out=tm[:, t1, :], in_=bank(ba), func=AF.Copy, scale=scl), [PB[ba]], [TM[t1]])
                        qi = sqrot.next()
                        k.op(DVE, lambda e, qi=qi, t1=t1: e.tensor_copy(out=sq[:, qi, :], in_=tm[:, t1, :]), [TM[t1]], [SQ[qi]])
                        k.op(DVE, lambda e, t1=t1, t=t: e.tensor_tensor(out=tm[:, t1, :], in0=tm[:, t1, :], in1=cs[:, 0, tsl(t)], op=ALU.mult), [TM[t1], CSB], [TM[t1]])

                        def rope2(qi=qi, t1=t1, t=t, m=m):
                            bb = wb.next()
                            k.group(PE, [mm(bank(bb), permb[:], sq[:, qi, :], True, True)], [SQ[qi], CONST], [PB[bb]])
                            t2 = tmrot.next()
                            k.op(DVE, lambda e: e.tensor_tensor(out=tm[:, t2, :], in0=cs[:, 1, tsl(t)], in1=bank(bb), op=ALU.mult), [PB[bb], CSB], [TM[t2]])
                            k.op(DVE, lambda e: e.tensor_tensor(out=qk[:, m, tsl(t)], in0=tm[:, t1, :], in1=tm[:, t2, :], op=ALU.add),
                                 [TM[t1], TM[t2]], [QB[m][t]])
                        defer(2, rope2)
                        tick()
                for tg in (range(NT) if "nov" not in dbg else ()):
                    bv = wb.next()
                    fns = []
                    for bl in range(4):
                        tok0 = tg * 512 + bl * 128
                        for c in range(NCH):
                            fns.append(mm(psum[:, bv, bl * 128:(bl + 1) * 128], hT[:, c, tok0:tok0 + 128], wv_[:, c, :], c == 0, c == NCH - 1))
                    k.group(PE, fns, [RING[si2]] + [HB[c][tg] for c in range(NCH)], [PB[bv]])
                    k.op(ACT, lambda e, tg=tg, bv=bv: e.activation(out=vv[:, 4 * tg:4 * tg + 4, :], in_=bank(bv).rearrange("p (b e) -> p b e", b=4), func=AF.Copy),
                         [PB[bv]], [VB[tg]])
                    tick()
                flush()
                units = []
                for jq in range(NT):
                    for c in range(4 * jq + 4):
                        units.append((jq, c))
                state = {}

                def s_stage(u):
                    jq, c = units[u]
                    r = c - 4 * jq
                    q0 = 128 * r if r > 0 else 0
                    ba, bb = wb.next(), wb.next()
                    fns = []
                    for m, b_ in ((0, ba), (1, bb)):
                        fns.append(mm(psum[:, b_, q0:512], qk[64 * m:64 * m + 64, 1, c * 128:(c + 1) * 128],
                                      qk[64 * m:64 * m + 64, 0, jq * 512 + q0:(jq + 1) * 512], True, True))
                    k.group(PE, fns, [QB[0][jq], QB[1][c // 4]], [PB[ba], PB[bb]])
                    pi = ptrot.next()
                    k.op(ACT, lambda e: e.activation(out=pT[:, pi, 0, q0:512], in_=psum[:, ba, q0:512], func=AF.Exp), [PB[ba]], [PT[pi]])
                    k.op(ACT, lambda e: e.activation(out=pT[:, pi, 1, q0:512], in_=psum[:, bb, q0:512], func=AF.Exp), [PB[bb]], [PT[pi]])
                    if r >= 0:
                        k.op(DVE, lambda e: e.tensor_tensor(out=pT[:, pi, :, q0:q0 + 128], in0=pT[:, pi, :, q0:q0 + 128], in1=tri2[:], op=ALU.mult),
                             [PT[pi], CONST], [PT[pi]])
                    state[u] = (pi, q0)

                def pv_stage(u):
                    jq, c = units[u]
                    pi, q0 = state.pop(u)
                    last = (c == 4 * jq + 3)
                    fns = []
                    for m in range(2):
                        fns.append(mm(psum[:, 4 + m, q0:512], vv[:, c, :], pT[:, pi, m, q0:512], c == 0, last))
                    for m in range(2):
                        fns.append(mm(psum[:, 6 + m, q0:512], onesb[:], pT[:, pi, m, q0:512], c == 0, last))
                    k.group(PE, fns, [PT[pi], VB[c // 4], CONST], [PB[4], PB[5], PB[6], PB[7]])
                    if last and "noepi" not in dbg:
                        epilogue(jq)

                def epilogue(jq):
                    o01 = ep[:, 0:2, :]
                    r01 = ep[:, 2:4, :]
                    k.op(ACT, lambda e: e.activation(out=ep[:, 2, :], in_=psum[:, 6, :], func=AF.Ln), [PB[6]], [EPB[2]])
                    k.op(DVE, lambda e: e.tensor_copy(out=ep[:, 0, :], in_=psum[:, 4, :]), [PB[4]], [EPB[0]])
                    k.op(ACT, lambda e: e.activation(out=ep[:, 3, :], in_=psum[:, 7, :], func=AF.Ln), [PB[7]], [EPB[3]])
                    k.op(DVE, lambda e: e.tensor_copy(out=ep[:, 1, :], in_=psum[:, 5, :]), [PB[5]], [EPB[1]])
                    k.op(ACT, lambda e: e.activation(out=r01, in_=r01, func=AF.Exp, scale=-1.0), [EPB[2], EPB[3]], [EPB[2], EPB[3]])
                    k.op(DVE, lambda e: e.tensor_tensor(out=o01, in0=o01, in1=r01, op=ALU.mult), [EPB[0], EPB[1], EPB[2], EPB[3]], [EPB[0], EPB[1]])
                    k.op(DVE, lambda e: e.scalar_tensor_tensor(out=ep[:, 2, :], in0=ep[:, 1, :], scalar=nlam[:, j:j + 1], in1=ep[:, 0, :], op0=ALU.mult, op1=ALU.add),
                         [EPB[0], EPB[1], CONST], [EPB[2]])
                    qi = sqrot.next()
                    k.op(ACT, lambda e: e.activation(out=sq[:, qi, :], in_=ep[:, 2, :], func=AF.Square), [EPB[2]], [SQ[qi]])

                    def ep2():
                        bs_ = wb.next()
                        k.group(PE, [mm(bank(bs_), onesb[:], sq[:, qi, :], True, True)], [SQ[qi], CONST], [PB[bs_]])
                        k.op(ACT, lambda e: e.activation(out=ep[:, 3, :], in_=bank(bs_), func=AF.Ln, scale=1.0 / 128.0, bias=epst[:, 1:2]), [PB[bs_], CONST], [EPB[3]])
                        k.op(ACT, lambda e: e.activation(out=ep[:, 3, :], in_=ep[:, 3, :], func=AF.Exp, scale=-0.5), [EPB[3]], [EPB[3]])
                        k.op(DVE, lambda e: e.scalar_tensor_tensor(out=oT[:, h, tsl(jq)], in0=ep[:, 2, :], scalar=subl[:, j:j + 1], in1=ep[:, 3, :], op0=ALU.mult, op1=ALU.mult),
                             [EPB[2], EPB[3], CONST], [OB[h][jq]])
                    defer(2, ep2)

                nu = len(units) if "nocore" not in dbg else 0
                if nu:
                    s_stage(0)
                for u in range(nu):
                    if u + 1 < nu:
                        s_stage(u + 1)
                    pv_stage(u)
                    tick()
                flush()
            for h in range(NCH):
                head(h)
            wbank_cur[0] = wbank
            if "noout" not in dbg:
                out_proj(i, NCH)

        def shortconv(i):
            wbank_cur[0] = wbank
            k.op(DVE, lambda e: e.memset(hcv[:, 0:2], 0.0), [], [GM["hc"]])

            def chunk(jc):
                si1, (wgb, wgc) = load_w([(f"win{i}", jc), (f"win{i}", 8 + jc)])
                si2, (wxs,) = load_w([(f"win{i}", 16 + jc)])
                for t in range(NT):
                    t0 = t * 512
                    bks = []
                    for (w_, si) in ((wgb, si1), (wgc, si1), (wxs, si2)):
                        bk = wbank.next()
                        k.group(PE, [mm(bank(bk), w_[:, c, :], hT[:, c, tsl(t)], c == 0, c == NCH - 1) for c in range(NCH)],
                                [RING[si]] + [HB[c][t] for c in range(NCH)], [PB[bk]])
                        bks.append(bk)
                    ba, bb, bc = bks
                    t1 = tmrot.next()
                    k.op(ACT, lambda e, t1=t1, bc=bc: e.activation(out=tm[:, t1, :], in_=bank(bc), func=AF.Copy), [PB[bc]], [TM[t1]])
                    k.op(DVE, lambda e, t1=t1, bb=bb, t0=t0: e.tensor_tensor(out=hcv[:, 2 + t0:2 + t0 + 512], in0=tm[:, t1, :], in1=bank(bb), op=ALU.mult),
                         [TM[t1], PB[bb]], [GM["hc"]])
                    t2 = tmrot.next()
                    k.op(DVE, lambda e, t2=t2, t0=t0: e.tensor_scalar(out=tm[:, t2, :], in0=hcv[:, 2 + t0:2 + t0 + 512], scalar1=convw[:, 2, jc:jc + 1], scalar2=None, op0=ALU.mult),
                         [GM["hc"], CONST], [TM[t2]])
                    k.op(DVE, lambda e, t2=t2, t0=t0: e.scalar_tensor_tensor(out=tm[:, t2, :], in0=hcv[:, 1 + t0:1 + t0 + 512], scalar=convw[:, 1, jc:jc + 1], in1=tm[:, t2, :], op0=ALU.mult, op1=ALU.add),
                         [GM["hc"], CONST, TM[t2]], [TM[t2]])
                    k.op(DVE, lambda e, t2=t2, t0=t0: e.scalar_tensor_tensor(out=tm[:, t2, :], in0=hcv[:, t0:t0 + 512], scalar=convw[:, 0, jc:jc + 1], in1=tm[:, t2, :], op0=ALU.mult, op1=ALU.add),
                         [GM["hc"], CONST, TM[t2]], [TM[t2]])
                    k.op(DVE, lambda e, t2=t2, ba=ba, t=t: e.tensor_tensor(out=oT[:, jc, tsl(t)], in0=tm[:, t2, :], in1=bank(ba), op=ALU.mult),
                         [TM[t2], PB[ba]], [OB[jc][t]])
            for jc in range(NCH):
                chunk(jc)
            out_proj(i, NCH)

        def gmlp(i):
            wbank_cur[0] = wbank
            ZV = [GM["zv0"], GM["zv1"]]

            def tile(t):
                for n in range(12):
                    si, (w_,) = load_w([(f"wu{i}", n)])
                    bk = wbank.next()
                    k.group(PE, [mm(bank(bk), w_[:, c, :], hT[:, c, tsl(t)], c == 0, c == NCH - 1) for c in range(NCH)],
                            [RING[si]] + [HB[c][t] for c in range(NCH)], [PB[bk]])
                    k.op(ACT, lambda e, n=n, bk=bk: e.activation(out=uT[:, n, :], in_=bank(bk), func=AF.Gelu, bias=bu[:, n:n + 1]), [PB[bk], CONST], [GM["uT"]])
                for bp in range(2):
                    for w2 in range(3):
                        sl = [load_w([(f"wv{i}", 2 * w2 + hh)]) for hh in range(2)]
                        for b2 in range(2):
                            bl = 2 * bp + b2
                            tok0 = t * 512 + bl * 128
                            bk = wbank.next()
                            fns = []
                            for hh in range(2):
                                wvt = sl[hh][1][0]
                                col0 = (2 * w2 + hh) * 256
                                for c in range(NCH):
                                    fns.append(mm(psum[:, bk, hh * 256:(hh + 1) * 256], hT[:, c, tok0:tok0 + 128], wvt[:, c, :], c == 0, False))
                                fns.append(mm(psum[:, bk, hh * 256:(hh + 1) * 256], onesb[0:2, :], bvhl[0:2, col0:col0 + 256], False, True))
                            k.group(PE, fns, [RING[sl[0][0]], RING[sl[1][0]], CONST] + [HB[c][t] for c in range(NCH)], [PB[bk]])
                            k.op(ACT, lambda e, b2=b2, w2=w2, bk=bk: e.activation(out=zv4[:, b2, w2 * 512:(w2 + 1) * 512], in_=bank(bk), func=AF.Gelu), [PB[bk]], [ZV[b2]])
                    for b2 in range(2):
                        bl = 2 * bp + b2
                        for w2 in range(3):
                            k.op(DVE, lambda e, b2=b2, w2=w2: e.bn_stats(out=gst[:, 6 * w2:6 * w2 + 6], in_=zv4[:, b2, w2 * 512:(w2 + 1) * 512]), [ZV[b2]], [GM["stats"]])
                        k.op(DVE, lambda e: e.bn_aggr(out=gst[:, 18:20], in_=gst[:, 0:18]), [GM["stats"]], [GM["stats"]])
                        k.op(ACT, lambda e: e.activation(out=gst[:, 20:21], in_=gst[:, 19:20], func=AF.Ln, bias=epst[:, 2:3]), [GM["stats"], CONST], [GM["stats"]])
                        k.op(ACT, lambda e: e.activation(out=gst[:, 20:21], in_=gst[:, 20:21], func=AF.Exp, scale=-0.5), [GM["stats"]], [GM["stats"]])
                        k.op(DVE, lambda e, b2=b2: e.tensor_scalar(out=vhat, in0=zv4[:, b2, :], scalar1=gst[:, 18:19], scalar2=gst[:, 20:21], op0=ALU.subtract, op1=ALU.mult),
                             [ZV[b2], GM["stats"]], [GM["vhat"]])
                        for grp in range(3):
                            bk = wbank.next()
                            fns = []
                            for jl in range(4):
                                jc = 4 * grp + jl
                                kk, r3 = jc // 3, jc % 3
                                if r3 == 0:
                                    parts = [(0, 128, 2 * kk)]
                                elif r3 == 2:
                                    parts = [(0, 128, 2 * kk + 1)]
                                else:
                                    parts = [(0, 64, 2 * kk), (64, 128, 2 * kk + 1)]
                                for (p0, p1, g) in parts:
                                    fns.append(mm(psum[p0:p1, bk, jl * 128:(jl + 1) * 128], vhat[:, jc * 128 + p0:jc * 128 + p1], wsTm[:, g, :], True, True))
                            k.group(PE, fns, [GM["vhat"], CONST], [PB[bk]])
                            sv = svrot.next()
                            SVB = GM["svt0"] if sv == 0 else GM["svt1"]
                            for jl in range(4):
                                jc = 4 * grp + jl
                                k.op(DVE, lambda e, jl=jl, jc=jc, bk=bk, sv=sv: e.scalar_tensor_tensor(out=svt[:, sv, jl, :], in0=psum[:, bk, jl * 128:(jl + 1) * 128], scalar=lng[:, jc:jc + 1], in1=Rt[:, jc, :],
                                                                                                      op0=ALU.mult, op1=ALU.add), [PB[bk], CONST], [SVB])
                            k.op(DVE, lambda e, grp=grp, bl=bl, sv=sv: e.tensor_tensor(out=uT[:, 4 * grp:4 * grp + 4, bl * 128:(bl + 1) * 128], in0=uT[:, 4 * grp:4 * grp + 4, bl * 128:(bl + 1) * 128],
                                                                                       in1=svt[:, sv, :, :], op=ALU.mult), [SVB, GM["uT"]], [GM["uT"]])
                for dc in range(NCH):
                    si, (wo,) = load_w([(f"wout{i}", dc)])
                    bo = wbank.next()
                    k.group(PE, [mm(bank(bo), wo[:, c, :], uT[:, c, :], c == 0, c == 11) for c in range(12)], [RING[si], GM["uT"]], [PB[bo]])
                    k.op(DVE, lambda e, dc=dc, bo=bo: e.tensor_tensor(out=xs[:, dc, tsl(t)], in0=xs[:, dc, tsl(t)], in1=bank(bo), op=ALU.add),
                         [PB[bo], XB[dc][t]], [XB[dc][t]])
            for t in range(NT):
                tile(t)

        for s in range(NS):
            for b in range(NB):
                gi = stgrot.next()
                k.dma(SP, [(stg[:, gi, :], x_d[s, b * 128:(b + 1) * 128, :])], writes=[STG[gi]] + ([RTMP] + gs_done if (s == 0 and b < 2) else []), sembuf=STG[gi])
                for hf in range(2):
                    bk = wbank.next()
                    k.group(PE, [lambda e, c=c, bk=bk, gi=gi, hf=hf: e.transpose(psum[:, bk, c * 128:(c + 1) * 128], stg[:, gi, (4 * hf + c) * 128:(4 * hf + c + 1) * 128], ident[:])
                                 for c in range(4)], [STG[gi], CONST], [PB[bk]])
                    eng = ACT if hf == 0 else DVE
                    wr = [XB[4 * hf + c][b // 4] for c in range(4)]
                    if hf == 0:
                        k.op(ACT, lambda e, bk=bk, b=b: e.activation(out=xs[:, 0:4, b * 128:(b + 1) * 128], in_=bank(bk).rearrange("p (c t) -> p c t", c=4), func=AF.Copy), [PB[bk]], wr)
                    else:
                        k.op(DVE, lambda e, bk=bk, b=b: e.tensor_copy(out=xs[:, 4:8, b * 128:(b + 1) * 128], in_=bank(bk).rearrange("p (c t) -> p c t", c=4)), [PB[bk]], wr)
            HI = 6.28125
            LO = 2.0 * math.pi - 6.28125
            angf = rtmp[:, 0, :]
            posi = rtmpi[:, 0, :]
            cosr = rtmp[:, 1, :]
            kf = rtmp[:, 2, :]
            ki = rtmpi[:, 2, :]
            k.dma(SP, [(posi, pos_d[s:s + 1, :].partition_broadcast(128))], reads=[STG[0], STG[1]], writes=[RTMP])
            k.op(DVE, lambda e: e.tensor_copy(out=angf, in_=posi), [RTMP], [RTMP])
            k.op(DVE, lambda e: e.tensor_scalar(out=angf, in0=angf, scalar1=cst[:, 0:1], scalar2=None, op0=ALU.mult), [RTMP, CONST], [RTMP])
            k.op(DVE, lambda e: e.tensor_scalar(out=cosr, in0=angf, scalar1=math.pi / 2.0, scalar2=None, op0=ALU.add), [RTMP], [RTMP])
            for which in range(2):
                src = cosr if which == 0 else angf
                k.op(DVE, lambda e, src=src: e.tensor_scalar(out=kf, in0=src, scalar1=1.0 / (2.0 * math.pi), scalar2=None, op0=ALU.mult), [RTMP], [RTMP])
                k.op(DVE, lambda e: e.tensor_copy(out=ki, in_=kf), [RTMP], [RTMP])
                k.op(DVE, lambda e: e.tensor_copy(out=kf, in_=ki), [RTMP], [RTMP])
                k.op(DVE, lambda e, src=src: e.scalar_tensor_tensor(out=src, in0=kf, scalar=-HI, in1=src, op0=ALU.mult, op1=ALU.add), [RTMP], [RTMP])
                k.op(DVE, lambda e, src=src: e.scalar_tensor_tensor(out=src, in0=kf, scalar=-LO, in1=src, op0=ALU.mult, op1=ALU.add), [RTMP], [RTMP])
                k.op(DVE, lambda e, src=src: e.tensor_scalar(out=src, in0=src, scalar1=math.pi, scalar2=-math.pi, op0=ALU.min, op1=ALU.max), [RTMP], [RTMP])
                if which == 0:
                    k.op(ACT, lambda e, src=src: e.activation(out=cs[:, 0, :], in_=src, func=AF.Sin), [RTMP], [CSB])
                else:
                    k.op(ACT, lambda e, src=src: e.activation(out=cs[:, 1, :], in_=src, func=AF.Sin, scale=cst[:, 1:2]), [RTMP, CONST], [CSB])

            for i in range(nlayers):
                kind, j = i % 3, i // 3
                for t in range(NT):
                    norm_tile_h(t, gmix, i)
                if "nomix" in dbg:
                    pass
                elif kind == 0:
                    attention(i, j)
                elif kind == 1:
                    gmlp(i)
                else:
                    shortconv(i)
                if "noffn" not in dbg:
                    ffn(i)

            wbank_cur[0] = wbank
            for t in range(NT):
                bk = norm_stage1(t)
                norm_stage2(t, bk, lambda c: gfin[:, c:c + 1], lambda c, t=t: xs[:, c, tsl(t)], lambda c, t=t: XB[c][t])
            for b in range(NB):
                gi = stgrot.next()
                for hf in range(2):
                    bk = wbank.next()
                    k.group(PE, [lambda e, c=c, bk=bk, b=b, hf=hf: e.transpose(psum[:, bk, c * 128:(c + 1) * 128], xs[:, 4 * hf + c, b * 128:(b + 1) * 128], ident[:])
                                 for c in range(4)], [XB[4 * hf + c][b // 4] for c in range(4)] + [CONST], [PB[bk]])
                    if hf == 0:
                        k.op(ACT, lambda e, bk=bk, gi=gi: e.activation(out=stg[:, gi, 0:512], in_=bank(bk), func=AF.Copy), [PB[bk]], [STG[gi]])
                    else:
                        k.op(DVE, lambda e, bk=bk, gi=gi: e.tensor_copy(out=stg[:, gi, 512:1024], in_=bank(bk)), [PB[bk]], [STG[gi]])
                k.dma(POOL, [(out_d[s, b * 128:(b + 1) * 128, :], stg[:, gi, :])], reads=[STG[gi]], writes=[OUTB], sembuf=OUTS[gi])

        for g_ in OUTS:
            if g_.dsem is not None:
                k._wait(POOL, (g_.dsem, g_.dcnt))
        k.emit()
    return nc


_CACHE = {}


def _consts():
    c = np.zeros((128, 132), np.float32)
    f = (1.0 / (np.float32(10000.0) ** (np.arange(0, 64, 2, dtype=np.float32) / np.float32(64)))).astype(np.float32)
    p = np.arange(128)
    c[:, 0] = f[p % 32]
    c[:, 1] = np.where((p // 32) % 2 == 0, -1.0, 1.0)
    perm = np.zeros((128, 128), np.float32)
    perm[p ^ 32, p] = 1.0
    c[:, 4:132] = perm
    return c


def kernel(**inputs):
    B = inputs["x"].shape[0]
    NS = B // N_CORES
    key = NS
    if key not in _CACHE:
        _CACHE[key] = build_program(NS)
    nc = _CACHE[key]
    cst = _consts()
    in_maps = []
    for r in range(N_CORES):
        m = {}
        for name, v in inputs.items():
            a = np.asarray(v)
            if name in ("x", "positions"):
                a = a[r * NS:(r + 1) * NS]
            m[name] = np.ascontiguousarray(a)
        m["cst"] = cst
        in_maps.append(m)
    res = run_bass_kernel_spmd(nc, in_maps, core_ids=list(range(N_CORES)))
    return np.concatenate([np.asarray(r_["out"]) for r_ in res.results], axis=0).astype(np.float32)
```

```python
import math
import contextlib
import numpy as np
import concourse.bass as bass
import concourse.mybir as mybir
from concourse.bass_utils import run_bass_kernel_spmd

F32 = mybir.dt.float32
BF16 = mybir.dt.bfloat16
I32 = mybir.dt.int32
AF = mybir.ActivationFunctionType
ALU = mybir.AluOpType

D = 1024
S = 2048
NCH = 8
NT = 4
NB = 16
DFF = 2816
NF = 22
GW = 1536
NORM_EPS = 1e-6
SUBLN_EPS = 1e-5
LN_EPS = 1e-5
N_CORES = 8


class Buf:
    __slots__ = ("name", "last_w", "readers", "dsem", "dcnt")

    def __init__(self, name=""):
        self.name = name
        self.last_w = None
        self.readers = {}
        self.dsem = None
        self.dcnt = 0


class EngW:
    def __init__(self, name, sem, is_pe=False):
        self.name = name
        self.sem = sem
        self.cnt = 0
        self.waited = {}
        self.prog = []
        self.is_pe = is_pe


class K:
    def __init__(self, nc, stack):
        self.nc = nc
        self.stack = stack
        self.nsem = 0
        self.pe = EngW("pe", self.new_sem("pe"), True)
        self.act = EngW("act", self.new_sem("act"))
        self.dve = EngW("dve", self.new_sem("dve"))
        self.pool = EngW("pool", self.new_sem("pool"))
        self.sp = EngW("sp", self.new_sem("sp"))

    def new_sem(self, name):
        self.nsem += 1
        return self.stack.enter_context(self.nc.semaphore(f"s{self.nsem}_{name}"))

    def sb(self, name, shape, dt):
        return self.stack.enter_context(self.nc.sbuf_tensor(name, list(shape), dt))

    def ps(self, name, shape, dt=F32):
        return self.stack.enter_context(self.nc.psum_tensor(name, list(shape), dt))

    def _wait(self, E, tok):
        if tok is None:
            return
        sem, val = tok
        if sem is E.sem and E.is_pe:
            return
        if E.waited.get(sem, 0) >= val:
            return
        E.waited[sem] = val
        E.prog.append(lambda eng, sem=sem, val=val: eng.wait_ge(sem, val))

    def _deps(self, E, reads, writes):
        for b in reads:
            self._wait(E, b.last_w)
        for b in writes:
            self._wait(E, b.last_w)
            for s, v in b.readers.items():
                self._wait(E, (s, v))

    def _mark(self, tok, reads, writes):
        for b in reads:
            if b.readers.get(tok[0], 0) < tok[1]:
                b.readers[tok[0]] = tok[1]
        for b in writes:
            b.last_w = tok
            b.readers = {}

    def op(self, E, fn, reads=(), writes=()):
        self._deps(E, reads, writes)
        E.cnt += 1
        sem = E.sem
        E.prog.append(lambda eng, fn=fn, sem=sem: fn(eng).then_inc(sem, 1))
        tok = (sem, E.cnt)
        self._mark(tok, reads, writes)
        return tok

    def group(self, E, fns, reads=(), writes=()):
        self._deps(E, reads, writes)
        E.cnt += 1
        sem = E.sem
        for f in fns[:-1]:
            E.prog.append(lambda eng, f=f: f(eng))
        E.prog.append(lambda eng, f=fns[-1], sem=sem: f(eng).then_inc(sem, 1))
        tok = (sem, E.cnt)
        self._mark(tok, reads, writes)
        return tok

    def dma(self, Q, pairs, reads=(), writes=(), sembuf=None, **kw):
        self._deps(Q, reads, writes)
        sb = sembuf if sembuf is not None else (writes[0] if writes else reads[0])
        if sb.dsem is None:
            sb.dsem = self.new_sem("d_" + sb.name)
        dsem = sb.dsem
        for (o, i) in pairs:
            sb.dcnt += 16
            Q.prog.append(lambda eng, o=o, i=i, dsem=dsem, kw=kw: eng.dma_start(out=o, in_=i, **kw).then_inc(dsem, 16))
        tok = (dsem, sb.dcnt)
        self._mark(tok, reads, writes)
        return tok

    def emit(self):
        nc = self.nc
        with nc.Block() as block:
            @block.tensor
            def _(e):
                for f in self.pe.prog:
                    f(e)

            @block.scalar
            def _(e):
                for f in self.act.prog:
                    f(e)

            @block.vector
            def _(e):
                for f in self.dve.prog:
                    f(e)

            @block.gpsimd
            def _(e):
                for f in self.pool.prog:
                    f(e)

            @block.sync
            def _(e):
                for f in self.sp.prog:
                    f(e)


class Rot:
    def __init__(self, items):
        self.items = list(items)
        self.i = 0

    def next(self):
        it = self.items[self.i % len(self.items)]
        self.i += 1
        return it


def mm(out, lhsT, rhs, start, stop):
    return lambda e: e.matmul(out, lhsT=lhsT, rhs=rhs, start=start, stop=stop)


def build_program(NS=4, nlayers=4, dbg=()):
    nc = bass.Bass("TRN2", target_bir_lowering=False)

    def din(name, shape, dt=F32):
        return nc.dram_tensor(name, list(shape), dt, kind="ExternalInput").ap()

    x_d = din("x", [NS, S, D])
    pos_d = din("positions", [NS, S], I32)
    mixn_d = din("mix_norm", [4, D])
    ffnn_d = din("ffn_norm", [4, D])
    finn_d = din("final_norm", [D])
    a_win_d = din("attn_w_in", [2, D, 3 * D])
    a_lam_d = din("attn_lambda", [2, 4, 64])
    a_sub_d = din("attn_subln", [2, 128])
    a_wout_d = din("attn_w_out", [2, D, D])
    g_win_d = din("gmlp_w_in", [1, D, 2 * GW])
    g_bin_d = din("gmlp_b_in", [1, 2 * GW])
    g_lng_d = din("gmlp_ln_g", [1, GW])
    g_lnb_d = din("gmlp_ln_b", [1, GW])
    g_ws_d = din("gmlp_w_s", [1, 8, 128, 128])
    g_bs_d = din("gmlp_b_s", [1, 8, 128])
    g_wout_d = din("gmlp_w_out", [1, GW, D])
    c_win_d = din("conv_w_in", [1, D, 3 * D])
    c_w_d = din("conv_w", [1, 3, D])
    c_wout_d = din("conv_w_out", [1, D, D])
    f_wgu_d = din("ffn_w_gate_up", [4, D, 2 * DFF])
    f_wd_d = din("ffn_w_down", [4, DFF, D])
    cst_d = din("cst", [128, 132])
    out_d = nc.dram_tensor("out", [NS, S, D], F32, kind="ExternalOutput").ap()

    with contextlib.ExitStack() as st:
        k = K(nc, st)
        PE, ACT, DVE, POOL, SP = k.pe, k.act, k.dve, k.pool, k.sp

        xs = k.sb("xs", [128, NCH, S], F32)
        hT = k.sb("hT", [128, NCH, S], BF16)
        big = k.sb("big", [128, NCH * S], BF16)
        qkv = k.sb("qkv", [128, 3 * S], BF16)
        qk = qkv[:, 0:2 * S].rearrange("p (m t) -> p m t", m=2)
        vv = qkv[:, 2 * S:3 * S].rearrange("p (b e) -> p b e", b=NB)
        pT = k.sb("pT", [128, 3, 2, 512], BF16)
        ring = k.sb("ring", [128, 4, 2048], BF16)
        cs = k.sb("cs", [128, 2, S], F32)
        tm = k.sb("tm", [128, 4, 512], F32)
        ep = tm
        sq = k.sb("sq", [128, 3, 512], BF16)
        ident = k.sb("ident", [128, 128], F32)
        onesf = k.sb("onesf", [128, 128], F32)
        onesb = k.sb("onesb", [128, 128], BF16)
        tri2 = k.sb("tri2", [128, 2, 128], BF16)
        permb = k.sb("permb", [128, 128], BF16)
        cst = k.sb("cst_sb", [128, 4], F32)
        gmix = k.sb("gmix", [128, 4, NCH], F32)
        gffn = k.sb("gffn", [128, 4, NCH], F32)
        gfin = k.sb("gfin", [128, NCH], F32)
        convw = k.sb("convw", [128, 3, NCH], F32)
        bu = k.sb("bu", [128, 12], F32)
        lng = k.sb("lng", [128, 12], F32)
        lnb = k.sb("lnb", [128, 12], F32)
        subl = k.sb("subl", [128, 2], F32)
        nlam = k.sb("nlam", [128, 2], F32)
        small = k.sb("small", [128, 16], F32)
        epst = k.sb("epst", [128, 4], F32)
        wsTm = k.sb("wsTm", [128, 8, 128], BF16)
        Rt = k.sb("Rt", [128, 12, 128], F32)
        bvhl = k.sb("bvhl", [2, GW], BF16)
        psum = k.ps("psum", [128, 8, 512], F32)

        aT = big[:, 0:11 * 1024].rearrange("p (f t) -> p f t", f=11)
        oT = big[:, 0:NCH * S].rearrange("p (c t) -> p c t", c=NCH)
        bigf = big[:, :].bitcast(F32)
        stg = bigf[:, 0:2048].rearrange("p (s d) -> p s d", s=2)
        rtmp = bigf[:, 2048:2048 + 3 * S].rearrange("p (a t) -> p a t", a=3)
        rtmpi = rtmp.bitcast(I32)
        uT = big[:, 0:12 * 512].rearrange("p (c t) -> p c t", c=12)
        zv4 = bigf[:, 3072:3072 + 2 * GW].rearrange("p (b c) -> p b c", b=2)
        vhat = big[:, 12288:12288 + GW]
        svt = bigf[:, 6912:6912 + 1024].rearrange("p (a c t) -> p a c t", a=2, c=4)
        gst = k.sb("gst", [128, 32], F32)
        hcv = qkv[:, :].bitcast(F32)[:, 0:2 + S]

        XB = [[Buf(f"x{c}_{t}") for t in range(NT)] for c in range(NCH)]
        HB = [[Buf(f"h{c}_{t}") for t in range(NT)] for c in range(NCH)]
        PB = [Buf(f"pb{i}") for i in range(8)]
        RING = [Buf(f"ring{i}") for i in range(4)]
        ringrot = Rot(range(4))
        SQ = [Buf(f"sq{i}") for i in range(3)]
        sqrot = Rot(range(3))
        TM = [Buf(f"tm{i}") for i in range(4)]
        tmrot = Rot(range(4))
        EPB = TM
        PT = [Buf(f"pT{i}") for i in range(3)]
        ptrot = Rot(range(3))
        QB = [[Buf(f"q{m}_{t}") for t in range(NT)] for m in range(2)]
        VB = [Buf(f"v{t}") for t in range(NT)]
        OB = [[Buf(f"o{c}_{t}") for t in range(NT)] for c in range(NCH)]
        AB = [[Buf(f"a{f}_{t}") for t in range(2)] for f in range(11)]
        CSB = Buf("cs")
        STG = [Buf("stg0"), Buf("stg1")]
        stgrot = Rot(range(2))
        RTMP = Buf("rtmp")
        CONST = Buf("const")
        OUTB = Buf("outd")
        OUTS = [Buf("outs0"), Buf("outs1")]
        GM = {n: Buf(n) for n in ["uT", "zv0", "zv1", "zv2", "zv3", "vhat", "svt0", "svt1", "stats", "hc"]}
        svrot = Rot(range(2))
        wbank = Rot(range(8))
        wbank4 = Rot(range(4))

        def bank(i):
            return psum[:, i, :]

        def tsl(t):
            return slice(t * 512, (t + 1) * 512)

        scr = {}

        def prep(name, W, Kdim, N, J):
            n_ch = N // J
            C = Kdim // 128
            t_ = nc.dram_tensor("scr_" + name, [n_ch, 128, C, J], BF16, kind="Internal").ap()
            b = Buf("scr_" + name)
            pairs = []
            for n in range(n_ch):
                pairs.append((t_[n], W[:, n * J:(n + 1) * J].rearrange("(c p) j -> p c j", p=128)))
            k.dma(POOL, pairs, writes=[b])
            scr[name] = (t_, b)

        k.dma(SP, [(cst[:], cst_d[:, 0:4])], writes=[CONST])
        k.dma(POOL, [(permb[:], cst_d[:, 4:132])], writes=[CONST], sembuf=Buf("permld"))
        ncs = dict(allow_slow_non_contiguous=True)
        k.dma(ACT, [(gmix[:], mixn_d.rearrange("l (c p) -> p l c", p=128)),
                   (gffn[:], ffnn_d.rearrange("l (c p) -> p l c", p=128)),
                   (gfin[:], finn_d.rearrange("(c p) -> p c", p=128)),
                   (convw[:], c_w_d[0].rearrange("w (c p) -> p w c", p=128)),
                   (bu[:], g_bin_d[0, 0:GW].rearrange("(c p) -> p c", p=128)),
                   (lng[:], g_lng_d[0].rearrange("(c p) -> p c", p=128)),
                   (lnb[:], g_lnb_d[0].rearrange("(c p) -> p c", p=128)),
                   (subl[:], a_sub_d.rearrange("j p -> p j"))], writes=[CONST], sembuf=Buf("parld"), **ncs)
        k.op(POOL, lambda e: e.memset(epst[:, 0:1], NORM_EPS), [], [CONST])
        k.op(POOL, lambda e: e.memset(epst[:, 1:2], SUBLN_EPS), [], [CONST])
        k.op(POOL, lambda e: e.memset(epst[:, 2:3], LN_EPS), [], [CONST])
        k.op(POOL, lambda e: e.memset(onesf[:], 1.0), [], [CONST])
        k.op(POOL, lambda e: e.memset(onesb[:], 1.0), [], [CONST])
        k.op(POOL, lambda e: e.affine_select(out=ident[:], in_=onesf[:], pattern=[[1, 128]], compare_op=ALU.is_equal,
                                             fill=0.0, base=0, channel_multiplier=-1), [CONST], [CONST])
        for m in range(2):
            k.op(POOL, lambda e, m=m: e.affine_select(out=tri2[:, m, :], in_=onesb[:], pattern=[[1, 128]], compare_op=ALU.is_ge,
                                                      fill=0.0, base=0, channel_multiplier=-1), [CONST], [CONST])

        lamt = bigf[:, 0:512].rearrange("p (j f) -> p j f", j=2)
        k.dma(SP, [(lamt[:, j, :], a_lam_d[j].rearrange("a f -> (a f)").partition_broadcast(128)) for j in range(2)],
              writes=[RTMP])
        for j in range(2):
            li = 0.8 - 0.6 * math.exp(-0.3 * (3 * j))
            prod = bigf[:, 512:640]
            k.op(DVE, lambda e, j=j: e.tensor_tensor(out=prod.rearrange("p (a f) -> p a f", a=2), in0=lamt[:, j, :].rearrange("p (a b f) -> p a b f", a=2, b=2)[:, :, 0, :],
                                                     in1=lamt[:, j, :].rearrange("p (a b f) -> p a b f", a=2, b=2)[:, :, 1, :], op=ALU.mult), [RTMP], [RTMP])
            k.op(DVE, lambda e: e.reduce_sum(out=small[:, 0:2], in_=prod.rearrange("p (a f) -> p a f", a=2), axis=mybir.AxisListType.X), [RTMP], [RTMP])
            k.op(ACT, lambda e: e.activation(out=small[:, 2:4], in_=small[:, 0:2], func=AF.Exp), [RTMP], [RTMP])
            k.op(DVE, lambda e: e.tensor_tensor(out=small[:, 4:5], in0=small[:, 3:4], in1=small[:, 2:3], op=ALU.subtract), [RTMP], [RTMP])
            k.op(DVE, lambda e, j=j, li=li: e.tensor_scalar(out=nlam[:, j:j + 1], in0=small[:, 4:5], scalar1=-li, scalar2=None, op0=ALU.add), [RTMP], [CONST])
            k.op(DVE, lambda e, j=j, li=li: e.tensor_scalar(out=subl[:, j:j + 1], in0=subl[:, j:j + 1], scalar1=(1.0 - li), scalar2=None, op0=ALU.mult), [CONST], [CONST])

        def prep_layer(i):
            kind, j = i % 3, i // 3
            if kind == 0:
                prep(f"win{i}", a_win_d[j], D, 3 * D, 128)
                prep(f"wout{i}", a_wout_d[j], D, D, 128)
            elif kind == 1:
                prep(f"wu{i}", g_win_d[0][:, 0:GW], D, GW, 128)
                prep(f"wv{i}", g_win_d[0][:, GW:2 * GW], D, GW, 256)
                prep(f"wout{i}", g_wout_d[0], GW, D, 128)
            else:
                prep(f"win{i}", c_win_d[0], D, 3 * D, 128)
                prep(f"wout{i}", c_wout_d[0], D, D, 128)
            prep(f"wgu{i}", f_wgu_d[i], D, 2 * DFF, 128)
            prep(f"wd{i}", f_wd_d[i], DFF, D, 128)

        for i_ in range(nlayers):
            prep_layer(i_)

        if nlayers > 1:
            wsf = bigf[:, 1024:2048].rearrange("p (g s) -> p g s", g=8)
            wsTf = bigf[:, 2048:3072].rearrange("p (g t) -> p g t", g=8)
            bsr = bigf[:, 3072:4096].rearrange("p (g t) -> p g t", g=8)
            bvf = bigf[0:2, 4096:4096 + GW]
            bvt = bigf[0:2, 4096 + GW:4096 + 2 * GW]
            GS = Buf("gsetup")
            k.dma(SP, [(wsf, g_ws_d[0].rearrange("g t s -> t g s")),
                       (bsr.rearrange("p g t -> p (g t)"), g_bs_d[0].rearrange("g t -> (g t)").partition_broadcast(128)),
                       (bvf[0:1, :], g_bin_d[0:1, GW:2 * GW])], writes=[GS])
            for g in range(8):
                bk = wbank.next()
                k.group(PE, [lambda e, g=g, bk=bk: e.transpose(psum[:, bk, 0:128], wsf[:, g, :], ident[:])], [GS, CONST], [PB[bk]])
                k.op(DVE, lambda e, g=g, bk=bk: e.tensor_copy(out=wsTf[:, g, :], in_=psum[:, bk, 0:128]), [PB[bk]], [GS])
            k.op(POOL, lambda e: e.affine_select(out=wsTf, in_=wsTf, pattern=[[0, 8], [1, 128]], compare_op=ALU.is_ge,
                                                 fill=0.0, base=0, channel_multiplier=-1), [GS], [GS])
            k.op(DVE, lambda e: e.tensor_copy(out=wsTm[:], in_=wsTf), [GS], [CONST])
            rws = bigf[:, 7168:8192].rearrange("p (g t) -> p g t", g=8)
            for hlf in range(2):
                bk = wbank.next()
                k.group(PE, [lambda e, hlf=hlf, bk=bk: e.matmul(psum[:, bk, :], lhsT=onesf[:], rhs=wsTf[:, 4 * hlf:4 * hlf + 4, :], start=True, stop=True)],
                        [GS, CONST], [PB[bk]])
                k.op(DVE, lambda e, hlf=hlf, bk=bk: e.tensor_copy(out=rws[:, 4 * hlf:4 * hlf + 4, :], in_=psum[:, bk, :].rearrange("p (g t) -> p g t", g=4)), [PB[bk]], [GS])
            for jc in range(12):
                kk, r3 = jc // 3, jc % 3
                if r3 == 0:
                    parts = [(0, 128, 2 * kk)]
                elif r3 == 2:
                    parts = [(0, 128, 2 * kk + 1)]
                else:
                    parts = [(0, 64, 2 * kk), (64, 128, 2 * kk + 1)]
                for (p0, p1, g) in parts:
                    k.op(DVE, lambda e, jc=jc, p0=p0, p1=p1, g=g: e.scalar_tensor_tensor(
                        out=Rt[p0:p1, jc, :], in0=rws[p0:p1, g, :], scalar=lnb[p0:p1, jc:jc + 1], in1=bsr[p0:p1, g, :],
                        op0=ALU.mult, op1=ALU.add), [GS, CONST], [CONST])
            k.op(DVE, lambda e: e.tensor_copy(out=bvhl[0:1, :], in_=bvf[0:1, :]), [GS], [CONST])
            k.op(DVE, lambda e: e.tensor_copy(out=bvt[0:1, :], in_=bvhl[0:1, :]), [CONST], [GS])
            k.op(DVE, lambda e: e.tensor_tensor(out=bvt[0:1, :], in0=bvf[0:1, :], in1=bvt[0:1, :], op=ALU.subtract), [GS], [GS])
            bvl_tmp = sq[0:1, :, :].rearrange("p a t -> p (a t)")
            k.op(DVE, lambda e: e.tensor_copy(out=bvl_tmp, in_=bvt[0:1, :]), [GS], [GS] + SQ)
            k.dma(SP, [(bvhl[1:2, :], bvl_tmp)], reads=[GS] + SQ, writes=[CONST], sembuf=Buf("bvmove"))
            gs_done = [GS]
        else:
            gs_done = []

        pending = []

        def defer(n, fn):
            pending.append([n, fn])

        def tick():
            for it in list(pending):
                it[0] -= 1
                if it[0] <= 0:
                    pending.remove(it)
                    it[1]()

        def flush():
            while pending:
                tick()

        def load_w(pieces):
            si = ringrot.next()
            pairs = []
            views = []
            off = 0
            bufs = []
            for (name, n) in pieces:
                t_, b = scr[name]
                C, J = t_.shape[2], t_.shape[3]
                v = ring[:, si, off:off + C * J].rearrange("p (c j) -> p c j", c=C)
                pairs.append((v, t_[n]))
                views.append(v)
                off += C * J
                if b not in bufs:
                    bufs.append(b)
            assert off <= 2048
            k.dma(SP, pairs, reads=bufs, writes=[RING[si]])
            return si, views

        def norm_stage1(t):
            bk = wbank_cur[0].next()
            for c in range(NCH):
                qi = sqrot.next()
                k.op(ACT, lambda e, c=c, qi=qi: e.activation(out=sq[:, qi, :], in_=xs[:, c, tsl(t)], func=AF.Square), [XB[c][t]], [SQ[qi]])
                k.group(PE, [mm(bank(bk), onesb[:], sq[:, qi, :], c == 0, c == NCH - 1)], [SQ[qi], CONST], [PB[bk]])
            return bk

        def norm_stage2(t, bk, gcol, dst, dstB, nfeat=1024.0, eps=NORM_EPS):
            ti = tmrot.next()
            rs = tm[:, ti, :]
            k.op(ACT, lambda e: e.activation(out=rs, in_=bank(bk), func=AF.Ln, scale=1.0 / nfeat, bias=epst[:, 0:1]), [PB[bk], CONST], [TM[ti]])
            k.op(ACT, lambda e: e.activation(out=rs, in_=rs, func=AF.Exp, scale=-0.5), [TM[ti]], [TM[ti]])
            for c in range(NCH):
                k.op(DVE, lambda e, c=c: e.scalar_tensor_tensor(out=dst(c), in0=xs[:, c, tsl(t)], scalar=gcol(c), in1=rs, op0=ALU.mult, op1=ALU.mult),
                     [XB[c][t], TM[ti], CONST], [dstB(c)])

        def norm_tile_h(t, gtile, li):
            bk = norm_stage1(t)
            norm_stage2(t, bk, lambda c: gtile[:, li, c:c + 1], lambda c: hT[:, c, tsl(t)], lambda c: HB[c][t])

        wbank_cur = [wbank]

        def ffn(i):
            wbank_cur[0] = wbank
            for half in range(2):
                tiles = [2 * half, 2 * half + 1]
                if half == 0:
                    for t in tiles:
                        norm_tile_h(t, gffn, i)
                for fp in range(2):
                    for fl in range(11):
                        f = 11 * fp + fl
                        si, (wg, wu_) = load_w([(f"wgu{i}", f), (f"wgu{i}", NF + f)])
                        for tt, t in enumerate(tiles):
                            bg, bu_ = wbank.next(), wbank.next()
                            k.group(PE, [mm(bank(bg), wg[:, c, :], hT[:, c, tsl(t)], c == 0, c == NCH - 1) for c in range(NCH)],
                                    [RING[si]] + [HB[c][t] for c in range(NCH)], [PB[bg]])
                            k.group(PE, [mm(bank(bu_), wu_[:, c, :], hT[:, c, tsl(t)], c == 0, c == NCH - 1) for c in range(NCH)],
                                    [RING[si]] + [HB[c][t] for c in range(NCH)], [PB[bu_]])
                            ti = tmrot.next()
                            k.op(ACT, lambda e, ti=ti, bg=bg: e.activation(out=tm[:, ti, :], in_=bank(bg), func=AF.Silu), [PB[bg]], [TM[ti]])
                            k.op(DVE, lambda e, ti=ti, bu_=bu_, fl=fl, tt=tt: e.tensor_tensor(out=aT[:, fl, tt * 512:(tt + 1) * 512], in0=tm[:, ti, :], in1=bank(bu_), op=ALU.mult),
                                 [TM[ti], PB[bu_]], [AB[fl][tt]])
                        if half == 0 and fp == 0 and fl == 8:
                            for t in (2, 3):
                                norm_tile_h(t, gffn, i)
                    t_, b_ = scr[f"wd{i}"]
                    for dc in range(NCH):
                        si = ringrot.next()
                        v = ring[:, si, 0:11 * 128].rearrange("p (c j) -> p c j", c=11)
                        k.dma(SP, [(v, t_[dc][:, 11 * fp:11 * fp + 11, :])], reads=[b_], writes=[RING[si]])
                        for tt, t in enumerate(tiles):
                            bo = wbank.next()
                            k.group(PE, [mm(bank(bo), v[:, fl, :], aT[:, fl, tt * 512:(tt + 1) * 512], fl == 0, fl == 10) for fl in range(11)],
                                    [RING[si]] + [AB[fl][tt] for fl in range(11)], [PB[bo]])
                            k.op(DVE, lambda e, dc=dc, t=t, bo=bo: e.tensor_tensor(out=xs[:, dc, tsl(t)], in0=xs[:, dc, tsl(t)], in1=bank(bo), op=ALU.add),
                                 [PB[bo], XB[dc][t]], [XB[dc][t]])

        def out_proj(i, n_in):
            for dc in range(NCH):
                si, (wo,) = load_w([(f"wout{i}", dc)])
                for t in range(NT):
                    bo = wbank.next()
                    k.group(PE, [mm(bank(bo), wo[:, c, :], oT[:, c, tsl(t)], c == 0, c == n_in - 1) for c in range(n_in)],
                            [RING[si]] + [OB[c][t] for c in range(n_in)], [PB[bo]])
                    k.op(DVE, lambda e, dc=dc, t=t, bo=bo: e.tensor_tensor(out=xs[:, dc, tsl(t)], in0=xs[:, dc, tsl(t)], in1=bank(bo), op=ALU.add),
                         [PB[bo], XB[dc][t]], [XB[dc][t]])

        def attention(i, j):
            wbank_cur[0] = wbank4
            wb = wbank4
            def head(h):
                si1, (wq, wk) = load_w([(f"win{i}", h), (f"win{i}", 8 + h)])
                si2, (wv_,) = load_w([(f"win{i}", 16 + h)])
                for t in (range(NT) if "noqk" not in dbg else ()):
                    for m, (w_, scl) in enumerate(((wq, 0.125), (wk, 1.0))):
                        ba = wb.next()
                        k.group(PE, [mm(bank(ba), w_[:, c, :], hT[:, c, tsl(t)], c == 0, c == NCH - 1) for c in range(NCH)],
                                [RING[si1]] + [HB[c][t] for c in range(NCH)], [PB[ba]])
                        t1 = tmrot.next()
                        k.op(ACT, lambda e, t1=t1, ba=ba, scl=scl: e.activation(out=tm[:, t1, :], in_=bank(ba), func=AF.Copy, scale=scl), [PB[ba]], [TM[t1]])
                        qi = sqrot.next()
                        k.op(DVE, lambda e, qi=qi, t1=t1: e.tensor_copy(out=sq[:, qi, :], in_=tm[:, t1, :]), [TM[t1]], [SQ[qi]])
                        k.op(DVE, lambda e, t1=t1, t=t: e.tensor_tensor(out=tm[:, t1, :], in0=tm[:, t1, :], in1=cs[:, 0, tsl(t)], op=ALU.mult), [TM[t1], CSB], [TM[t1]])

                        def rope2(qi=qi, t1=t1, t=t, m=m):
                            bb = wb.next()
                            k.group(PE, [mm(bank(bb), permb[:], sq[:, qi, :], True, True)], [SQ[qi], CONST], [PB[bb]])
                            t2 = tmrot.next()
                            k.op(DVE, lambda e: e.tensor_tensor(out=tm[:, t2, :], in0=cs[:, 1, tsl(t)], in1=bank(bb), op=ALU.mult), [PB[bb], CSB], [TM[t2]])
                            k.op(DVE, lambda e: e.tensor_tensor(out=qk[:, m, tsl(t)], in0=tm[:, t1, :], in1=tm[:, t2, :], op=ALU.add),
                                 [TM[t1], TM[t2]], [QB[m][t]])
                        defer(2, rope2)
                        tick()
                for tg in (range(NT) if "nov" not in dbg else ()):
                    bv = wb.next()
                    fns = []
                    for bl in range(4):
                        tok0 = tg * 512 + bl * 128
                        for c in range(NCH):
                            fns.append(mm(psum[:, bv, bl * 128:(bl + 1) * 128], hT[:, c, tok0:tok0 + 128], wv_[:, c, :], c == 0, c == NCH - 1))
                    k.group(PE, fns, [RING[si2]] + [HB[c][tg] for c in range(NCH)], [PB[bv]])
                    k.op(ACT, lambda e, tg=tg, bv=bv: e.activation(out=vv[:, 4 * tg:4 * tg + 4, :], in_=bank(bv).rearrange("p (b e) -> p b e", b=4), func=AF.Copy),
                         [PB[bv]], [VB[tg]])
                    tick()
                flush()
                units = []
                for jq in range(NT):
                    for c in range(4 * jq + 4):
                        units.append((jq, c))
                state = {}

                def s_stage(u):
                    jq, c = units[u]
                    r = c - 4 * jq
                    q0 = 128 * r if r > 0 else 0
                    ba, bb = wb.next(), wb.next()
                    fns = []
                    for m, b_ in ((0, ba), (1, bb)):
                        fns.append(mm(psum[:, b_, q0:512], qk[64 * m:64 * m + 64, 1, c * 128:(c + 1) * 128],
                                      qk[64 * m:64 * m + 64, 0, jq * 512 + q0:(jq + 1) * 512], True, True))
                    k.group(PE, fns, [QB[0][jq], QB[1][c // 4]], [PB[ba], PB[bb]])
                    pi = ptrot.next()
                    k.op(ACT, lambda e: e.activation(out=pT[:, pi, 0, q0:512], in_=psum[:, ba, q0:512], func=AF.Exp), [PB[ba]], [PT[pi]])
                    k.op(ACT, lambda e: e.activation(out=pT[:, pi, 1, q0:512], in_=psum[:, bb, q0:512], func=AF.Exp), [PB[bb]], [PT[pi]])
                    if r >= 0:
                        k.op(DVE, lambda e: e.tensor_tensor(out=pT[:, pi, :, q0:q0 + 128], in0=pT[:, pi, :, q0:q0 + 128], in1=tri2[:], op=ALU.mult),
                             [PT[pi], CONST], [PT[pi]])
                    state[u] = (pi, q0)

                def pv_stage(u):
                    jq, c = units[u]
                    pi, q0 = state.pop(u)
                    last = (c == 4 * jq + 3)
                    fns = []
                    for m in range(2):
                        fns.append(mm(psum[:, 4 + m, q0:512], vv[:, c, :], pT[:, pi, m, q0:512], c == 0, last))
                    for m in range(2):
                        fns.append(mm(psum[:, 6 + m, q0:512], onesb[:], pT[:, pi, m, q0:512], c == 0, last))
                    k.group(PE, fns, [PT[pi], VB[c // 4], CONST], [PB[4], PB[5], PB[6], PB[7]])
                    if last and "noepi" not in dbg:
                        epilogue(jq)

                def epilogue(jq):
                    o01 = ep[:, 0:2, :]
                    r01 = ep[:, 2:4, :]
                    k.op(ACT, lambda e: e.activation(out=ep[:, 2, :], in_=psum[:, 6, :], func=AF.Ln), [PB[6]], [EPB[2]])
                    k.op(DVE, lambda e: e.tensor_copy(out=ep[:, 0, :], in_=psum[:, 4, :]), [PB[4]], [EPB[0]])
                    k.op(ACT, lambda e: e.activation(out=ep[:, 3, :], in_=psum[:, 7, :], func=AF.Ln), [PB[7]], [EPB[3]])
                    k.op(DVE, lambda e: e.tensor_copy(out=ep[:, 1, :], in_=psum[:, 5, :]), [PB[5]], [EPB[1]])
                    k.op(ACT, lambda e: e.activation(out=r01, in_=r01, func=AF.Exp, scale=-1.0), [EPB[2], EPB[3]], [EPB[2], EPB[3]])
                    k.op(DVE, lambda e: e.tensor_tensor(out=o01, in0=o01, in1=r01, op=ALU.mult), [EPB[0], EPB[1], EPB[2], EPB[3]], [EPB[0], EPB[1]])
                    k.op(DVE, lambda e: e.scalar_tensor_tensor(out=ep[:, 2, :], in0=ep[:, 1, :], scalar=nlam[:, j:j + 1], in1=ep[:, 0, :], op0=ALU.mult, op1=ALU.add),
                         [EPB[0], EPB[1], CONST], [EPB[2]])
                    qi = sqrot.next()
                    k.op(DVE, lambda e: e.tensor_tensor(out=sq[:, qi, :], in0=ep[:, 2, :], in1=ep[:, 2, :], op=ALU.mult), [EPB[2]], [SQ[qi]])

                    def ep2():
                        bs_ = wb.next()
                        wb.next()
                        k.group(PE, [mm(bank(bs_), onesb[:], sq[:, qi, :], True, True)], [SQ[qi], CONST], [PB[bs_]])
                        k.op(ACT, lambda e: e.activation(out=ep[:, 3, :], in_=bank(bs_), func=AF.Ln, scale=1.0 / 128.0, bias=epst[:, 1:2]), [PB[bs_], CONST], [EPB[3]])
                        k.op(ACT, lambda e: e.activation(out=ep[:, 3, :], in_=ep[:, 3, :], func=AF.Exp, scale=-0.5), [EPB[3]], [EPB[3]])
                        k.op(DVE, lambda e: e.scalar_tensor_tensor(out=oT[:, h, tsl(jq)], in0=ep[:, 2, :], scalar=subl[:, j:j + 1], in1=ep[:, 3, :], op0=ALU.mult, op1=ALU.mult),
                             [EPB[2], EPB[3], CONST], [OB[h][jq]])
                    defer(2, ep2)

                nu = len(units) if "nocore" not in dbg else 0
                if nu:
                    s_stage(0)
                for u in range(nu):
                    if u + 1 < nu:
                        s_stage(u + 1)
                    pv_stage(u)
                    tick()
                flush()
            for h in range(NCH):
                head(h)
            wbank_cur[0] = wbank
            if "noout" not in dbg:
                out_proj(i, NCH)

        def shortconv(i):
            wbank_cur[0] = wbank
            k.op(DVE, lambda e: e.memset(hcv[:, 0:2], 0.0), [], [GM["hc"]])

            def chunk(jc):
                si1, (wgb, wgc) = load_w([(f"win{i}", jc), (f"win{i}", 8 + jc)])
                si2, (wxs,) = load_w([(f"win{i}", 16 + jc)])
                for t in range(NT):
                    t0 = t * 512
                    bks = []
                    for (w_, si) in ((wgb, si1), (wgc, si1), (wxs, si2)):
                        bk = wbank.next()
                        k.group(PE, [mm(bank(bk), w_[:, c, :], hT[:, c, tsl(t)], c == 0, c == NCH - 1) for c in range(NCH)],
                                [RING[si]] + [HB[c][t] for c in range(NCH)], [PB[bk]])
                        bks.append(bk)
                    ba, bb, bc = bks
                    t1 = tmrot.next()
                    k.op(ACT, lambda e, t1=t1, bc=bc: e.activation(out=tm[:, t1, :], in_=bank(bc), func=AF.Copy), [PB[bc]], [TM[t1]])
                    k.op(DVE, lambda e, t1=t1, bb=bb, t0=t0: e.tensor_tensor(out=hcv[:, 2 + t0:2 + t0 + 512], in0=tm[:, t1, :], in1=bank(bb), op=ALU.mult),
                         [TM[t1], PB[bb]], [GM["hc"]])
                    t2 = tmrot.next()
                    k.op(DVE, lambda e, t2=t2, t0=t0: e.tensor_scalar(out=tm[:, t2, :], in0=hcv[:, 2 + t0:2 + t0 + 512], scalar1=convw[:, 2, jc:jc + 1], scalar2=None, op0=ALU.mult),
                         [GM["hc"], CONST], [TM[t2]])
                    k.op(DVE, lambda e, t2=t2, t0=t0: e.scalar_tensor_tensor(out=tm[:, t2, :], in0=hcv[:, 1 + t0:1 + t0 + 512], scalar=convw[:, 1, jc:jc + 1], in1=tm[:, t2, :], op0=ALU.mult, op1=ALU.add),
                         [GM["hc"], CONST, TM[t2]], [TM[t2]])
                    k.op(DVE, lambda e, t2=t2, t0=t0: e.scalar_tensor_tensor(out=tm[:, t2, :], in0=hcv[:, t0:t0 + 512], scalar=convw[:, 0, jc:jc + 1], in1=tm[:, t2, :], op0=ALU.mult, op1=ALU.add),
                         [GM["hc"], CONST, TM[t2]], [TM[t2]])
                    k.op(DVE, lambda e, t2=t2, ba=ba, t=t: e.tensor_tensor(out=oT[:, jc, tsl(t)], in0=tm[:, t2, :], in1=bank(ba), op=ALU.mult),
                         [TM[t2], PB[ba]], [OB[jc][t]])
            for jc in range(NCH):
                chunk(jc)
            out_proj(i, NCH)

        def gmlp(i):
            wbank_cur[0] = wbank
            ZV = [GM["zv0"], GM["zv1"]]

            def tile(t):
                for n in range(12):
                    si, (w_,) = load_w([(f"wu{i}", n)])
                    bk = wbank.next()
                    k.group(PE, [mm(bank(bk), w_[:, c, :], hT[:, c, tsl(t)], c == 0, c == NCH - 1) for c in range(NCH)],
                            [RING[si]] + [HB[c][t] for c in range(NCH)], [PB[bk]])
                    k.op(ACT, lambda e, n=n, bk=bk: e.activation(out=uT[:, n, :], in_=bank(bk), func=AF.Gelu, bias=bu[:, n:n + 1]), [PB[bk], CONST], [GM["uT"]])
                for bp in range(2):
                    for w2 in range(3):
                        sl = [load_w([(f"wv{i}", 2 * w2 + hh)]) for hh in range(2)]
                        for b2 in range(2):
                            bl = 2 * bp + b2
                            tok0 = t * 512 + bl * 128
                            bk = wbank.next()
                            fns = []
                            for hh in range(2):
                                wvt = sl[hh][1][0]
                                col0 = (2 * w2 + hh) * 256
                                for c in range(NCH):
                                    fns.append(mm(psum[:, bk, hh * 256:(hh + 1) * 256], hT[:, c, tok0:tok0 + 128], wvt[:, c, :], c == 0, False))
                                fns.append(mm(psum[:, bk, hh * 256:(hh + 1) * 256], onesb[0:2, :], bvhl[0:2, col0:col0 + 256], False, True))
                            k.group(PE, fns, [RING[sl[0][0]], RING[sl[1][0]], CONST] + [HB[c][t] for c in range(NCH)], [PB[bk]])
                            k.op(ACT, lambda e, b2=b2, w2=w2, bk=bk: e.activation(out=zv4[:, b2, w2 * 512:(w2 + 1) * 512], in_=bank(bk), func=AF.Gelu), [PB[bk]], [ZV[b2]])
                    for b2 in range(2):
                        bl = 2 * bp + b2
                        for w2 in range(3):
                            k.op(DVE, lambda e, b2=b2, w2=w2: e.bn_stats(out=gst[:, 6 * w2:6 * w2 + 6], in_=zv4[:, b2, w2 * 512:(w2 + 1) * 512]), [ZV[b2]], [GM["stats"]])
                        k.op(DVE, lambda e: e.bn_aggr(out=gst[:, 18:20], in_=gst[:, 0:18]), [GM["stats"]], [GM["stats"]])
                        k.op(ACT, lambda e: e.activation(out=gst[:, 20:21], in_=gst[:, 19:20], func=AF.Ln, bias=epst[:, 2:3]), [GM["stats"], CONST], [GM["stats"]])
                        k.op(ACT, lambda e: e.activation(out=gst[:, 20:21], in_=gst[:, 20:21], func=AF.Exp, scale=-0.5), [GM["stats"]], [GM["stats"]])
                        k.op(DVE, lambda e, b2=b2: e.tensor_scalar(out=vhat, in0=zv4[:, b2, :], scalar1=gst[:, 18:19], scalar2=gst[:, 20:21], op0=ALU.subtract, op1=ALU.mult),
                             [ZV[b2], GM["stats"]], [GM["vhat"]])
                        for grp in range(3):
                            bk = wbank.next()
                            fns = []
                            for jl in range(4):
                                jc = 4 * grp + jl
                                kk, r3 = jc // 3, jc % 3
                                if r3 == 0:
                                    parts = [(0, 128, 2 * kk)]
                                elif r3 == 2:
                                    parts = [(0, 128, 2 * kk + 1)]
                                else:
                                    parts = [(0, 64, 2 * kk), (64, 128, 2 * kk + 1)]
                                for (p0, p1, g) in parts:
                                    fns.append(mm(psum[p0:p1, bk, jl * 128:(jl + 1) * 128], vhat[:, jc * 128 + p0:jc * 128 + p1], wsTm[:, g, :], True, True))
                            k.group(PE, fns, [GM["vhat"], CONST], [PB[bk]])
                            sv = svrot.next()
                            SVB = GM["svt0"] if sv == 0 else GM["svt1"]
                            for jl in range(4):
                                jc = 4 * grp + jl
                                k.op(DVE, lambda e, jl=jl, jc=jc, bk=bk, sv=sv: e.scalar_tensor_tensor(out=svt[:, sv, jl, :], in0=psum[:, bk, jl * 128:(jl + 1) * 128], scalar=lng[:, jc:jc + 1], in1=Rt[:, jc, :],
                                                                                                      op0=ALU.mult, op1=ALU.add), [PB[bk], CONST], [SVB])
                            k.op(DVE, lambda e, grp=grp, bl=bl, sv=sv: e.tensor_tensor(out=uT[:, 4 * grp:4 * grp + 4, bl * 128:(bl + 1) * 128], in0=uT[:, 4 * grp:4 * grp + 4, bl * 128:(bl + 1) * 128],
                                                                                       in1=svt[:, sv, :, :], op=ALU.mult), [SVB, GM["uT"]], [GM["uT"]])
                for dc in range(NCH):
                    si, (wo,) = load_w([(f"wout{i}", dc)])
                    bo = wbank.next()
                    k.group(PE, [mm(bank(bo), wo[:, c, :], uT[:, c, :], c == 0, c == 11) for c in range(12)], [RING[si], GM["uT"]], [PB[bo]])
                    k.op(DVE, lambda e, dc=dc, bo=bo: e.tensor_tensor(out=xs[:, dc, tsl(t)], in0=xs[:, dc, tsl(t)], in1=bank(bo), op=ALU.add),
                         [PB[bo], XB[dc][t]], [XB[dc][t]])
            for t in range(NT):
                tile(t)

        for s in range(NS):
            for b in range(NB):
                gi = stgrot.next()
                k.dma(SP, [(stg[:, gi, :], x_d[s, b * 128:(b + 1) * 128, :])], writes=[STG[gi]] + ([RTMP] + gs_done if (s == 0 and b < 2) else []), sembuf=STG[gi])
                for hf in range(2):
                    bk = wbank.next()
                    k.group(PE, [lambda e, c=c, bk=bk, gi=gi, hf=hf: e.transpose(psum[:, bk, c * 128:(c + 1) * 128], stg[:, gi, (4 * hf + c) * 128:(4 * hf + c + 1) * 128], ident[:])
                                 for c in range(4)], [STG[gi], CONST], [PB[bk]])
                    eng = ACT if hf == 0 else DVE
                    wr = [XB[4 * hf + c][b // 4] for c in range(4)]
                    if hf == 0:
                        k.op(ACT, lambda e, bk=bk, b=b: e.activation(out=xs[:, 0:4, b * 128:(b + 1) * 128], in_=bank(bk).rearrange("p (c t) -> p c t", c=4), func=AF.Copy), [PB[bk]], wr)
                    else:
                        k.op(DVE, lambda e, bk=bk, b=b: e.tensor_copy(out=xs[:, 4:8, b * 128:(b + 1) * 128], in_=bank(bk).rearrange("p (c t) -> p c t", c=4)), [PB[bk]], wr)
            HI = 6.28125
            LO = 2.0 * math.pi - 6.28125
            angf = rtmp[:, 0, :]
            posi = rtmpi[:, 0, :]
            cosr = rtmp[:, 1, :]
            kf = rtmp[:, 2, :]
            ki = rtmpi[:, 2, :]
            k.dma(SP, [(posi, pos_d[s:s + 1, :].partition_broadcast(128))], reads=[STG[0], STG[1]], writes=[RTMP])
            k.op(DVE, lambda e: e.tensor_copy(out=angf, in_=posi), [RTMP], [RTMP])
            k.op(DVE, lambda e: e.tensor_scalar(out=angf, in0=angf, scalar1=cst[:, 0:1], scalar2=None, op0=ALU.mult), [RTMP, CONST], [RTMP])
            k.op(DVE, lambda e: e.tensor_scalar(out=cosr, in0=angf, scalar1=math.pi / 2.0, scalar2=None, op0=ALU.add), [RTMP], [RTMP])
            for which in range(2):
                src = cosr if which == 0 else angf
                k.op(DVE, lambda e, src=src: e.tensor_scalar(out=kf, in0=src, scalar1=1.0 / (2.0 * math.pi), scalar2=None, op0=ALU.mult), [RTMP], [RTMP])
                k.op(DVE, lambda e: e.tensor_copy(out=ki, in_=kf), [RTMP], [RTMP])
                k.op(DVE, lambda e: e.tensor_copy(out=kf, in_=ki), [RTMP], [RTMP])
                k.op(DVE, lambda e, src=src: e.scalar_tensor_tensor(out=src, in0=kf, scalar=-HI, in1=src, op0=ALU.mult, op1=ALU.add), [RTMP], [RTMP])
                k.op(DVE, lambda e, src=src: e.scalar_tensor_tensor(out=src, in0=kf, scalar=-LO, in1=src, op0=ALU.mult, op1=ALU.add), [RTMP], [RTMP])
                k.op(DVE, lambda e, src=src: e.tensor_scalar(out=src, in0=src, scalar1=math.pi, scalar2=-math.pi, op0=ALU.min, op1=ALU.max), [RTMP], [RTMP])
                if which == 0:
                    k.op(ACT, lambda e, src=src: e.activation(out=cs[:, 0, :], in_=src, func=AF.Sin), [RTMP], [CSB])
                else:
                    k.op(ACT, lambda e, src=src: e.activation(out=cs[:, 1, :], in_=src, func=AF.Sin, scale=cst[:, 1:2]), [RTMP, CONST], [CSB])

            for i in range(nlayers):
                kind, j = i % 3, i // 3
                for t in range(NT):
                    norm_tile_h(t, gmix, i)
                if "nomix" in dbg:
                    pass
                elif kind == 0:
                    attention(i, j)
                elif kind == 1:
                    gmlp(i)
                else:
                    shortconv(i)
                if "noffn" not in dbg:
                    ffn(i)

            wbank_cur[0] = wbank
            for t in range(NT):
                bk = norm_stage1(t)
                norm_stage2(t, bk, lambda c: gfin[:, c:c + 1], lambda c, t=t: xs[:, c, tsl(t)], lambda c, t=t: XB[c][t])
            for b in range(NB):
                gi = stgrot.next()
                for hf in range(2):
                    bk = wbank.next()
                    k.group(PE, [lambda e, c=c, bk=bk, b=b, hf=hf: e.transpose(psum[:, bk, c * 128:(c + 1) * 128], xs[:, 4 * hf + c, b * 128:(b + 1) * 128], ident[:])
                                 for c in range(4)], [XB[4 * hf + c][b // 4] for c in range(4)] + [CONST], [PB[bk]])
                    if hf == 0:
                        k.op(ACT, lambda e, bk=bk, gi=gi: e.activation(out=stg[:, gi, 0:512], in_=bank(bk), func=AF.Copy), [PB[bk]], [STG[gi]])
                    else:
                        k.op(DVE, lambda e, bk=bk, gi=gi: e.tensor_copy(out=stg[:, gi, 512:1024], in_=bank(bk)), [PB[bk]], [STG[gi]])
                k.dma(POOL, [(out_d[s, b * 128:(b + 1) * 128, :], stg[:, gi, :])], reads=[STG[gi]], writes=[OUTB], sembuf=OUTS[gi])

        for g_ in OUTS:
            if g_.dsem is not None:
                k._wait(POOL, (g_.dsem, g_.dcnt))
        k.emit()
    return nc


_CACHE = {}


def _consts():
    c = np.zeros((128, 132), np.float32)
    f = (1.0 / (np.float32(10000.0) ** (np.arange(0, 64, 2, dtype=np.float32) / np.float32(64)))).astype(np.float32)
    p = np.arange(128)
    c[:, 0] = f[p % 32]
    c[:, 1] = np.where((p // 32) % 2 == 0, -1.0, 1.0)
    perm = np.zeros((128, 128), np.float32)
    perm[p ^ 32, p] = 1.0
    c[:, 4:132] = perm
    return c


def kernel(**inputs):
    B = inputs["x"].shape[0]
    NS = B // N_CORES
    key = NS
    if key not in _CACHE:
        _CACHE[key] = build_program(NS)
    nc = _CACHE[key]
    cst = _consts()
    in_maps = []
    for r in range(N_CORES):
        m = {}
        for name, v in inputs.items():
            a = np.asarray(v)
            if name in ("x", "positions"):
                a = a[r * NS:(r + 1) * NS]
            m[name] = np.ascontiguousarray(a)
        m["cst"] = cst
        in_maps.append(m)
    res = run_bass_kernel_spmd(nc, in_maps, core_ids=list(range(N_CORES)))
    return np.concatenate([np.asarray(r_["out"]) for r_ in res.results], axis=0).astype(np.float32)
```

```python
import math
import contextlib
import numpy as np
import concourse.bass as bass
import concourse.mybir as mybir
from concourse.bass_utils import run_bass_kernel_spmd

F32 = mybir.dt.float32
BF16 = mybir.dt.bfloat16
I32 = mybir.dt.int32
AF = mybir.ActivationFunctionType
ALU = mybir.AluOpType

D = 1024
S = 2048
NCH = 8
NT = 4
NB = 16
DFF = 2816
NF = 22
GW = 1536
NORM_EPS = 1e-6
SUBLN_EPS = 1e-5
LN_EPS = 1e-5
N_CORES = 8


class Buf:
    __slots__ = ("name", "last_w", "readers", "dsem", "dcnt")

    def __init__(self, name=""):
        self.name = name
        self.last_w = None
        self.readers = {}
        self.dsem = None
        self.dcnt = 0


class EngW:
    def __init__(self, name, sem, is_pe=False):
        self.name = name
        self.sem = sem
        self.cnt = 0
        self.waited = {}
        self.prog = []
        self.is_pe = is_pe


class K:
    def __init__(self, nc, stack):
        self.nc = nc
        self.stack = stack
        self.nsem = 0
        self.pe = EngW("pe", self.new_sem("pe"), True)
        self.act = EngW("act", self.new_sem("act"))
        self.dve = EngW("dve", self.new_sem("dve"))
        self.pool = EngW("pool", self.new_sem("pool"))
        self.sp = EngW("sp", self.new_sem("sp"))

    def new_sem(self, name):
        self.nsem += 1
        return self.stack.enter_context(self.nc.semaphore(f"s{self.nsem}_{name}"))

    def sb(self, name, shape, dt):
        return self.stack.enter_context(self.nc.sbuf_tensor(name, list(shape), dt))

    def ps(self, name, shape, dt=F32):
        return self.stack.enter_context(self.nc.psum_tensor(name, list(shape), dt))

    def _wait(self, E, tok):
        if tok is None:
            return
        sem, val = tok
        if sem is E.sem and E.is_pe:
            return
        if E.waited.get(sem, 0) >= val:
            return
        E.waited[sem] = val
        E.prog.append(lambda eng, sem=sem, val=val: eng.wait_ge(sem, val))

    def _deps(self, E, reads, writes):
        for b in reads:
            self._wait(E, b.last_w)
        for b in writes:
            self._wait(E, b.last_w)
            for s, v in b.readers.items():
                self._wait(E, (s, v))

    def _mark(self, tok, reads, writes):
        for b in reads:
            if b.readers.get(tok[0], 0) < tok[1]:
                b.readers[tok[0]] = tok[1]
        for b in writes:
            b.last_w = tok
            b.readers = {}

    def op(self, E, fn, reads=(), writes=()):
        self._deps(E, reads, writes)
        E.cnt += 1
        sem = E.sem
        E.prog.append(lambda eng, fn=fn, sem=sem: fn(eng).then_inc(sem, 1))
        tok = (sem, E.cnt)
        self._mark(tok, reads, writes)
        return tok

    def group(self, E, fns, reads=(), writes=()):
        self._deps(E, reads, writes)
        E.cnt += 1
        sem = E.sem
        for f in fns[:-1]:
            E.prog.append(lambda eng, f=f: f(eng))
        E.prog.append(lambda eng, f=fns[-1], sem=sem: f(eng).then_inc(sem, 1))
        tok = (sem, E.cnt)
        self._mark(tok, reads, writes)
        return tok

    def dma(self, Q, pairs, reads=(), writes=(), sembuf=None, **kw):
        self._deps(Q, reads, writes)
        sb = sembuf if sembuf is not None else (writes[0] if writes else reads[0])
        if sb.dsem is None:
            sb.dsem = self.new_sem("d_" + sb.name)
        dsem = sb.dsem
        for (o, i) in pairs:
            sb.dcnt += 16
            Q.prog.append(lambda eng, o=o, i=i, dsem=dsem, kw=kw: eng.dma_start(out=o, in_=i, **kw).then_inc(dsem, 16))
        tok = (dsem, sb.dcnt)
        self._mark(tok, reads, writes)
        return tok

    def emit(self):
        nc = self.nc
        with nc.Block() as block:
            @block.tensor
            def _(e):
                for f in self.pe.prog:
                    f(e)

            @block.scalar
            def _(e):
                for f in self.act.prog:
                    f(e)

            @block.vector
            def _(e):
                for f in self.dve.prog:
                    f(e)

            @block.gpsimd
            def _(e):
                for f in self.pool.prog:
                    f(e)

            @block.sync
            def _(e):
                for f in self.sp.prog:
                    f(e)


class Rot:
    def __init__(self, items):
        self.items = list(items)
        self.i = 0

    def next(self):
        it = self.items[self.i % len(self.items)]
        self.i += 1
        return it


def mm(out, lhsT, rhs, start, stop):
    return lambda e: e.matmul(out, lhsT=lhsT, rhs=rhs, start=start, stop=stop)


def build_program(NS=4, nlayers=4, dbg=()):
    nc = bass.Bass("TRN2", target_bir_lowering=False)

    def din(name, shape, dt=F32):
        return nc.dram_tensor(name, list(shape), dt, kind="ExternalInput").ap()

    x_d = din("x", [NS, S, D])
    pos_d = din("positions", [NS, S], I32)
    mixn_d = din("mix_norm", [4, D])
    ffnn_d = din("ffn_norm", [4, D])
    finn_d = din("final_norm", [D])
    a_win_d = din("attn_w_in", [2, D, 3 * D])
    a_lam_d = din("attn_lambda", [2, 4, 64])
    a_sub_d = din("attn_subln", [2, 128])
    a_wout_d = din("attn_w_out", [2, D, D])
    g_win_d = din("gmlp_w_in", [1, D, 2 * GW])
    g_bin_d = din("gmlp_b_in", [1, 2 * GW])
    g_lng_d = din("gmlp_ln_g", [1, GW])
    g_lnb_d = din("gmlp_ln_b", [1, GW])
    g_ws_d = din("gmlp_w_s", [1, 8, 128, 128])
    g_bs_d = din("gmlp_b_s", [1, 8, 128])
    g_wout_d = din("gmlp_w_out", [1, GW, D])
    c_win_d = din("conv_w_in", [1, D, 3 * D])
    c_w_d = din("conv_w", [1, 3, D])
    c_wout_d = din("conv_w_out", [1, D, D])
    f_wgu_d = din("ffn_w_gate_up", [4, D, 2 * DFF])
    f_wd_d = din("ffn_w_down", [4, DFF, D])
    cst_d = din("cst", [128, 132])
    out_d = nc.dram_tensor("out", [NS, S, D], F32, kind="ExternalOutput").ap()

    with contextlib.ExitStack() as st:
        k = K(nc, st)
        PE, ACT, DVE, POOL, SP = k.pe, k.act, k.dve, k.pool, k.sp

        xs = k.sb("xs", [128, NCH, S], F32)
        hT = k.sb("hT", [128, NCH, S], BF16)
        big = k.sb("big", [128, NCH * S], BF16)
        qkv = k.sb("qkv", [128, 3 * S], BF16)
        qk = qkv[:, 0:2 * S].rearrange("p (m t) -> p m t", m=2)
        vv = qkv[:, 2 * S:3 * S].rearrange("p (b e) -> p b e", b=NB)
        pT = k.sb("pT", [128, 3, 2, 512], BF16)
        ring = k.sb("ring", [128, 4, 2048], BF16)
        cs = k.sb("cs", [128, 2, S], F32)
        tm = k.sb("tm", [128, 4, 512], F32)
        ep = tm
        sq = k.sb("sq", [128, 3, 512], BF16)
        ident = k.sb("ident", [128, 128], F32)
        onesf = k.sb("onesf", [128, 128], F32)
        onesb = k.sb("onesb", [128, 128], BF16)
        tri2 = k.sb("tri2", [128, 2, 128], BF16)
        permb = k.sb("permb", [128, 128], BF16)
        cst = k.sb("cst_sb", [128, 4], F32)
        gmix = k.sb("gmix", [128, 4, NCH], F32)
        gffn = k.sb("gffn", [128, 4, NCH], F32)
        gfin = k.sb("gfin", [128, NCH], F32)
        convw = k.sb("convw", [128, 3, NCH], F32)
        bu = k.sb("bu", [128, 12], F32)
        lng = k.sb("lng", [128, 12], F32)
        lnb = k.sb("lnb", [128, 12], F32)
        subl = k.sb("subl", [128, 2], F32)
        nlam = k.sb("nlam", [128, 2], F32)
        small = k.sb("small", [128, 16], F32)
        epst = k.sb("epst", [128, 4], F32)
        wsTm = k.sb("wsTm", [128, 8, 128], BF16)
        Rt = k.sb("Rt", [128, 12, 128], F32)
        bvhl = k.sb("bvhl", [2, GW], BF16)
        psum = k.ps("psum", [128, 8, 512], F32)

        aT = big[:, 0:11 * 1024].rearrange("p (f t) -> p f t", f=11)
        oT = big[:, 0:NCH * S].rearrange("p (c t) -> p c t", c=NCH)
        bigf = big[:, :].bitcast(F32)
        stg = bigf[:, 0:2048].rearrange("p (s d) -> p s d", s=2)
        rtmp = bigf[:, 2048:2048 + 3 * S].rearrange("p (a t) -> p a t", a=3)
        rtmpi = rtmp.bitcast(I32)
        uT = big[:, 0:12 * 512].rearrange("p (c t) -> p c t", c=12)
        zv4 = bigf[:, 3072:3072 + 2 * GW].rearrange("p (b c) -> p b c", b=2)
        vhat = big[:, 12288:12288 + GW]
        svt = bigf[:, 6912:6912 + 1024].rearrange("p (a c t) -> p a c t", a=2, c=4)
        gst = k.sb("gst", [128, 32], F32)
        hcv = qkv[:, :].bitcast(F32)[:, 0:2 + S]

        XB = [[Buf(f"x{c}_{t}") for t in range(NT)] for c in range(NCH)]
        HB = [[Buf(f"h{c}_{t}") for t in range(NT)] for c in range(NCH)]
        PB = [Buf(f"pb{i}") for i in range(8)]
        RING = [Buf(f"ring{i}") for i in range(4)]
        ringrot = Rot(range(4))
        SQ = [Buf(f"sq{i}") for i in range(3)]
        sqrot = Rot(range(3))
        TM = [Buf(f"tm{i}") for i in range(4)]
        tmrot = Rot(range(4))
        EPB = TM
        PT = [Buf(f"pT{i}") for i in range(3)]
        ptrot = Rot(range(3))
        QB = [[Buf(f"q{m}_{t}") for t in range(NT)] for m in range(2)]
        VB = [Buf(f"v{t}") for t in range(NT)]
        OB = [[Buf(f"o{c}_{t}") for t in range(NT)] for c in range(NCH)]
        AB = [[Buf(f"a{f}_{t}") for t in range(2)] for f in range(11)]
        CSB = Buf("cs")
        STG = [Buf("stg0"), Buf("stg1")]
        stgrot = Rot(range(2))
        RTMP = Buf("rtmp")
        CONST = Buf("const")
        OUTB = Buf("outd")
        OUTS = [Buf("outs0"), Buf("outs1")]
        GM = {n: Buf(n) for n in ["uT", "zv0", "zv1", "zv2", "zv3", "vhat", "svt0", "svt1", "stats", "hc"]}
        svrot = Rot(range(2))
        wbank = Rot(range(8))
        wbank4 = Rot(range(4))

        def bank(i):
            return psum[:, i, :]

        def tsl(t):
            return slice(t * 512, (t + 1) * 512)

        scr = {}

        def prep(name, W, Kdim, N, J):
            n_ch = N // J
            C = Kdim // 128
            t_ = nc.dram_tensor("scr_" + name, [n_ch, 128, C, J], BF16, kind="Internal").ap()
            b = Buf("scr_" + name)
            pairs = []
            for n in range(n_ch):
                pairs.append((t_[n], W[:, n * J:(n + 1) * J].rearrange("(c p) j -> p c j", p=128)))
            k.dma(POOL, pairs, writes=[b])
            scr[name] = (t_, b)

        k.dma(SP, [(cst[:], cst_d[:, 0:4])], writes=[CONST])
        k.dma(POOL, [(permb[:], cst_d[:, 4:132])], writes=[CONST], sembuf=Buf("permld"))
        ncs = dict(allow_slow_non_contiguous=True)
        k.dma(ACT, [(gmix[:], mixn_d.rearrange("l (c p) -> p l c", p=128)),
                   (gffn[:], ffnn_d.rearrange("l (c p) -> p l c", p=128)),
                   (gfin[:], finn_d.rearrange("(c p) -> p c", p=128)),
                   (convw[:], c_w_d[0].rearrange("w (c p) -> p w c", p=128)),
                   (bu[:], g_bin_d[0, 0:GW].rearrange("(c p) -> p c", p=128)),
                   (lng[:], g_lng_d[0].rearrange("(c p) -> p c", p=128)),
                   (lnb[:], g_lnb_d[0].rearrange("(c p) -> p c", p=128)),
                   (subl[:], a_sub_d.rearrange("j p -> p j"))], writes=[CONST], sembuf=Buf("parld"), **ncs)
        k.op(POOL, lambda e: e.memset(epst[:, 0:1], NORM_EPS), [], [CONST])
        k.op(POOL, lambda e: e.memset(epst[:, 1:2], SUBLN_EPS), [], [CONST])
        k.op(POOL, lambda e: e.memset(epst[:, 2:3], LN_EPS), [], [CONST])
        k.op(POOL, lambda e: e.memset(onesf[:], 1.0), [], [CONST])
        k.op(POOL, lambda e: e.memset(onesb[:], 1.0), [], [CONST])
        k.op(POOL, lambda e: e.affine_select(out=ident[:], in_=onesf[:], pattern=[[1, 128]], compare_op=ALU.is_equal,
                                             fill=0.0, base=0, channel_multiplier=-1), [CONST], [CONST])
        for m in range(2):
            k.op(POOL, lambda e, m=m: e.affine_select(out=tri2[:, m, :], in_=onesb[:], pattern=[[1, 128]], compare_op=ALU.is_ge,
                                                      fill=0.0, base=0, channel_multiplier=-1), [CONST], [CONST])

        lamt = bigf[:, 0:512].rearrange("p (j f) -> p j f", j=2)
        k.dma(SP, [(lamt[:, j, :], a_lam_d[j].rearrange("a f -> (a f)").partition_broadcast(128)) for j in range(2)],
              writes=[RTMP])
        for j in range(2):
            li = 0.8 - 0.6 * math.exp(-0.3 * (3 * j))
            prod = bigf[:, 512:640]
            k.op(DVE, lambda e, j=j: e.tensor_tensor(out=prod.rearrange("p (a f) -> p a f", a=2), in0=lamt[:, j, :].rearrange("p (a b f) -> p a b f", a=2, b=2)[:, :, 0, :],
                                                     in1=lamt[:, j, :].rearrange("p (a b f) -> p a b f", a=2, b=2)[:, :, 1, :], op=ALU.mult), [RTMP], [RTMP])
            k.op(DVE, lambda e: e.reduce_sum(out=small[:, 0:2], in_=prod.rearrange("p (a f) -> p a f", a=2), axis=mybir.AxisListType.X), [RTMP], [RTMP])
            k.op(ACT, lambda e: e.activation(out=small[:, 2:4], in_=small[:, 0:2], func=AF.Exp), [RTMP], [RTMP])
            k.op(DVE, lambda e: e.tensor_tensor(out=small[:, 4:5], in0=small[:, 3:4], in1=small[:, 2:3], op=ALU.subtract), [RTMP], [RTMP])
            k.op(DVE, lambda e, j=j, li=li: e.tensor_scalar(out=nlam[:, j:j + 1], in0=small[:, 4:5], scalar1=-li, scalar2=None, op0=ALU.add), [RTMP], [CONST])
            k.op(DVE, lambda e, j=j, li=li: e.tensor_scalar(out=subl[:, j:j + 1], in0=subl[:, j:j + 1], scalar1=(1.0 - li), scalar2=None, op0=ALU.mult), [CONST], [CONST])

        for i in range(nlayers):
            kind, j = i % 3, i // 3
            if kind == 0:
                prep(f"win{i}", a_win_d[j], D, 3 * D, 128)
                prep(f"wout{i}", a_wout_d[j], D, D, 128)
            elif kind == 1:
                prep(f"wu{i}", g_win_d[0][:, 0:GW], D, GW, 128)
                prep(f"wv{i}", g_win_d[0][:, GW:2 * GW], D, GW, 256)
                prep(f"wout{i}", g_wout_d[0], GW, D, 128)
            else:
                prep(f"win{i}", c_win_d[0], D, 3 * D, 128)
                prep(f"wout{i}", c_wout_d[0], D, D, 128)
            prep(f"wgu{i}", f_wgu_d[i], D, 2 * DFF, 128)
            prep(f"wd{i}", f_wd_d[i], DFF, D, 128)

        if nlayers > 1:
            wsf = bigf[:, 1024:2048].rearrange("p (g s) -> p g s", g=8)
            wsTf = bigf[:, 2048:3072].rearrange("p (g t) -> p g t", g=8)
            bsr = bigf[:, 3072:4096].rearrange("p (g t) -> p g t", g=8)
            bvf = bigf[0:2, 4096:4096 + GW]
            bvt = bigf[0:2, 4096 + GW:4096 + 2 * GW]
            GS = Buf("gsetup")
            k.dma(SP, [(wsf, g_ws_d[0].rearrange("g t s -> t g s")),
                       (bsr.rearrange("p g t -> p (g t)"), g_bs_d[0].rearrange("g t -> (g t)").partition_broadcast(128)),
                       (bvf[0:1, :], g_bin_d[0:1, GW:2 * GW])], writes=[GS])
            for g in range(8):
                bk = wbank.next()
                k.group(PE, [lambda e, g=g, bk=bk: e.transpose(psum[:, bk, 0:128], wsf[:, g, :], ident[:])], [GS, CONST], [PB[bk]])
                k.op(DVE, lambda e, g=g, bk=bk: e.tensor_copy(out=wsTf[:, g, :], in_=psum[:, bk, 0:128]), [PB[bk]], [GS])
            k.op(POOL, lambda e: e.affine_select(out=wsTf, in_=wsTf, pattern=[[0, 8], [1, 128]], compare_op=ALU.is_ge,
                                                 fill=0.0, base=0, channel_multiplier=-1), [GS], [GS])
            k.op(DVE, lambda e: e.tensor_copy(out=wsTm[:], in_=wsTf), [GS], [CONST])
            rws = bigf[:, 7168:8192].rearrange("p (g t) -> p g t", g=8)
            for hlf in range(2):
                bk = wbank.next()
                k.group(PE, [lambda e, hlf=hlf, bk=bk: e.matmul(psum[:, bk, :], lhsT=onesf[:], rhs=wsTf[:, 4 * hlf:4 * hlf + 4, :], start=True, stop=True)],
                        [GS, CONST], [PB[bk]])
                k.op(DVE, lambda e, hlf=hlf, bk=bk: e.tensor_copy(out=rws[:, 4 * hlf:4 * hlf + 4, :], in_=psum[:, bk, :].rearrange("p (g t) -> p g t", g=4)), [PB[bk]], [GS])
            for jc in range(12):
                kk, r3 = jc // 3, jc % 3
                if r3 == 0:
                    parts = [(0, 128, 2 * kk)]
                elif r3 == 2:
                    parts = [(0, 128, 2 * kk + 1)]
                else:
                    parts = [(0, 64, 2 * kk), (64, 128, 2 * kk + 1)]
                for (p0, p1, g) in parts:
                    k.op(DVE, lambda e, jc=jc, p0=p0, p1=p1, g=g: e.scalar_tensor_tensor(
                        out=Rt[p0:p1, jc, :], in0=rws[p0:p1, g, :], scalar=lnb[p0:p1, jc:jc + 1], in1=bsr[p0:p1, g, :],
                        op0=ALU.mult, op1=ALU.add), [GS, CONST], [CONST])
            k.op(DVE, lambda e: e.tensor_copy(out=bvhl[0:1, :], in_=bvf[0:1, :]), [GS], [CONST])
            k.op(DVE, lambda e: e.tensor_copy(out=bvt[0:1, :], in_=bvhl[0:1, :]), [CONST], [GS])
            k.op(DVE, lambda e: e.tensor_tensor(out=bvt[0:1, :], in0=bvf[0:1, :], in1=bvt[0:1, :], op=ALU.subtract), [GS], [GS])
            bvl_tmp = sq[0:1, :, :].rearrange("p a t -> p (a t)")
            k.op(DVE, lambda e: e.tensor_copy(out=bvl_tmp, in_=bvt[0:1, :]), [GS], [GS] + SQ)
            k.dma(SP, [(bvhl[1:2, :], bvl_tmp)], reads=[GS] + SQ, writes=[CONST], sembuf=Buf("bvmove"))
            gs_done = [GS]
        else:
            gs_done = []

        pending = []

        def defer(n, fn):
            pending.append([n, fn])

        def tick():
            for it in list(pending):
                it[0] -= 1
                if it[0] <= 0:
                    pending.remove(it)
                    it[1]()

        def flush():
            while pending:
                tick()

        def load_w(pieces):
            si = ringrot.next()
            pairs = []
            views = []
            off = 0
            bufs = []
            for (name, n) in pieces:
                t_, b = scr[name]
                C, J = t_.shape[2], t_.shape[3]
                v = ring[:, si, off:off + C * J].rearrange("p (c j) -> p c j", c=C)
                pairs.append((v, t_[n]))
                views.append(v)
                off += C * J
                if b not in bufs:
                    bufs.append(b)
            assert off <= 2048
            k.dma(SP, pairs, reads=bufs, writes=[RING[si]])
            return si, views

        def norm_stage1(t):
            bk = wbank_cur[0].next()
            for c in range(NCH):
                qi = sqrot.next()
                k.op(ACT, lambda e, c=c, qi=qi: e.activation(out=sq[:, qi, :], in_=xs[:, c, tsl(t)], func=AF.Square), [XB[c][t]], [SQ[qi]])
                k.group(PE, [mm(bank(bk), onesb[:], sq[:, qi, :], c == 0, c == NCH - 1)], [SQ[qi], CONST], [PB[bk]])
            return bk

        def norm_stage2(t, bk, gcol, dst, dstB, nfeat=1024.0, eps=NORM_EPS):
            ti = tmrot.next()
            rs = tm[:, ti, :]
            k.op(ACT, lambda e: e.activation(out=rs, in_=bank(bk), func=AF.Ln, scale=1.0 / nfeat, bias=epst[:, 0:1]), [PB[bk], CONST], [TM[ti]])
            k.op(ACT, lambda e: e.activation(out=rs, in_=rs, func=AF.Exp, scale=-0.5), [TM[ti]], [TM[ti]])
            for c in range(NCH):
                k.op(DVE, lambda e, c=c: e.scalar_tensor_tensor(out=dst(c), in0=xs[:, c, tsl(t)], scalar=gcol(c), in1=rs, op0=ALU.mult, op1=ALU.mult),
                     [XB[c][t], TM[ti], CONST], [dstB(c)])

        def norm_tile_h(t, gtile, li):
            bk = norm_stage1(t)
            norm_stage2(t, bk, lambda c: gtile[:, li, c:c + 1], lambda c: hT[:, c, tsl(t)], lambda c: HB[c][t])

        wbank_cur = [wbank]

        def ffn(i, next_norm=None):
            wbank_cur[0] = wbank
            for half in range(2):
                tiles = [2 * half, 2 * half + 1]
                if half == 0:
                    for t in tiles:
                        norm_tile_h(t, gffn, i)
                for fp in range(2):
                    for fl in range(11):
                        f = 11 * fp + fl
                        si, (wg, wu_) = load_w([(f"wgu{i}", f), (f"wgu{i}", NF + f)])
                        for tt, t in enumerate(tiles):
                            bg, bu_ = wbank.next(), wbank.next()
                            k.group(PE, [mm(bank(bg), wg[:, c, :], hT[:, c, tsl(t)], c == 0, c == NCH - 1) for c in range(NCH)],
                                    [RING[si]] + [HB[c][t] for c in range(NCH)], [PB[bg]])
                            k.group(PE, [mm(bank(bu_), wu_[:, c, :], hT[:, c, tsl(t)], c == 0, c == NCH - 1) for c in range(NCH)],
                                    [RING[si]] + [HB[c][t] for c in range(NCH)], [PB[bu_]])
                            ti = tmrot.next()
                            k.op(ACT, lambda e, ti=ti, bg=bg: e.activation(out=tm[:, ti, :], in_=bank(bg), func=AF.Silu), [PB[bg]], [TM[ti]])
                            k.op(DVE, lambda e, ti=ti, bu_=bu_, fl=fl, tt=tt: e.tensor_tensor(out=aT[:, fl, tt * 512:(tt + 1) * 512], in0=tm[:, ti, :], in1=bank(bu_), op=ALU.mult),
                                 [TM[ti], PB[bu_]], [AB[fl][tt]])
                        if half == 0 and fp == 0 and fl == 8:
                            for t in (2, 3):
                                norm_tile_h(t, gffn, i)
                        if half == 1 and fp == 0 and fl == 8 and next_norm is not None:
                            next_norm()
                    t_, b_ = scr[f"wd{i}"]
                    for dc in range(NCH):
                        si = ringrot.next()
                        v = ring[:, si, 0:11 * 128].rearrange("p (c j) -> p c j", c=11)
                        k.dma(SP, [(v, t_[dc][:, 11 * fp:11 * fp + 11, :])], reads=[b_], writes=[RING[si]])
                        for tt, t in enumerate(tiles):
                            bo = wbank.next()
                            k.group(PE, [mm(bank(bo), v[:, fl, :], aT[:, fl, tt * 512:(tt + 1) * 512], fl == 0, fl == 10) for fl in range(11)],
                                    [RING[si]] + [AB[fl][tt] for fl in range(11)], [PB[bo]])
                            k.op(DVE, lambda e, dc=dc, t=t, bo=bo: e.tensor_tensor(out=xs[:, dc, tsl(t)], in0=xs[:, dc, tsl(t)], in1=bank(bo), op=ALU.add),
                                 [PB[bo], XB[dc][t]], [XB[dc][t]])

        def out_proj(i, n_in):
            for dc in range(NCH):
                si, (wo,) = load_w([(f"wout{i}", dc)])
                for t in range(NT):
                    bo = wbank.next()
                    k.group(PE, [mm(bank(bo), wo[:, c, :], oT[:, c, tsl(t)], c == 0, c == n_in - 1) for c in range(n_in)],
                            [RING[si]] + [OB[c][t] for c in range(n_in)], [PB[bo]])
                    k.op(DVE, lambda e, dc=dc, t=t, bo=bo: e.tensor_tensor(out=xs[:, dc, tsl(t)], in0=xs[:, dc, tsl(t)], in1=bank(bo), op=ALU.add),
                         [PB[bo], XB[dc][t]], [XB[dc][t]])

        def attention(i, j):
            wbank_cur[0] = wbank4
            wb = wbank4
            def head(h):
                si1, (wq, wk) = load_w([(f"win{i}", h), (f"win{i}", 8 + h)])
                si2, (wv_,) = load_w([(f"win{i}", 16 + h)])
                for t in (range(NT) if "noqk" not in dbg else ()):
                    for m, (w_, scl) in enumerate(((wq, 0.125), (wk, 1.0))):
                        ba = wb.next()
                        k.group(PE, [mm(bank(ba), w_[:, c, :], hT[:, c, tsl(t)], c == 0, c == NCH - 1) for c in range(NCH)],
                                [RING[si1]] + [HB[c][t] for c in range(NCH)], [PB[ba]])
                        t1 = tmrot.next()
                        k.op(ACT, lambda e, t1=t1, ba=ba, scl=scl: e.activation(out=tm[:, t1, :], in_=bank(ba), func=AF.Copy, scale=scl), [PB[ba]], [TM[t1]])
                        qi = sqrot.next()
                        k.op(DVE, lambda e, qi=qi, t1=t1: e.tensor_copy(out=sq[:, qi, :], in_=tm[:, t1, :]), [TM[t1]], [SQ[qi]])
                        k.op(DVE, lambda e, t1=t1, t=t: e.tensor_tensor(out=tm[:, t1, :], in0=tm[:, t1, :], in1=cs[:, 0, tsl(t)], op=ALU.mult), [TM[t1], CSB], [TM[t1]])

                        def rope2(qi=qi, t1=t1, t=t, m=m):
                            bb = wb.next()
                            k.group(PE, [mm(bank(bb), permb[:], sq[:, qi, :], True, True)], [SQ[qi], CONST], [PB[bb]])
                            t2 = tmrot.next()
                            k.op(DVE, lambda e: e.tensor_tensor(out=tm[:, t2, :], in0=cs[:, 1, tsl(t)], in1=bank(bb), op=ALU.mult), [PB[bb], CSB], [TM[t2]])
                            k.op(DVE, lambda e: e.tensor_tensor(out=qk[:, m, tsl(t)], in0=tm[:, t1, :], in1=tm[:, t2, :], op=ALU.add),
                                 [TM[t1], TM[t2]], [QB[m][t]])
                        defer(2, rope2)
                        tick()
                for tg in (range(NT) if "nov" not in dbg else ()):
                    bv = wb.next()
                    fns = []
                    for bl in range(4):
                        tok0 = tg * 512 + bl * 128
                        for c in range(NCH):
                            fns.append(mm(psum[:, bv, bl * 128:(bl + 1) * 128], hT[:, c, tok0:tok0 + 128], wv_[:, c, :], c == 0, c == NCH - 1))
                    k.group(PE, fns, [RING[si2]] + [HB[c][tg] for c in range(NCH)], [PB[bv]])
                    k.op(ACT, lambda e, tg=tg, bv=bv: e.activation(out=vv[:, 4 * tg:4 * tg + 4, :], in_=bank(bv).rearrange("p (b e) -> p b e", b=4), func=AF.Copy),
                         [PB[bv]], [VB[tg]])
                    tick()
                flush()
                units = []
                for jq in range(NT):
                    for c in range(4 * jq + 4):
                        units.append((jq, c))
                state = {}

                def s_stage(u):
                    jq, c = units[u]
                    r = c - 4 * jq
                    q0 = 128 * r if r > 0 else 0
                    ba, bb = wb.next(), wb.next()
                    fns = []
                    for m, b_ in ((0, ba), (1, bb)):
                        fns.append(mm(psum[:, b_, q0:512], qk[64 * m:64 * m + 64, 1, c * 128:(c + 1) * 128],
                                      qk[64 * m:64 * m + 64, 0, jq * 512 + q0:(jq + 1) * 512], True, True))
                    k.group(PE, fns, [QB[0][jq], QB[1][c // 4]], [PB[ba], PB[bb]])
                    pi = ptrot.next()
                    k.op(ACT, lambda e: e.activation(out=pT[:, pi, 0, q0:512], in_=psum[:, ba, q0:512], func=AF.Exp), [PB[ba]], [PT[pi]])
                    k.op(ACT, lambda e: e.activation(out=pT[:, pi, 1, q0:512], in_=psum[:, bb, q0:512], func=AF.Exp), [PB[bb]], [PT[pi]])
                    if r >= 0:
                        k.op(DVE, lambda e: e.tensor_tensor(out=pT[:, pi, :, q0:q0 + 128], in0=pT[:, pi, :, q0:q0 + 128], in1=tri2[:], op=ALU.mult),
                             [PT[pi], CONST], [PT[pi]])
                    state[u] = (pi, q0)

                def pv_stage(u):
                    jq, c = units[u]
                    pi, q0 = state.pop(u)
                    last = (c == 4 * jq + 3)
                    fns = []
                    for m in range(2):
                        fns.append(mm(psum[:, 4 + m, q0:512], vv[:, c, :], pT[:, pi, m, q0:512], c == 0, last))
                    for m in range(2):
                        fns.append(mm(psum[:, 6 + m, q0:512], onesb[:], pT[:, pi, m, q0:512], c == 0, last))
                    k.group(PE, fns, [PT[pi], VB[c // 4], CONST], [PB[4], PB[5], PB[6], PB[7]])
                    if last and "noepi" not in dbg:
                        epilogue(jq)

                def epilogue(jq):
                    o01 = ep[:, 0:2, :]
                    r01 = ep[:, 2:4, :]
                    k.op(ACT, lambda e: e.activation(out=ep[:, 2, :], in_=psum[:, 6, :], func=AF.Ln), [PB[6]], [EPB[2]])
                    k.op(DVE, lambda e: e.tensor_copy(out=ep[:, 0, :], in_=psum[:, 4, :]), [PB[4]], [EPB[0]])
                    k.op(ACT, lambda e: e.activation(out=ep[:, 3, :], in_=psum[:, 7, :], func=AF.Ln), [PB[7]], [EPB[3]])
                    k.op(DVE, lambda e: e.tensor_copy(out=ep[:, 1, :], in_=psum[:, 5, :]), [PB[5]], [EPB[1]])
                    k.op(ACT, lambda e: e.activation(out=r01, in_=r01, func=AF.Exp, scale=-1.0), [EPB[2], EPB[3]], [EPB[2], EPB[3]])
                    k.op(DVE, lambda e: e.tensor_tensor(out=o01, in0=o01, in1=r01, op=ALU.mult), [EPB[0], EPB[1], EPB[2], EPB[3]], [EPB[0], EPB[1]])
                    k.op(DVE, lambda e: e.scalar_tensor_tensor(out=ep[:, 2, :], in0=ep[:, 1, :], scalar=nlam[:, j:j + 1], in1=ep[:, 0, :], op0=ALU.mult, op1=ALU.add),
                         [EPB[0], EPB[1], CONST], [EPB[2]])
                    qi = sqrot.next()
                    k.op(ACT, lambda e: e.activation(out=sq[:, qi, :], in_=ep[:, 2, :], func=AF.Square), [EPB[2]], [SQ[qi]])

                    def ep2():
                        bs_ = wb.next()
                        k.group(PE, [mm(bank(bs_), onesb[:], sq[:, qi, :], True, True)], [SQ[qi], CONST], [PB[bs_]])
                        k.op(ACT, lambda e: e.activation(out=ep[:, 3, :], in_=bank(bs_), func=AF.Ln, scale=1.0 / 128.0, bias=epst[:, 1:2]), [PB[bs_], CONST], [EPB[3]])
                        k.op(ACT, lambda e: e.activation(out=ep[:, 3, :], in_=ep[:, 3, :], func=AF.Exp, scale=-0.5), [EPB[3]], [EPB[3]])
                        k.op(DVE, lambda e: e.scalar_tensor_tensor(out=oT[:, h, tsl(jq)], in0=ep[:, 2, :], scalar=subl[:, j:j + 1], in1=ep[:, 3, :], op0=ALU.mult, op1=ALU.mult),
                             [EPB[2], EPB[3], CONST], [OB[h][jq]])
                    defer(2, ep2)

                nu = len(units) if "nocore" not in dbg else 0
                if nu:
                    s_stage(0)
                for u in range(nu):
                    if u + 1 < nu:
                        s_stage(u + 1)
                    pv_stage(u)
                    tick()
                flush()
            for h in range(NCH):
                head(h)
            wbank_cur[0] = wbank
            if "noout" not in dbg:
                out_proj(i, NCH)

        def shortconv(i):
            wbank_cur[0] = wbank
            k.op(DVE, lambda e: e.memset(hcv[:, 0:2], 0.0), [], [GM["hc"]])

            def chunk(jc):
                si1, (wgb, wgc) = load_w([(f"win{i}", jc), (f"win{i}", 8 + jc)])
                si2, (wxs,) = load_w([(f"win{i}", 16 + jc)])
                for t in range(NT):
                    t0 = t * 512
                    bks = []
                    for (w_, si) in ((wgb, si1), (wgc, si1), (wxs, si2)):
                        bk = wbank.next()
                        k.group(PE, [mm(bank(bk), w_[:, c, :], hT[:, c, tsl(t)], c == 0, c == NCH - 1) for c in range(NCH)],
                                [RING[si]] + [HB[c][t] for c in range(NCH)], [PB[bk]])
                        bks.append(bk)
                    ba, bb, bc = bks
                    t1 = tmrot.next()
                    k.op(ACT, lambda e, t1=t1, bc=bc: e.activation(out=tm[:, t1, :], in_=bank(bc), func=AF.Copy), [PB[bc]], [TM[t1]])
                    k.op(DVE, lambda e, t1=t1, bb=bb, t0=t0: e.tensor_tensor(out=hcv[:, 2 + t0:2 + t0 + 512], in0=tm[:, t1, :], in1=bank(bb), op=ALU.mult),
                         [TM[t1], PB[bb]], [GM["hc"]])
                    t2 = tmrot.next()
                    k.op(DVE, lambda e, t2=t2, t0=t0: e.tensor_scalar(out=tm[:, t2, :], in0=hcv[:, 2 + t0:2 + t0 + 512], scalar1=convw[:, 2, jc:jc + 1], scalar2=None, op0=ALU.mult),
                         [GM["hc"], CONST], [TM[t2]])
                    k.op(DVE, lambda e, t2=t2, t0=t0: e.scalar_tensor_tensor(out=tm[:, t2, :], in0=hcv[:, 1 + t0:1 + t0 + 512], scalar=convw[:, 1, jc:jc + 1], in1=tm[:, t2, :], op0=ALU.mult, op1=ALU.add),
                         [GM["hc"], CONST, TM[t2]], [TM[t2]])
                    k.op(DVE, lambda e, t2=t2, t0=t0: e.scalar_tensor_tensor(out=tm[:, t2, :], in0=hcv[:, t0:t0 + 512], scalar=convw[:, 0, jc:jc + 1], in1=tm[:, t2, :], op0=ALU.mult, op1=ALU.add),
                         [GM["hc"], CONST, TM[t2]], [TM[t2]])
                    k.op(DVE, lambda e, t2=t2, ba=ba, t=t: e.tensor_tensor(out=oT[:, jc, tsl(t)], in0=tm[:, t2, :], in1=bank(ba), op=ALU.mult),
                         [TM[t2], PB[ba]], [OB[jc][t]])
            for jc in range(NCH):
                chunk(jc)
            out_proj(i, NCH)

        def gmlp(i):
            wbank_cur[0] = wbank
            ZV = [GM["zv0"], GM["zv1"]]

            def tile(t):
                for n in range(12):
                    si, (w_,) = load_w([(f"wu{i}", n)])
                    bk = wbank.next()
                    k.group(PE, [mm(bank(bk), w_[:, c, :], hT[:, c, tsl(t)], c == 0, c == NCH - 1) for c in range(NCH)],
                            [RING[si]] + [HB[c][t] for c in range(NCH)], [PB[bk]])
                    k.op(ACT, lambda e, n=n, bk=bk: e.activation(out=uT[:, n, :], in_=bank(bk), func=AF.Gelu, bias=bu[:, n:n + 1]), [PB[bk], CONST], [GM["uT"]])
                for bp in range(2):
                    for w2 in range(3):
                        sl = [load_w([(f"wv{i}", 2 * w2 + hh)]) for hh in range(2)]
                        for b2 in range(2):
                            bl = 2 * bp + b2
                            tok0 = t * 512 + bl * 128
                            bk = wbank.next()
                            fns = []
                            for hh in range(2):
                                wvt = sl[hh][1][0]
                                col0 = (2 * w2 + hh) * 256
                                for c in range(NCH):
                                    fns.append(mm(psum[:, bk, hh * 256:(hh + 1) * 256], hT[:, c, tok0:tok0 + 128], wvt[:, c, :], c == 0, False))
                                fns.append(mm(psum[:, bk, hh * 256:(hh + 1) * 256], onesb[0:2, :], bvhl[0:2, col0:col0 + 256], False, True))
                            k.group(PE, fns, [RING[sl[0][0]], RING[sl[1][0]], CONST] + [HB[c][t] for c in range(NCH)], [PB[bk]])
                            k.op(ACT, lambda e, b2=b2, w2=w2, bk=bk: e.activation(out=zv4[:, b2, w2 * 512:(w2 + 1) * 512], in_=bank(bk), func=AF.Gelu), [PB[bk]], [ZV[b2]])
                    for b2 in range(2):
                        bl = 2 * bp + b2
                        for w2 in range(3):
                            k.op(DVE, lambda e, b2=b2, w2=w2: e.bn_stats(out=gst[:, 6 * w2:6 * w2 + 6], in_=zv4[:, b2, w2 * 512:(w2 + 1) * 512]), [ZV[b2]], [GM["stats"]])
                        k.op(DVE, lambda e: e.bn_aggr(out=gst[:, 18:20], in_=gst[:, 0:18]), [GM["stats"]], [GM["stats"]])
                        k.op(ACT, lambda e: e.activation(out=gst[:, 20:21], in_=gst[:, 19:20], func=AF.Ln, bias=epst[:, 2:3]), [GM["stats"], CONST], [GM["stats"]])
                        k.op(ACT, lambda e: e.activation(out=gst[:, 20:21], in_=gst[:, 20:21], func=AF.Exp, scale=-0.5), [GM["stats"]], [GM["stats"]])
                        k.op(DVE, lambda e, b2=b2: e.tensor_scalar(out=vhat, in0=zv4[:, b2, :], scalar1=gst[:, 18:19], scalar2=gst[:, 20:21], op0=ALU.subtract, op1=ALU.mult),
                             [ZV[b2], GM["stats"]], [GM["vhat"]])
                        for grp in range(3):
                            bk = wbank.next()
                            fns = []
                            for jl in range(4):
                                jc = 4 * grp + jl
                                kk, r3 = jc // 3, jc % 3
                                if r3 == 0:
                                    parts = [(0, 128, 2 * kk)]
                                elif r3 == 2:
                                    parts = [(0, 128, 2 * kk + 1)]
                                else:
                                    parts = [(0, 64, 2 * kk), (64, 128, 2 * kk + 1)]
                                for (p0, p1, g) in parts:
                                    fns.append(mm(psum[p0:p1, bk, jl * 128:(jl + 1) * 128], vhat[:, jc * 128 + p0:jc * 128 + p1], wsTm[:, g, :], True, True))
                            k.group(PE, fns, [GM["vhat"], CONST], [PB[bk]])
                            sv = svrot.next()
                            SVB = GM["svt0"] if sv == 0 else GM["svt1"]
                            for jl in range(4):
                                jc = 4 * grp + jl
                                k.op(DVE, lambda e, jl=jl, jc=jc, bk=bk, sv=sv: e.scalar_tensor_tensor(out=svt[:, sv, jl, :], in0=psum[:, bk, jl * 128:(jl + 1) * 128], scalar=lng[:, jc:jc + 1], in1=Rt[:, jc, :],
                                                                                                      op0=ALU.mult, op1=ALU.add), [PB[bk], CONST], [SVB])
                            k.op(DVE, lambda e, grp=grp, bl=bl, sv=sv: e.tensor_tensor(out=uT[:, 4 * grp:4 * grp + 4, bl * 128:(bl + 1) * 128], in0=uT[:, 4 * grp:4 * grp + 4, bl * 128:(bl + 1) * 128],
                                                                                       in1=svt[:, sv, :, :], op=ALU.mult), [SVB, GM["uT"]], [GM["uT"]])
                for dc in range(NCH):
                    si, (wo,) = load_w([(f"wout{i}", dc)])
                    bo = wbank.next()
                    k.group(PE, [mm(bank(bo), wo[:, c, :], uT[:, c, :], c == 0, c == 11) for c in range(12)], [RING[si], GM["uT"]], [PB[bo]])
                    k.op(DVE, lambda e, dc=dc, bo=bo: e.tensor_tensor(out=xs[:, dc, tsl(t)], in0=xs[:, dc, tsl(t)], in1=bank(bo), op=ALU.add),
                         [PB[bo], XB[dc][t]], [XB[dc][t]])
            for t in range(NT):
                tile(t)

        sto = bigf[:, 2048:4096].rearrange("p (s d) -> p s d", s=2)
        STO = [Buf("sto0"), Buf("sto1")]
        storot = Rot(range(2))
        HI = 6.28125
        LO = 2.0 * math.pi - 6.28125

        big_users = [bb_ for row in AB for bb_ in row] + [bb_ for row in OB for bb_ in row] + list(GM.values())

        def load_tile(s, t):
            for b in range(4 * t, 4 * t + 4):
                gi = stgrot.next()
                k.dma(SP, [(stg[:, gi, :], x_d[s, b * 128:(b + 1) * 128, :])], writes=[STG[gi]] + ([RTMP] + gs_done if (s == 0 and b < 2) else []) + (big_users if b < 2 else []), sembuf=STG[gi])
                for hf in range(2):
                    bk = wbank.next()
                    k.group(PE, [lambda e, c=c, bk=bk, gi=gi, hf=hf: e.transpose(psum[:, bk, c * 128:(c + 1) * 128], stg[:, gi, (4 * hf + c) * 128:(4 * hf + c + 1) * 128], ident[:])
                                 for c in range(4)], [STG[gi], CONST], [PB[bk]])
                    wr = [XB[4 * hf + c][b // 4] for c in range(4)]
                    if hf == 0:
                        k.op(ACT, lambda e, bk=bk, b=b: e.activation(out=xs[:, 0:4, b * 128:(b + 1) * 128], in_=bank(bk).rearrange("p (c t) -> p c t", c=4), func=AF.Copy), [PB[bk]], wr)
                    else:
                        k.op(DVE, lambda e, bk=bk, b=b: e.tensor_copy(out=xs[:, 4:8, b * 128:(b + 1) * 128], in_=bank(bk).rearrange("p (c t) -> p c t", c=4)), [PB[bk]], wr)

        def store_tile(s, t):
            for b in range(4 * t, 4 * t + 4):
                gi = storot.next()
                for hf in range(2):
                    bk = wbank.next()
                    k.group(PE, [lambda e, c=c, bk=bk, b=b, hf=hf: e.transpose(psum[:, bk, c * 128:(c + 1) * 128], xs[:, 4 * hf + c, b * 128:(b + 1) * 128], ident[:])
                                 for c in range(4)], [XB[4 * hf + c][b // 4] for c in range(4)] + [CONST], [PB[bk]])
                    if hf == 0:
                        k.op(ACT, lambda e, bk=bk, gi=gi: e.activation(out=sto[:, gi, 0:512], in_=bank(bk), func=AF.Copy), [PB[bk]], [STO[gi]])
                    else:
                        k.op(DVE, lambda e, bk=bk, gi=gi: e.tensor_copy(out=sto[:, gi, 512:1024], in_=bank(bk)), [PB[bk]], [STO[gi]])
                k.dma(POOL, [(out_d[s, b * 128:(b + 1) * 128, :], sto[:, gi, :])], reads=[STO[gi]], writes=[OUTB], sembuf=OUTS[gi])

        def rope_tables(s):
            angf = rtmp[:, 0, :]
            posi = rtmpi[:, 0, :]
            cosr = rtmp[:, 1, :]
            kf = rtmp[:, 2, :]
            ki = rtmpi[:, 2, :]
            k.dma(SP, [(posi, pos_d[s:s + 1, :].partition_broadcast(128))], reads=[STG[0], STG[1]], writes=[RTMP, STO[0], STO[1]], sembuf=RTMP)
            k.op(DVE, lambda e: e.tensor_copy(out=angf, in_=posi), [RTMP], [RTMP])
            k.op(DVE, lambda e: e.tensor_scalar(out=angf, in0=angf, scalar1=cst[:, 0:1], scalar2=None, op0=ALU.mult), [RTMP, CONST], [RTMP])
            k.op(DVE, lambda e: e.tensor_scalar(out=cosr, in0=angf, scalar1=math.pi / 2.0, scalar2=None, op0=ALU.add), [RTMP], [RTMP])
            for which in range(2):
                src = cosr if which == 0 else angf
                k.op(DVE, lambda e, src=src: e.tensor_scalar(out=kf, in0=src, scalar1=1.0 / (2.0 * math.pi), scalar2=None, op0=ALU.mult), [RTMP], [RTMP])
                k.op(DVE, lambda e: e.tensor_copy(out=ki, in_=kf), [RTMP], [RTMP])
                k.op(DVE, lambda e: e.tensor_copy(out=kf, in_=ki), [RTMP], [RTMP])
                k.op(DVE, lambda e, src=src: e.scalar_tensor_tensor(out=src, in0=kf, scalar=-HI, in1=src, op0=ALU.mult, op1=ALU.add), [RTMP], [RTMP])
                k.op(DVE, lambda e, src=src: e.scalar_tensor_tensor(out=src, in0=kf, scalar=-LO, in1=src, op0=ALU.mult, op1=ALU.add), [RTMP], [RTMP])
                k.op(DVE, lambda e, src=src: e.tensor_scalar(out=src, in0=src, scalar1=math.pi, scalar2=-math.pi, op0=ALU.min, op1=ALU.max), [RTMP], [RTMP])
                if which == 0:
                    k.op(ACT, lambda e, src=src: e.activation(out=cs[:, 0, :], in_=src, func=AF.Sin), [RTMP], [CSB])
                else:
                    k.op(ACT, lambda e, src=src: e.activation(out=cs[:, 1, :], in_=src, func=AF.Sin, scale=cst[:, 1:2]), [RTMP, CONST], [CSB])

        for t in range(NT):
            load_tile(0, t)
        rope_tables(0)
        for s in range(NS):
            prenormed = set()
            for i in range(nlayers):
                kind, j = i % 3, i // 3
                for t in range(NT):
                    if (i, t) not in prenormed:
                        norm_tile_h(t, gmix, i)
                if "nomix" in dbg:
                    pass
                elif kind == 0:
                    attention(i, j)
                elif kind == 1:
                    gmlp(i)
                else:
                    shortconv(i)
                if "noffn" not in dbg:
                    nn = None
                    if i + 1 < nlayers:
                        def nn(i=i):
                            for t in (0, 1):
                                norm_tile_h(t, gmix, i + 1)
                                prenormed.add((i + 1, t))
                    ffn(i, nn)

            wbank_cur[0] = wbank
            for t in range(NT):
                bk = norm_stage1(t)
                norm_stage2(t, bk, lambda c: gfin[:, c:c + 1], lambda c, t=t: xs[:, c, tsl(t)], lambda c, t=t: XB[c][t])
            for t in range(NT):
                store_tile(s, t)
                if s + 1 < NS:
                    load_tile(s + 1, t)
            if s + 1 < NS:
                rope_tables(s + 1)

        for g_ in OUTS:
            if g_.dsem is not None:
                k._wait(POOL, (g_.dsem, g_.dcnt))
        k.emit()
    return nc


_CACHE = {}


def _consts():
    c = np.zeros((128, 132), np.float32)
    f = (1.0 / (np.float32(10000.0) ** (np.arange(0, 64, 2, dtype=np.float32) / np.float32(64)))).astype(np.float32)
    p = np.arange(128)
    c[:, 0] = f[p % 32]
    c[:, 1] = np.where((p // 32) % 2 == 0, -1.0, 1.0)
    perm = np.zeros((128, 128), np.float32)
    perm[p ^ 32, p] = 1.0
    c[:, 4:132] = perm
    return c


def kernel(**inputs):
    B = inputs["x"].shape[0]
    NS = B // N_CORES
    key = NS
    if key not in _CACHE:
        _CACHE[key] = build_program(NS)
    nc = _CACHE[key]
    cst = _consts()
    in_maps = []
    for r in range(N_CORES):
        m = {}
        for name, v in inputs.items():
            a = np.asarray(v)
            if name in ("x", "positions"):
                a = a[r * NS:(r + 1) * NS]
            m[name] = np.ascontiguousarray(a)
        m["cst"] = cst
        in_maps.append(m)
    res = run_bass_kernel_spmd(nc, in_maps, core_ids=list(range(N_CORES)))
    return np.concatenate([np.asarray(r_["out"]) for r_ in res.results], axis=0).astype(np.float32)
```
